# Optimizing a Trainium2 kernel written in Bass

```python
import math
import jax
import jax.numpy as jnp
from jax import lax
import numpy as np

D_MODEL = 1024
BATCH = 8
SEQ = 4096
DEPTH = 2

GRID_W = 64
CTX_LEN = 256
HEAD_DIM = 64
ROPE_THETA = 10000.0
NORM_EPS = 1e-6
Q_BLOCK = 128
NEG_INF = -1e30
N_MOD = 6

HALF_WIDTH = D_MODEL // 2

A_HEADS = HALF_WIDTH // HEAD_DIM
A_KV_HEADS = 2
A_GROUP = A_HEADS // A_KV_HEADS
A_SCALE = HEAD_DIM ** -0.5

B_NOPE = 64
B_ROPE = 32
B_VDIM = 64
B_HEADS = HALF_WIDTH // B_VDIM
B_Q_RANK = 384
B_KV_RANK = 256
B_SCALE = (B_NOPE + B_ROPE) ** -0.5

C_WIDTH = HALF_WIDTH
C_EMB_DIM = 5
C_BANDS = (C_EMB_DIM - 1) // 2
C_FILTER_WIDTH = 64
C_MIN_DECAY = math.log(1e-2) / 1.5
C_MAX_DECAY = math.log(1e-2) / 0.3

D_HEADS = HALF_WIDTH // HEAD_DIM
D_KV_HEADS = 2
D_GROUP = D_HEADS // D_KV_HEADS
D_SCALE = HEAD_DIM ** -0.5
WINDOW = 128

FFN_DIM = 2816

EV_SIZES = (A_HEADS * HEAD_DIM, B_Q_RANK, A_KV_HEADS * HEAD_DIM, A_KV_HEADS * HEAD_DIM, B_KV_RANK, B_ROPE)
EV_Q_COLS = A_HEADS * HEAD_DIM + B_Q_RANK
EV_COLS = EV_Q_COLS + 2 * A_KV_HEADS * HEAD_DIM + B_KV_RANK + B_ROPE
EV_OUT = A_HEADS * HEAD_DIM + B_HEADS * B_VDIM
OD_SIZES = (D_HEADS * HEAD_DIM, 3 * C_WIDTH, D_KV_HEADS * HEAD_DIM, D_KV_HEADS * HEAD_DIM)
OD_Q_COLS = D_HEADS * HEAD_DIM + 3 * C_WIDTH
OD_COLS = OD_Q_COLS + 2 * D_KV_HEADS * HEAD_DIM
OD_OUT = D_HEADS * HEAD_DIM + C_WIDTH

kernel_name = "hybrid_dit_prefix_trunk"


def _split(z, sizes):
    out, start = [], 0
    for s in sizes:
        out.append(z[..., start:start + s])
        start += s
    return out


def _rmsnorm(x, gain):
    xf = x.astype(jnp.float32)
    inv = lax.rsqrt(jnp.mean(xf * xf, axis=-1, keepdims=True) + NORM_EPS)
    return (xf * inv).astype(x.dtype) * gain


def _modulate(h, shift, scale):
    return h * (1 + scale) + shift


def _axial_rope(rows, rope_dim):
    row_idx = jnp.repeat(jnp.arange(rows), GRID_W).astype(jnp.float32)
    col_idx = jnp.tile(jnp.arange(GRID_W), rows).astype(jnp.float32)
    d_axis = rope_dim // 2
    inv_freq = ROPE_THETA ** (-jnp.arange(0, d_axis, 2, dtype=jnp.float32) / d_axis)
    ang = jnp.concatenate([row_idx[:, None] * inv_freq, col_idx[:, None] * inv_freq], axis=-1)
    return jnp.cos(ang), jnp.sin(ang)


def _apply_rope(x, cos, sin):
    shape = (1, cos.shape[0]) + (1,) * (x.ndim - 3) + (cos.shape[1],)
    cos = cos.reshape(shape).astype(x.dtype)
    sin = sin.reshape(shape).astype(x.dtype)
    xr = x.reshape(x.shape[:-1] + (-1, 2))
    x1, x2 = xr[..., 0], xr[..., 1]
    return jnp.stack([x1 * cos - x2 * sin, x1 * sin + x2 * cos], axis=-1).reshape(x.shape)


def _attend(q, k, v, scale, mask=None, sink=None):
    s = jnp.einsum("bqhgd,bkhd->bhgqk", q, k).astype(jnp.float32) * scale
    if mask is not None:
        s = jnp.where(mask, s, NEG_INF)
    if sink is not None:
        sink_col = jnp.broadcast_to(sink.astype(jnp.float32)[None, :, :, None, None], s.shape[:-1] + (1,))
        p = jax.nn.softmax(jnp.concatenate([s, sink_col], axis=-1), axis=-1)[..., :-1]
    else:
        p = jax.nn.softmax(s, axis=-1)
    return jnp.einsum("bhgqk,bkhd->bqhgd", p.astype(v.dtype), v)


def _blocked_attention(q, k, v, scale):
    bsz, n, hk, g, dq = q.shape
    nb = n // Q_BLOCK
    qb = jnp.moveaxis(q.reshape(bsz, nb, Q_BLOCK, hk, g, dq), 1, 0)
    ob = lax.map(lambda qi: _attend(qi, k, v, scale), qb)
    return jnp.moveaxis(ob, 0, 1).reshape(bsz, n, hk, g, v.shape[-1])


def _window_attention(q, k, v, k_ctx, v_ctx, sink, scale):
    bsz, n, hk, g, dh = q.shape
    nb = n // Q_BLOCK
    span = Q_BLOCK + 2 * WINDOW
    pad = ((0, 0), (WINDOW, WINDOW), (0, 0), (0, 0))
    kp = jnp.pad(k, pad)
    vp = jnp.pad(v, pad)
    qb = jnp.moveaxis(q.reshape(bsz, nb, Q_BLOCK, hk, g, dh), 1, 0)
    r = jnp.arange(Q_BLOCK)[:, None]
    j = jnp.arange(span)[None, :]
    band = (j >= r) & (j <= r + 2 * WINDOW)
    ctx_mask = jnp.ones((Q_BLOCK, k_ctx.shape[1]), dtype=bool)

    def block(args):
        i, qi = args
        start = i * Q_BLOCK
        ki = lax.dynamic_slice_in_dim(kp, start, span, axis=1)
        vi = lax.dynamic_slice_in_dim(vp, start, span, axis=1)
        pos = start - WINDOW + j
        valid = band & (pos >= 0) & (pos < n)
        mask = jnp.concatenate([ctx_mask, valid], axis=1)
        return _attend(qi, jnp.concatenate([k_ctx, ki], axis=1), jnp.concatenate([v_ctx, vi], axis=1), scale, mask, sink)

    ob = lax.map(block, (jnp.arange(nb), qb))
    return jnp.moveaxis(ob, 0, 1).reshape(bsz, n, hk, g, dh)


def _dwconv3(x, w, b):
    xp = jnp.pad(x, ((0, 0), (1, 1), (0, 0)))
    return xp[:, :-2] * w[0] + xp[:, 1:-1] * w[1] + xp[:, 2:] * w[2] + b


def _hyena_filters(n, w1, b1, w2, b2, w3, b3, w4, freq):
    t = jnp.linspace(0.0, 1.0, n, dtype=jnp.float32)[:, None]
    w = 2 * math.pi * jnp.arange(n, dtype=jnp.float32)[:, None] / n
    f = jnp.linspace(1e-4, C_BANDS - 1, C_BANDS, dtype=jnp.float32)[None, :]
    z = jnp.concatenate([t, jnp.cos(f * w), -jnp.sin(f * w)], axis=-1).astype(w1.dtype)
    h = jnp.sin(freq[0] * (z @ w1 + b1))
    h = jnp.sin(freq[1] * (h @ w2 + b2))
    h = jnp.sin(freq[2] * (h @ w3 + b3))
    h = (h @ w4).reshape(n, 2, C_WIDTH)
    deltas = jnp.abs(jnp.linspace(C_MIN_DECAY, C_MAX_DECAY, C_WIDTH, dtype=jnp.float32))
    h = h * jnp.exp(-t[:, :, None] * deltas)
    return h[:, 0], h[:, 1]


def _bidir_long_conv(u, h_fwd, h_bwd, bias):
    n = u.shape[1]
    n_fft = 2 * n
    k_full = jnp.concatenate([h_fwd, jnp.zeros_like(h_fwd[:1]), h_bwd[:0:-1]], axis=0).astype(jnp.float32)
    k_f = jnp.fft.rfft(k_full, n=n_fft, axis=0)
    uf = u.astype(jnp.float32)
    u_f = jnp.fft.rfft(uf, n=n_fft, axis=1)
    y = jnp.fft.irfft(u_f * k_f[None], n=n_fft, axis=1)[:, :n]
    return (y + uf * bias.astype(jnp.float32)).astype(u.dtype)


def _hyena(z, conv_w, conv_b, filt, bias):
    z = _dwconv3(z, conv_w, conv_b)
    x0, x1, v = _split(z, (C_WIDTH, C_WIDTH, C_WIDTH))
    h_fwd, h_bwd = _hyena_filters(z.shape[1], *filt)
    return x0 * _bidir_long_conv(v * x1, h_fwd, h_bwd, bias)


def _a_queries(za, gain, rope):
    bsz, n = za.shape[:2]
    q = _rmsnorm(za.reshape(bsz, n, A_KV_HEADS, A_GROUP, HEAD_DIM), gain)
    return q if rope is None else _apply_rope(q, *rope)


def _a_keys_values(zk, zv, gain, rope):
    bsz, n = zk.shape[:2]
    k = _rmsnorm(zk.reshape(bsz, n, A_KV_HEADS, HEAD_DIM), gain)
    if rope is not None:
        k = _apply_rope(k, *rope)
    return k, zv.reshape(bsz, n, A_KV_HEADS, HEAD_DIM)


def _b_queries(zcq, gain, w_uq, rope):
    bsz, n = zcq.shape[:2]
    q = (_rmsnorm(zcq, gain) @ w_uq).reshape(bsz, n, B_HEADS, 1, B_NOPE + B_ROPE)
    if rope is not None:
        q = jnp.concatenate([q[..., :B_NOPE], _apply_rope(q[..., B_NOPE:], *rope)], axis=-1)
    return q


def _b_keys_values(zckv, zkr, gain, w_ukv, rope):
    bsz, n = zckv.shape[:2]
    kv = (_rmsnorm(zckv, gain) @ w_ukv).reshape(bsz, n, B_HEADS, B_NOPE + B_VDIM)
    k_nope, v = _split(kv, (B_NOPE, B_VDIM))
    k_rope = zkr[:, :, None, :]
    if rope is not None:
        k_rope = _apply_rope(k_rope, *rope)
    k = jnp.concatenate([k_nope, jnp.broadcast_to(k_rope, (bsz, n, B_HEADS, B_ROPE))], axis=-1)
    return k, v


def _d_queries(zq, rope):
    bsz, n = zq.shape[:2]
    q = zq.reshape(bsz, n, D_KV_HEADS, D_GROUP, HEAD_DIM)
    return q if rope is None else _apply_rope(q, *rope)


def _d_keys_values(zk, zv, rope):
    bsz, n = zk.shape[:2]
    k = zk.reshape(bsz, n, D_KV_HEADS, HEAD_DIM)
    if rope is not None:
        k = _apply_rope(k, *rope)
    return k, zv.reshape(bsz, n, D_KV_HEADS, HEAD_DIM)


def _even_mixer(h_lat, h_ctx, need_ctx, rope_hd, rope_mla, w_in, w_out, a_qn, a_kn, b_qn, b_w_uq, b_kvn, b_w_ukv):
    bsz, n = h_lat.shape[:2]
    za_q, zb_q, za_k, za_v, zb_kv, zb_kr = _split(h_lat @ w_in, EV_SIZES)
    ca_k, ca_v, cb_kv, cb_kr = _split(h_ctx @ w_in[:, EV_Q_COLS:], EV_SIZES[2:])
    ka_c, va_c = _a_keys_values(ca_k, ca_v, a_kn, None)
    kb_c, vb_c = _b_keys_values(cb_kv, cb_kr, b_kvn, b_w_ukv, None)
    ka, va = _a_keys_values(za_k, za_v, a_kn, rope_hd)
    kb, vb = _b_keys_values(zb_kv, zb_kr, b_kvn, b_w_ukv, rope_mla)
    oa = _blocked_attention(_a_queries(za_q, a_qn, rope_hd),
                            jnp.concatenate([ka_c, ka], axis=1), jnp.concatenate([va_c, va], axis=1), A_SCALE)
    ob = _blocked_attention(_b_queries(zb_q, b_qn, b_w_uq, rope_mla),
                            jnp.concatenate([kb_c, kb], axis=1), jnp.concatenate([vb_c, vb], axis=1), B_SCALE)
    y_lat = jnp.concatenate([oa.reshape(bsz, n, -1), ob.reshape(bsz, n, -1)], axis=-1) @ w_out
    if not need_ctx:
        return y_lat, None
    m = h_ctx.shape[1]
    ca_q, cb_q = _split(h_ctx @ w_in[:, :EV_Q_COLS], EV_SIZES[:2])
    oa_c = _attend(_a_queries(ca_q, a_qn, None), ka_c, va_c, A_SCALE)
    ob_c = _attend(_b_queries(cb_q, b_qn, b_w_uq, None), kb_c, vb_c, B_SCALE)
    y_ctx = jnp.concatenate([oa_c.reshape(bsz, m, -1), ob_c.reshape(bsz, m, -1)], axis=-1) @ w_out
    return y_lat, y_ctx


def _odd_mixer(h_lat, h_ctx, need_ctx, rope_hd, w_in, w_out, sink, conv_w, conv_b, filt, c_bias):
    bsz, n = h_lat.shape[:2]
    sink = sink.reshape(D_KV_HEADS, D_GROUP)
    zd_q, zc, zd_k, zd_v = _split(h_lat @ w_in, OD_SIZES)
    cd_k, cd_v = _split(h_ctx @ w_in[:, OD_Q_COLS:], OD_SIZES[2:])
    kd_c, vd_c = _d_keys_values(cd_k, cd_v, None)
    kd, vd = _d_keys_values(zd_k, zd_v, rope_hd)
    od = _window_attention(_d_queries(zd_q, rope_hd), kd, vd, kd_c, vd_c, sink, D_SCALE)
    oc = _hyena(zc, conv_w, conv_b, filt, c_bias)
    y_lat = jnp.concatenate([od.reshape(bsz, n, -1), oc], axis=-1) @ w_out
    if not need_ctx:
        return y_lat, None
    m = h_ctx.shape[1]
    cd_q, cc = _split(h_ctx @ w_in[:, :OD_Q_COLS], OD_SIZES[:2])
    od_c = _attend(_d_queries(cd_q, None), kd_c, vd_c, D_SCALE, sink=sink)
    oc_c = _hyena(cc, conv_w, conv_b, filt, c_bias)
    y_ctx = jnp.concatenate([od_c.reshape(bsz, m, -1), oc_c], axis=-1) @ w_out
    return y_lat, y_ctx


def _conv_ffn(h, w_up, conv_w, conv_b, w_down):
    g, v = _split(_dwconv3(h @ w_up, conv_w, conv_b), (FFN_DIM, FFN_DIM))
    return (jax.nn.silu(g) * v) @ w_down


def setup_inputs(seed: int = 0) -> dict:
    key = jax.random.key(seed)
    keys = iter(jax.random.split(key, 40))

    def nrm(shape, scale):
        return jax.random.normal(next(keys), shape, jnp.float32) * scale

    def gain(shape):
        return 1.0 + nrm(shape, 0.05)

    n_even = (DEPTH + 1) // 2
    n_odd = DEPTH // 2
    return {
        "x": nrm((BATCH, SEQ, D_MODEL), 1.0),
        "c": nrm((BATCH, D_MODEL), 1.0),
        "ctx": nrm((BATCH, CTX_LEN, D_MODEL), 1.0),
        "c_ctx": nrm((D_MODEL,), 1.0),
        "w_mod": nrm((DEPTH, D_MODEL, N_MOD * D_MODEL), 0.5 * D_MODEL ** -0.5),
        "b_mod": nrm((DEPTH, N_MOD * D_MODEL), 0.02),
        "norm_mix": gain((DEPTH, D_MODEL)),
        "norm_ffn": gain((DEPTH, D_MODEL)),
        "ev_w_in": nrm((n_even, D_MODEL, EV_COLS), D_MODEL ** -0.5),
        "ev_w_out": nrm((n_even, EV_OUT, D_MODEL), EV_OUT ** -0.5),
        "a_q_norm": gain((n_even, HEAD_DIM)),
        "a_k_norm": gain((n_even, HEAD_DIM)),
        "b_q_norm": gain((n_even, B_Q_RANK)),
        "b_w_uq": nrm((n_even, B_Q_RANK, B_HEADS * (B_NOPE + B_ROPE)), B_Q_RANK ** -0.5),
        "b_kv_norm": gain((n_even, B_KV_RANK)),
        "b_w_ukv": nrm((n_even, B_KV_RANK, B_HEADS * (B_NOPE + B_VDIM)), B_KV_RANK ** -0.5),
        "od_w_in": nrm((n_odd, D_MODEL, OD_COLS), D_MODEL ** -0.5),
        "od_w_out": nrm((n_odd, OD_OUT, D_MODEL), OD_OUT ** -0.5),
        "d_sink": nrm((n_odd, D_HEADS), 0.5),
        "c_conv_w": nrm((n_odd, 3, 3 * C_WIDTH), 3 ** -0.5),
        "c_conv_b": nrm((n_odd, 3 * C_WIDTH), 0.02),
        "c_filt_w1": nrm((n_odd, C_EMB_DIM, C_FILTER_WIDTH), 1.0),
        "c_filt_b1": nrm((n_odd, C_FILTER_WIDTH), 0.1),
        "c_filt_w2": nrm((n_odd, C_FILTER_WIDTH, C_FILTER_WIDTH), C_FILTER_WIDTH ** -0.5),
        "c_filt_b2": nrm((n_odd, C_FILTER_WIDTH), 0.1),
        "c_filt_w3": nrm((n_odd, C_FILTER_WIDTH, C_FILTER_WIDTH), C_FILTER_WIDTH ** -0.5),
        "c_filt_b3": nrm((n_odd, C_FILTER_WIDTH), 0.1),
        "c_filt_w4": nrm((n_odd, C_FILTER_WIDTH, 2 * C_WIDTH), 0.05 * C_FILTER_WIDTH ** -0.5),
        "c_filt_freq": gain((n_odd, 3, C_FILTER_WIDTH)),
        "c_bias": nrm((n_odd, C_WIDTH), 0.5),
        "ffn_w_up": nrm((DEPTH, D_MODEL, 2 * FFN_DIM), D_MODEL ** -0.5),
        "ffn_conv_w": nrm((DEPTH, 3, 2 * FFN_DIM), 3 ** -0.5),
        "ffn_conv_b": nrm((DEPTH, 2 * FFN_DIM), 0.02),
        "ffn_w_down": nrm((DEPTH, FFN_DIM, D_MODEL), FFN_DIM ** -0.5),
        "final_norm": gain((D_MODEL,)),
    }


def reference(x, c, ctx, c_ctx, w_mod, b_mod, norm_mix, norm_ffn,
              ev_w_in, ev_w_out, a_q_norm, a_k_norm, b_q_norm, b_w_uq, b_kv_norm, b_w_ukv,
              od_w_in, od_w_out, d_sink, c_conv_w, c_conv_b,
              c_filt_w1, c_filt_b1, c_filt_w2, c_filt_b2, c_filt_w3, c_filt_b3, c_filt_w4, c_filt_freq, c_bias,
              ffn_w_up, ffn_conv_w, ffn_conv_b, ffn_w_down, final_norm):
    rows = x.shape[1] // GRID_W
    rope_hd = _axial_rope(rows, HEAD_DIM)
    rope_mla = _axial_rope(rows, B_ROPE)
    silu_c = jax.nn.silu(c)
    silu_cc = jax.nn.silu(c_ctx)
    for layer in range(DEPTH):
        need_ctx = layer < DEPTH - 1
        mods_l = _split(silu_c @ w_mod[layer] + b_mod[layer], (D_MODEL,) * N_MOD)
        sh_l, sc_l, gt_l, sh2_l, sc2_l, gt2_l = [m_[:, None, :] for m_ in mods_l]
        sh_c, sc_c, gt_c, sh2_c, sc2_c, gt2_c = _split(silu_cc @ w_mod[layer] + b_mod[layer], (D_MODEL,) * N_MOD)
        h_lat = _modulate(_rmsnorm(x, norm_mix[layer]), sh_l, sc_l)
        h_ctx = _modulate(_rmsnorm(ctx, norm_mix[layer]), sh_c, sc_c)
        if layer % 2 == 0:
            e = layer // 2
            y_lat, y_ctx = _even_mixer(h_lat, h_ctx, need_ctx, rope_hd, rope_mla, ev_w_in[e], ev_w_out[e],
                                       a_q_norm[e], a_k_norm[e], b_q_norm[e], b_w_uq[e], b_kv_norm[e], b_w_ukv[e])
        else:
            o = layer // 2
            filt = (c_filt_w1[o], c_filt_b1[o], c_filt_w2[o], c_filt_b2[o], c_filt_w3[o], c_filt_b3[o],
                    c_filt_w4[o], c_filt_freq[o])
            y_lat, y_ctx = _odd_mixer(h_lat, h_ctx, need_ctx, rope_hd, od_w_in[o], od_w_out[o], d_sink[o],
                                      c_conv_w[o], c_conv_b[o], filt, c_bias[o])
        x = x + gt_l * y_lat
        x = x + gt2_l * _conv_ffn(_modulate(_rmsnorm(x, norm_ffn[layer]), sh2_l, sc2_l),
                                  ffn_w_up[layer], ffn_conv_w[layer], ffn_conv_b[layer], ffn_w_down[layer])
        if need_ctx:
            ctx = ctx + gt_c * y_ctx
            ctx = ctx + gt2_c * _conv_ffn(_modulate(_rmsnorm(ctx, norm_ffn[layer]), sh2_c, sc2_c),
                                          ffn_w_up[layer], ffn_conv_w[layer], ffn_conv_b[layer], ffn_w_down[layer])
    return _rmsnorm(x, final_norm)
```

```python
import contextlib
import math
import numpy as np
import ml_dtypes
import concourse.bass as bass
import concourse.mybir as mybir
from concourse.bass_utils import run_bass_kernel_spmd

F32 = mybir.dt.float32
BF16 = mybir.dt.bfloat16
ALU = mybir.AluOpType
AF = mybir.ActivationFunctionType

T = 4096
M = 256
TA = T + M
D = 1024
FF = 2816
EPS = 1e-6
NFFT = 8192
MAGIC = 12582912.0
I2P = float(1.0 / (2 * math.pi))

COMPUTE = ("pe", "dve", "act", "pool")
QUEUES = ("pe", "dve", "act", "pool", "sp")
N_DMA_SEMS = {"sp": 12, "pool": 8}


class Res:
    __slots__ = ("name", "last_w", "readers", "excl")

    def __init__(self, name="", excl=False):
        self.name = name
        self.last_w = None
        self.readers = []
        self.excl = excl


class Op:
    __slots__ = ("q", "fn", "waits", "tok", "dma")

    def __init__(self, q, fn, dma):
        self.q = q
        self.fn = fn
        self.dma = dma
        self.waits = {}
        self.tok = None


class Prog:
    def __init__(self, nc):
        self.nc = nc
        self.ops = []
        self.cnt = {q: 0 for q in COMPUTE}
        self.dma_cnt = {}
        self.dma_rr = {q: 0 for q in N_DMA_SEMS}
        self.known = {q: {} for q in QUEUES}
        self.maxtok = {}

    def _deps(self, op, reads, writes):
        toks = []
        for r in reads:
            if r.last_w is not None:
                toks.append(r.last_w)
        for w in writes:
            if w.last_w is not None:
                toks.append(w.last_w)
            toks.extend(w.readers)
        kn = self.known[op.q]
        for (sk, v) in toks:
            if op.q == "pe" and sk == "pe":
                continue
            if kn.get(sk, 0) >= v:
                continue
            if op.waits.get(sk, 0) < v:
                op.waits[sk] = v
        for sk, v in op.waits.items():
            kn[sk] = max(kn.get(sk, 0), v)

    def _commit(self, op, reads, writes):
        for w in writes:
            w.last_w = op.tok
            w.readers = []
        for r in reads:
            if r not in writes:
                r.readers.append(op.tok)
                if len(r.readers) > 64:
                    best = {}
                    for (sk, v) in r.readers:
                        if best.get(sk, 0) < v:
                            best[sk] = v
                    r.readers = list(best.items())
        self.maxtok[op.tok[0]] = op.tok[1]
        self.ops.append(op)

    def op(self, q, fn, reads=(), writes=()):
        ex = [r for r in reads if r.excl and r not in writes]
        if ex:
            writes = list(writes) + ex
            reads = [r for r in reads if not r.excl]
        o = Op(q, fn, False)
        self._deps(o, reads, writes)
        self.cnt[q] += 1
        o.tok = (q, self.cnt[q])
        self._commit(o, reads, writes)
        return o

    def dma(self, q, fn, reads=(), writes=()):
        o = Op(q, fn, True)
        self._deps(o, reads, writes)
        k = self.dma_rr[q]
        self.dma_rr[q] = (k + 1) % N_DMA_SEMS[q]
        sk = ("dma", q, k)
        n = self.dma_cnt.get(sk, 0)
        if n > 0 and self.known[q].get(sk, 0) < 16 * n:
            o.waits[sk] = max(o.waits.get(sk, 0), 16 * n)
            self.known[q][sk] = 16 * n
        self.dma_cnt[sk] = n + 1
        o.tok = (sk, 16 * (n + 1))
        self._commit(o, reads, writes)
        return o

    def barrier(self):
        for q in QUEUES:
            o = Op(q, None, False)
            for sk, v in self.maxtok.items():
                if q == "pe" and sk == "pe":
                    continue
                if self.known[q].get(sk, 0) < v:
                    o.waits[sk] = v
                    self.known[q][sk] = v
            if o.waits:
                self.ops.append(o)

    def emit_block(self, sems):
        nc = self.nc
        ops = self.ops
        self.ops = []
        byq = {q: [o for o in ops if o.q == q] for q in QUEUES}
        with nc.Block() as block:
            def run(eng, q):
                for o in byq[q]:
                    for sk, v in o.waits.items():
                        eng.wait_ge(sems[sk], v)
                    if o.fn is None:
                        continue
                    ins = o.fn(eng)
                    ins.then_inc(sems[o.tok[0]], 16 if o.dma else 1)

            @block.tensor
            def _(e):
                run(e, "pe")

            @block.vector
            def _(e):
                run(e, "dve")

            @block.scalar
            def _(e):
                run(e, "act")

            @block.gpsimd
            def _(e):
                run(e, "pool")

            @block.sync
            def _(e):
                run(e, "sp")


class B:
    def __init__(self, nc, dbg, gstack):
        self.nc = nc
        self.P = Prog(nc)
        self.dbg = dbg
        self.gstack = gstack
        self.stack = gstack
        self.uid = 0
        self.consts = {}
        self.sems = {}
        for q in COMPUTE:
            self.sems[q] = gstack.enter_context(nc.semaphore("s_" + q))
        for q, n in N_DMA_SEMS.items():
            for k in range(n):
                self.sems[("dma", q, k)] = gstack.enter_context(nc.semaphore("d_%s%d" % (q, k)))
        self.psbig = gstack.enter_context(nc.psum_tensor("psbig", [128, 4096], F32))
        self.ps = [self.psbig[:, i * 512:(i + 1) * 512] for i in range(8)]
        self.psr = [Res("ps%d" % i, excl=True) for i in range(8)]
        self.psi = 0
        self.pools = None
        self.pool_i = {}

    def const(self, name):
        return self.consts[name]

    def load_consts(self, d, names_bf, names_f32):
        for n in names_bf:
            ap = d[n]
            t = self.gstack.enter_context(self.nc.sbuf_tensor("c_" + n, list(ap.shape), BF16))
            r = Res(n)
            self.dma("sp", t[:], ap, W=[r])
            self.consts[n] = (t, r)
        for n in names_f32:
            ap = d[n]
            t = self.gstack.enter_context(self.nc.sbuf_tensor("c_" + n, list(ap.shape), F32))
            r = Res(n)
            self.dma("sp", t[:], ap, W=[r])
            self.consts[n] = (t, r)
        t = self.gstack.enter_context(self.nc.sbuf_tensor("c_eps", [128, 2], F32))
        r = Res("eps")
        self.memset("dve", t[:, 0:1], EPS, [r])
        self.memset("dve", t[:, 1:2], 0.0, [r])
        self.eps_col = t
        self.eps_res = r
        self.P.barrier()
        self.P.emit_block(self.sems)

    def sb(self, shape, dtype, name=None):
        self.uid += 1
        t = self.stack.enter_context(self.nc.sbuf_tensor("%s_%d" % (name or "t", self.uid), list(shape), dtype))
        return t

    def ring(self, n, shape, dtype, name=None):
        return Ring([(self.sb(shape, dtype, name), Res(name)) for _ in range(n)])

    def psum(self, role=None):
        if role is None or self.pools is None:
            i = self.psi
            self.psi = (i + 1) % len(self.ps)
            return self.ps[i], self.psr[i]
        banks = self.pools[role]
        k = self.pool_i.get(role, 0)
        self.pool_i[role] = (k + 1) % len(banks)
        i = banks[k]
        return self.ps[i], self.psr[i]

    def set_pools(self, pools):
        self.pools = pools
        self.pool_i = {}

    @contextlib.contextmanager
    def phase(self):
        old = self.stack
        with contextlib.ExitStack() as st:
            self.stack = st
            yield
            self.P.barrier()
            self.P.emit_block(self.sems)
        self.stack = old

    def mm(self, out, lhsT, rhs, start, stop, R, W):
        self.P.op("pe", lambda e: e.matmul(out, lhsT=lhsT, rhs=rhs, start=start, stop=stop), R, W)

    def ts(self, q, out, in0, s1, s2, op0, op1, R, W):
        if op1 is None:
            self.P.op(q, lambda e: e.tensor_scalar(out=out, in0=in0, scalar1=s1, scalar2=None, op0=op0), R, W)
        else:
            self.P.op(q, lambda e: e.tensor_scalar(out=out, in0=in0, scalar1=s1, scalar2=s2, op0=op0, op1=op1), R, W)

    def tt(self, q, out, in0, in1, op, R, W):
        self.P.op(q, lambda e: e.tensor_tensor(out=out, in0=in0, in1=in1, op=op), R, W)

    def stt(self, q, out, in0, scalar, in1, op0, op1, R, W):
        self.P.op(q, lambda e: e.scalar_tensor_tensor(out=out, in0=in0, scalar=scalar, in1=in1, op0=op0, op1=op1), R, W)

    def act(self, out, in_, func, R, W, bias=None, scale=1.0, accum=None):
        def f(e):
            kw = {}
            if bias is not None:
                kw["bias"] = bias
            if accum is not None:
                kw["accum_out"] = accum
            return e.activation(out=out, in_=in_, func=func, scale=scale, **kw)
        self.P.op("act", f, R, W)

    def cp(self, q, out, in_, R, W):
        if q == "act":
            self.act(out, in_, AF.Copy, R, W)
        else:
            self.P.op(q, lambda e: e.tensor_copy(out=out, in_=in_), R, W)

    def recip(self, q, out, in_, R, W):
        self.P.op(q, lambda e: e.reciprocal(out=out, in_=in_), R, W)

    def memset(self, q, ap, val, W):
        self.P.op(q, lambda e: e.memset(ap, val), (), W)

    def dma(self, q, out, in_, R=(), W=(), slow=False):
        if slow:
            self.P.dma(q, lambda e: e.dma_start(out=out, in_=in_, allow_slow_non_contiguous=True), R, W)
        else:
            self.P.dma(q, lambda e: e.dma_start(out=out, in_=in_), R, W)


class Ring:
    def __init__(self, items):
        self.items = items
        self.i = 0

    def next(self):
        it = self.items[self.i]
        self.i = (self.i + 1) % len(self.items)
        return it


def _bf(a):
    return np.ascontiguousarray(np.asarray(a, np.float32)).astype(ml_dtypes.bfloat16)


def _rope_angles(rope_dim):
    rows = T // 64
    row_idx = np.repeat(np.arange(rows), 64).astype(np.float32)
    col_idx = np.tile(np.arange(64), rows).astype(np.float32)
    d_axis = rope_dim // 2
    inv_freq = (np.float32(10000.0) ** (-np.arange(0, d_axis, 2, dtype=np.float32) / np.float32(d_axis))).astype(np.float32)
    ang = np.concatenate([row_idx[:, None] * inv_freq, col_idx[:, None] * inv_freq], axis=-1).astype(np.float32)
    return ang


_CONST_CACHE = {}


def host_constants():
    if _CONST_CACHE:
        return _CONST_CACHE
    c = {}
    c["ident"] = _bf(np.eye(128))
    bo = np.zeros((128, 128), np.float32)
    bo[:64, :64] = 1
    bo[64:, 64:] = 1
    c["blockones"] = _bf(bo)
    c["allones"] = _bf(np.ones((128, 128)))
    angA = _rope_angles(64)
    cosA = np.ones((128, TA), np.float32)
    sinA = np.zeros((128, TA), np.float32)
    for p in range(128):
        cosA[p, M:] = np.cos(angA[:, p % 32])
        sinA[p, M:] = np.sin(angA[:, p % 32])
    c["cosA"], c["sinA"] = cosA, sinA
    pa = np.zeros((128, 128), np.float32)
    for hb in (0, 64):
        for i in range(32):
            pa[hb + 32 + i, hb + i] = -1.0
            pa[hb + i, hb + 32 + i] = 1.0
    c["permA"] = _bf(pa)
    angB = _rope_angles(32)
    cosB = np.ones((128, TA), np.float32)
    sinB = np.zeros((128, TA), np.float32)
    for p in range(64, 96):
        cosB[p, M:] = np.cos(angB[:, (p - 64) % 16])
        sinB[p, M:] = np.sin(angB[:, (p - 64) % 16])
    c["cosB"], c["sinB"] = cosB, sinB
    pb = np.zeros((128, 128), np.float32)
    for i in range(16):
        pb[80 + i, 64 + i] = -1.0
        pb[64 + i, 80 + i] = 1.0
    c["permB"] = _bf(pb)
    jj = np.arange(128)[:, None]
    rr = np.arange(128)[None, :]
    c["maskP"] = _bf(np.tile((jj >= rr).astype(np.float32), (1, 4)))
    c["maskN"] = _bf(np.tile((jj <= rr).astype(np.float32), (1, 4)))
    t = np.linspace(0.0, 1.0, T, dtype=np.float32)[:, None]
    w = (2 * np.float32(math.pi) * np.arange(T, dtype=np.float32)[:, None] / np.float32(T)).astype(np.float32)
    f = np.linspace(1e-4, 1.0, 2, dtype=np.float32)[None, :]
    z = np.concatenate([t, np.cos(f * w), -np.sin(f * w)], axis=-1).astype(np.float32)
    c["hy_zT"] = np.ascontiguousarray(z.T)
    cmin = math.log(1e-2) / 1.5
    cmax = math.log(1e-2) / 0.3
    deltas = np.abs(np.linspace(cmin, cmax, 512, dtype=np.float32))
    dec = np.exp(-t * deltas[None, :]).astype(np.float32)
    decb = dec.copy()
    decb[0, :] = 0.0
    c["hy_dec"] = np.ascontiguousarray(np.concatenate([dec, decb], axis=1))
    NE = 4224
    a = np.arange(4096, dtype=np.int64)
    prod = (a[:, None] * a[None, :]) % NFFT
    angd = prod.astype(np.float64) * (2 * math.pi / NFFT)
    C = np.zeros((NE, NE), np.float32)
    S = np.zeros((NE, NE), np.float32)
    C[:4096, :4096] = np.cos(angd)
    S[:4096, :4096] = np.sin(angd)
    alt = np.where(a % 2 == 0, 1.0, -1.0).astype(np.float32)
    C[:4096, 4096] = alt
    C[4096, :4096] = alt
    del angd, prod

    def tiles(Mx):
        Mb = _bf(Mx)
        out = np.zeros((17, 128, 33, 256), ml_dtypes.bfloat16)
        for jc in range(17):
            wd = min(256, NE - jc * 256)
            blk = Mb[:, jc * 256: jc * 256 + wd].reshape(33, 128, wd).transpose(1, 0, 2)
            out[jc, :, :, :wd] = blk
        return out
    c["dftC"] = tiles(C)
    c["dftS"] = tiles(S)
    del C, S
    wt = np.full((128, 33), 2.0 / NFFT, np.float32)
    wt[0, 0] = 1.0 / NFFT
    wt[:, 32] = 1.0 / NFFT
    c["hy_w"] = wt
    _CONST_CACHE.update(c)
    return c


PERM64 = np.concatenate([np.arange(0, 64, 2), np.arange(1, 64, 2)])
PERM32 = np.concatenate([np.arange(0, 32, 2), np.arange(1, 32, 2)])


def colvec(v, n):
    return np.ascontiguousarray(np.asarray(v, np.float32).reshape(n, 128).T)


def host_weights(inp):
    w = {}
    f32 = np.float32
    w["w_mod"] = np.ascontiguousarray(inp["w_mod"].reshape(2 * 1024, 6144))
    w["b_mod"] = np.ascontiguousarray(inp["b_mod"])
    w["norm_mix"] = np.ascontiguousarray(inp["norm_mix"])
    w["norm_ffn"] = np.ascontiguousarray(inp["norm_ffn"])
    w["final_norm"] = np.ascontiguousarray(inp["final_norm"].reshape(1, 1024))
    wi = inp["ev_w_in"][0]
    cols = []
    for h in range(8):
        cols.append(0 + h * 64 + PERM64)
    aq = np.concatenate(cols)
    ak = np.concatenate([896 + g * 64 + PERM64 for g in range(2)])
    av = 1024 + np.arange(128)
    bcq = 512 + np.arange(384)
    bckv = 1152 + np.arange(256)
    kr = 1408 + PERM32
    wev = np.zeros((1024, 1536), f32)
    wev[:, 0:512] = wi[:, aq]
    wev[:, 512:640] = wi[:, ak]
    wev[:, 640:768] = wi[:, av]
    wev[:, 768:1152] = wi[:, bcq]
    wev[:, 1152:1408] = wi[:, bckv]
    wev[:, 1408 + 64:1408 + 96] = wi[:, kr]
    w["wev"] = wev
    wuq = np.zeros((384, 1024), f32)
    uq = inp["b_w_uq"][0]
    for h in range(8):
        wuq[:, h * 128: h * 128 + 64] = uq[:, h * 96: h * 96 + 64]
        wuq[:, h * 128 + 64: h * 128 + 96] = uq[:, h * 96 + 64 + PERM32]
    w["wuq"] = wuq
    ukv = inp["b_w_ukv"][0]
    wukv = np.zeros((256, 1024), f32)
    for h in range(8):
        wukv[:, h * 64:(h + 1) * 64] = ukv[:, h * 128: h * 128 + 64]
        wukv[:, 512 + h * 64: 512 + (h + 1) * 64] = ukv[:, h * 128 + 64: h * 128 + 128]
    w["wukv"] = wukv
    w["ev_wout"] = np.ascontiguousarray(inp["ev_w_out"][0])
    ecol = np.zeros((128, 8), f32)
    ecol[:, 0] = np.tile(inp["a_q_norm"][0][PERM64], 2)
    ecol[:, 1] = np.tile(inp["a_k_norm"][0][PERM64], 2)
    ecol[:, 2:5] = colvec(inp["b_q_norm"][0], 3)
    ecol[:, 5:7] = colvec(inp["b_kv_norm"][0], 2)
    w["ecol"] = ecol
    wo = inp["od_w_in"][0]
    dq = np.concatenate([h * 64 + PERM64 for h in range(8)])
    dk = np.concatenate([2048 + g * 64 + PERM64 for g in range(2)])
    wod = np.zeros((1024, 2304), f32)
    wod[:, 0:512] = wo[:, dq]
    wod[:, 512:640] = wo[:, dk]
    wod[:, 640:768] = wo[:, 2176:2304]
    wod[:, 768:2304] = wo[:, 512:2048]
    w["wod"] = wod
    w["od_wout"] = np.ascontiguousarray(inp["od_w_out"][0])
    ocol = np.zeros((128, 80), f32)
    cw = inp["c_conv_w"][0]
    for k in range(3):
        ocol[:, k * 12:(k + 1) * 12] = colvec(cw[k], 12)
    ocol[:, 36:48] = colvec(inp["c_conv_b"][0], 12)
    ocol[:, 48:52] = colvec(inp["c_bias"][0], 4)
    ocol[:, 52:60] = np.broadcast_to(inp["d_sink"][0][None, :], (128, 8))
    ocol[:64, 60] = inp["c_filt_b1"][0]
    ocol[:64, 61] = inp["c_filt_b2"][0]
    ocol[:64, 62] = inp["c_filt_b3"][0]
    ocol[:64, 63:66] = inp["c_filt_freq"][0].T
    w["ocol"] = ocol
    w["hy_w1"] = np.ascontiguousarray(inp["c_filt_w1"][0])
    w["hy_w2"] = np.ascontiguousarray(inp["c_filt_w2"][0])
    w["hy_w3"] = np.ascontiguousarray(inp["c_filt_w3"][0])
    w["hy_w4"] = np.ascontiguousarray(inp["c_filt_w4"][0])
    w["ffn_up"] = np.ascontiguousarray(inp["ffn_w_up"].reshape(2 * 1024, 5632))
    w["ffn_down"] = np.ascontiguousarray(inp["ffn_w_down"].reshape(2 * 2816, 1024))
    fcol = np.zeros((2, 128, 176), f32)
    for l in range(2):
        for k in range(3):
            fcol[l, :, k * 44:(k + 1) * 44] = colvec(inp["ffn_conv_w"][l, k], 44)
        fcol[l, :, 132:176] = colvec(inp["ffn_conv_b"][l], 44)
    w["fcol"] = fcol.reshape(256, 176)
    return w


def load_bc(b, row_ap, q="sp", name="bc"):
    F = row_ap.shape[-1]
    t = b.sb([128, F], F32, name)
    r = Res(name)
    b.dma(q, t[:], row_ap.partition_broadcast(128), W=[r])
    return t, r


def load_w_bf16(b, dram_ap, nk, ncol, name, ngrp=1, order=None):
    t = b.sb([128, nk, ncol], BF16, name)
    if ngrp == 1:
        r = Res(name)
        for j in range(nk):
            b.dma("pool", t[:, j, :], dram_ap[j * 128:(j + 1) * 128, :], W=[r])
        return t, r
    gw = ncol // ngrp
    rs = [Res("%s%d" % (name, g)) for g in range(ngrp)]
    for g in (order or range(ngrp)):
        for j in range(nk):
            b.dma("pool", t[:, j, g * gw:(g + 1) * gw], dram_ap[j * 128:(j + 1) * 128, g * gw:(g + 1) * gw], W=[rs[g]])
    return t, rs


class NormCtx:
    def __init__(self, b):
        self.junk = b.ring(2, [128, 1024], BF16, "junk")
        self.ss = b.ring(6, [128, 4], F32, "ss")
        self.h32 = b.ring(3, [128, 1024], F32, "h32")
        self.hb = b.ring(3, [128, 1024], BF16, "hb")


def rstd_col(b, ss, rss, n):
    b.act(ss[:, 1:2], ss[:, 0:1], AF.Ln, R=[rss, b.eps_res], W=[rss], bias=b.eps_col[:, 0:1], scale=1.0 / n)
    b.act(ss[:, 2:3], ss[:, 1:2], AF.Exp, R=[rss], W=[rss], scale=-0.5)


def NOOP():
    pass


def run_pipeline(items):
    n = len(items)
    if n == 0:
        return
    S = max(len(it) for it in items)
    for step in range(n + S - 1):
        for s_ in reversed(range(S)):
            t = step - s_
            if 0 <= t < n and s_ < len(items[t]):
                items[t][s_]()


def norm_stages(b, nx, c, mult, rmult, sh, rsh, ident, rid):
    def s_sq():
        junk, rj = nx.junk.next()
        ss, rss = nx.ss.next()
        b.act(junk[:], c["x"], AF.Square, R=[c["rx"]], W=[rj, rss], accum=ss[:, 0:1])
        c["ss"] = (ss, rss)

    def s_scale():
        ss, rss = c["ss"]
        rstd_col(b, ss, rss, 1024)
        h32, rh = nx.h32.next()
        b.stt("dve", h32[:], c["x"], ss[:, 2:3], mult[:], ALU.mult, ALU.mult, R=[c["rx"], rss, rmult], W=[rh])
        hb, rhb = nx.hb.next()
        b.tt("pool", hb[:], h32[:], sh[:], ALU.add, R=[rh, rsh], W=[rhb])
        c["hb"] = (hb, rhb)

    def s_tr():
        hb, rhb = c["hb"]
        hTg, rhTg, col = c["hTg"], c["rhTg"], c["col"]
        for half in range(2):
            ps, rps = b.psum()
            for jj in range(4):
                j = half * 4 + jj
                b.mm(ps[:, jj * 128:(jj + 1) * 128], lhsT=hb[:, j * 128:(j + 1) * 128], rhs=ident[:], start=True, stop=True,
                     R=[rhb, rid], W=[rps])
            dst = hTg[:, half * 4:(half + 1) * 4, col:col + 128]
            src = ps[:, :].rearrange("p (j t) -> p j t", j=4)
            b.cp("act" if half == 0 else "dve", dst, src, R=[rps], W=[rhTg])
        if c.get("after") is not None:
            c["after"]()
    return [s_sq, s_scale, s_tr]


def mods_jobs(b, d, l, wring, bmr, orow):
    scl, rscl = b.const("scl")
    jobs = []
    for t in range(12):
        def job(t=t):
            wt, rw = wring.next()
            b.dma("sp", wt[:], d["w_mod"][l * 1024:(l + 1) * 1024, t * 512:(t + 1) * 512].rearrange("(j p) c -> p j c", p=128), W=[rw])
            bm, rbm = bmr.next()
            b.dma("sp", bm[:], d["b_mod"][l, t * 512:(t + 1) * 512].partition_broadcast(2), W=[rbm])
            ps, rps = b.psum("s")
            for j in range(8):
                b.mm(ps[0:2, :], lhsT=scl[:, 2 * j:2 * j + 2], rhs=wt[:, j, :], start=(j == 0), stop=(j == 7), R=[rscl, rw], W=[rps])
            o, ro = orow.next()
            b.tt("dve", o[:], ps[0:2, :], bm[:], ALU.add, R=[rps, rbm], W=[ro])
            b.dma("sp", d["mods"][l, :, t * 512:(t + 1) * 512], o[:], R=[ro])
        jobs.append(job)
    return jobs


def phase_mods(b, d, layers=(0,)):
    scl = b.gstack.enter_context(b.nc.sbuf_tensor("c_scl", [128, 16], F32))
    rscl = Res("scl")
    b.consts["scl"] = (scl, rscl)
    with b.phase():
        cv = b.sb([128, 16], F32, "cv")
        rcv = Res()
        b.dma("sp", cv[:], d["cvec"], W=[rcv])
        b.act(scl[:], cv[:], AF.Silu, R=[rcv], W=[rscl])
        wring = b.ring(7, [128, 8, 512], F32, "wm")
        bmr = b.ring(7, [2, 512], F32, "bm")
        orow = b.ring(4, [2, 512], F32, "orow")
        for l in layers:
            for job in mods_jobs(b, d, l, wring, bmr, orow):
                job()


def load_mod(b, d, l, which, m, name):
    return load_bc(b, d["mods"][l, which, m * 1024:(m + 1) * 1024], name=name)


def load_mult(b, d, l, which, m, gain_row, name):
    sc, rsc = load_mod(b, d, l, which, m, name + "_sc")
    g, rg = load_bc(b, gain_row, name=name + "_g")
    b.stt("dve", sc[:], sc[:], 1.0, g[:], ALU.add, ALU.mult, R=[rsc, rg], W=[rsc])
    return sc, rsc


def phase_norm(b, d, l, src_lat, src_ctx):
    with b.phase():
        ident, rid = b.const("ident")
        nx = NormCtx(b)
        xr = b.ring(7, [128, 1024], F32, "xt")
        hg = b.ring(4, [128, 8, 512], BF16, "hTg")
        items = []
        for which, src, ntile, dst in ((1, src_ctx, M // 128, d["HTc"]), (0, src_lat, T // 128, d["HTl"])):
            mult, rmult = load_mult(b, d, l, which, 1, d["norm_mix"][l, :], "m1")
            sh, rsh = load_mod(b, d, l, which, 0, "sh1")
            for g0 in range(0, ntile, 4):
                ng = min(4, ntile - g0)
                grp = {}
                for i in range(ng):
                    c = {"col": i * 128}

                    def s_load(c=c, grp=grp, i=i, src=src, g0=g0):
                        if i == 0:
                            grp["h"] = hg.next()
                        c["hTg"], c["rhTg"] = grp["h"]
                        xt, rx = xr.next()
                        b.dma("sp", xt[:], src[(g0 + i) * 128:(g0 + i + 1) * 128, :], W=[rx])
                        c["x"], c["rx"] = xt[:], rx
                    if i == ng - 1:
                        def after(c=c, dst=dst, g0=g0, ng=ng):
                            b.dma("sp", dst[:, 1 + g0 * 128: 1 + (g0 + ng) * 128].rearrange("(j p) t -> p j t", p=128),
                                  c["hTg"][:, :, 0:ng * 128], R=[c["rhTg"]])
                        c["after"] = after
                    items.append([s_load, NOOP, NOOP] + norm_stages(b, nx, c, mult, rmult, sh, rsh, ident, rid))
        run_pipeline(items)


IN_SPECS = [
    ("x", [T, D], F32), ("ctx", [M, D], F32), ("cvec", [128, 16], F32),
    ("w_mod", [2048, 6144], F32), ("b_mod", [2, 6144], F32),
    ("norm_mix", [2, D], F32), ("norm_ffn", [2, D], F32), ("final_norm", [1, D], F32),
    ("wev", [1024, 1536], F32), ("wuq", [384, 1024], F32), ("wukv", [256, 1024], F32),
    ("ev_wout", [1024, 1024], F32), ("ecol", [128, 8], F32),
    ("wod", [1024, 2304], F32), ("od_wout", [1024, 1024], F32), ("ocol", [128, 80], F32),
    ("hy_w1", [5, 64], F32), ("hy_w2", [64, 64], F32), ("hy_w3", [64, 64], F32), ("hy_w4", [64, 1024], F32),
    ("ffn_up", [2048, 5632], F32), ("ffn_down", [5632, 1024], F32), ("fcol", [256, 176], F32),
    ("ident", [128, 128], BF16), ("blockones", [128, 128], BF16), ("allones", [128, 128], BF16),
    ("permA", [128, 128], BF16), ("permB", [128, 128], BF16), ("maskP", [128, 512], BF16), ("maskN", [128, 512], BF16),
    ("cosA", [128, TA], F32), ("sinA", [128, TA], F32), ("cosB", [128, TA], F32), ("sinB", [128, TA], F32),
    ("hy_zT", [5, T], F32), ("hy_dec", [T, 1024], F32), ("hy_w", [128, 33], F32),
    ("dftC", [17, 128, 33, 256], BF16), ("dftS", [17, 128, 33, 256], BF16),
]

SCRATCH = [
    ("mods", [2, 2, 6144], F32),
    ("HTl", [D, T + 2], BF16), ("HTc", [D, M + 2], BF16),
    ("xres", [T, D], F32), ("cres", [M, D], F32),
    ("QA", [512, TA], BF16), ("KA", [128, TA], BF16), ("VA", [TA, 128], BF16),
    ("QB", [8, 96, TA], BF16), ("KBn", [512, TA], BF16), ("KR", [32, TA], BF16), ("VB", [TA, 512], BF16),
    ("OT", [D, TA], BF16),
    ("X0", [512, T], F32), ("UF", [512, T], F32), ("UT", [T, 512], BF16),
    ("YC", [33 * 128, 512], BF16), ("YS", [33 * 128, 512], BF16),
]


def build(dbg=(), stop=None):
    nc = bass.Bass("TRN2", target_bir_lowering=False)
    d = {}
    for name, shape, dt in IN_SPECS:
        d[name] = nc.dram_tensor(name, list(shape), dt, kind="ExternalInput").ap()
    for name, shape, dt in SCRATCH:
        if name in dbg:
            d[name] = nc.dram_tensor(name, list(shape), dt, kind="ExternalOutput").ap()
        else:
            d[name] = nc.dram_tensor(name, list(shape), dt).ap()
    d["out"] = nc.dram_tensor("out", [T, D], F32, kind="ExternalOutput").ap()
    with contextlib.ExitStack() as gstack:
        b = B(nc, dbg, gstack)
        b.load_consts(d, ["ident", "blockones", "allones", "permA", "permB", "maskP", "maskN"], ["ecol", "ocol", "hy_w"])
        emit_program(b, d, stop)
    return nc


def emit_program(b, d, stop):
    with b.phase():
        z = b.sb([128, 8, 2], BF16, "z")
        rz = Res()
        b.memset("dve", z[:], 0.0, [rz])
        for t, n in ((d["HTl"], T), (d["HTc"], M)):
            for c in (0, n + 1):
                b.dma("sp", t[:, c:c + 1].rearrange("(j p) t -> p j t", p=128), z[:, :, 0:1], R=[rz], slow=True)
    phase_mods(b, d)
    if stop == "mods":
        return
    phase_norm(b, d, 0, d["x"], d["ctx"])
    if stop == "norm0":
        return
    phase_proj_even(b, d)
    if stop == "proj0":
        return
    phase_attn_even(b, d)
    if stop == "attn0":
        return
    phase_outproj(b, d, 0, d["ev_wout"], d["x"], d["ctx"], True)
    if stop == "outproj0":
        return
    phase_ffn(b, d, 0, True)
    if stop == "ffn0":
        return
    phase_norm(b, d, 1, d["xres"], d["cres"])
    phase_proj_odd(b, d)
    if stop == "proj1":
        return
    phase_attn_odd(b, d)
    if stop == "attn1":
        return
    phase_hyena_fwd(b, d)
    phase_hyena_inv(b, d)
    if stop == "hyena":
        return
    phase_outproj(b, d, 1, d["od_wout"], d["xres"], None, False)
    phase_ffn(b, d, 1, False, final=True)


_W_CACHE = {}


def core_inputs(inp, bidx):
    c = host_constants()
    if "w" not in _W_CACHE:
        _W_CACHE["w"] = host_weights(inp)
    w = _W_CACHE["w"]
    m = dict(c)
    m.update(w)
    m["x"] = np.ascontiguousarray(inp["x"][bidx])
    m["ctx"] = np.ascontiguousarray(inp["ctx"][bidx])
    cv = np.zeros((128, 16), np.float32)
    cv[:, 0::2] = colvec(inp["c"][bidx], 8)
    cv[:, 1::2] = colvec(inp["c_ctx"], 8)
    m["cvec"] = cv
    return m


class ProjTmp:
    def __init__(self, b):
        self.sq = b.ring(3, [128, 512], BF16, "sq")
        self.q32 = b.ring(4, [128, 512], F32, "q32")
        self.lnt = b.ring(2, [128, 512], F32, "lnt")
        self.rs = b.ring(3, [128, 512], F32, "rs")
        self.qn = b.ring(3, [128, 512], BF16, "qn")
        self.t1 = b.ring(3, [128, 512], BF16, "t1")
        self.t2 = b.ring(3, [128, 512], BF16, "t2")
        self.ob = b.ring(4, [128, 512], BF16, "ob")
        self.ss = b.ring(6, [128, 4], F32, "sstok")


def rs_from_ss(b, tmp, ssps, rssps, W, n):
    lnt, rl = tmp.lnt.next()
    b.act(lnt[:, :W], ssps[:, :W], AF.Ln, R=[rssps, b.eps_res], W=[rl], bias=b.eps_col[:, 0:1], scale=1.0 / n)
    rs, rrs = tmp.rs.next()
    b.act(rs[:, :W], lnt[:, :W], AF.Exp, R=[rl], W=[rrs], scale=-0.5)
    return rs, rrs


def proj_fm(b, ps, rps, W, wt, rw, nk, c0, hT, rhT):
    for j in range(nk):
        b.mm(ps[:, :W], lhsT=wt[:, j, c0:c0 + 128], rhs=hT[:, j, :W], start=(j == 0), stop=(j == nk - 1), R=[rw, rhT], W=[rps])


def rope_stages(b, tmp, c, W, perm, rperm, cos, sin, rtab, store):
    ident, rid = b.const("ident")

    def s_swap():
        t1, rt1 = tmp.t1.next()
        t2, rt2 = tmp.t2.next()
        if c.get("xps") is not None:
            ps, rps = c["xps"]
            b.tt("dve", t1[:, :W], ps[:, :W], cos[:, :W], ALU.mult, R=[rps, rtab], W=[rt1])
            b.tt("dve", t2[:, :W], ps[:, :W], sin[:, :W], ALU.mult, R=[rps, rtab], W=[rt2])
        else:
            qn, rqn = c["qn"]
            b.tt("dve", t1[:, :W], qn[:, :W], cos[:, :W], ALU.mult, R=[rqn, rtab], W=[rt1])
            b.tt("pool", t2[:, :W], qn[:, :W], sin[:, :W], ALU.mult, R=[rqn, rtab], W=[rt2])
        c["t12"] = (t1, rt1, t2, rt2)

    def s_comb():
        t1, rt1, t2, rt2 = c["t12"]
        sw, rsw = b.psum("w")
        b.mm(sw[:, :W], lhsT=ident[:], rhs=t1[:, :W], start=True, stop=False, R=[rid, rt1], W=[rsw])
        b.mm(sw[:, :W], lhsT=perm[:], rhs=t2[:, :W], start=False, stop=True, R=[rperm, rt2], W=[rsw])
        ob, rob = tmp.ob.next()
        b.cp("act", ob[:, :W], sw[:, :W], R=[rsw], W=[rob])
        store(ob, rob)
    return s_swap, s_comb


def groups_all():
    g = [(True, 1, 0, M)]
    for i in range(T // 512):
        g.append((False, 1 + 512 * i, M + 512 * i, 512))
    return g


def load_scaled_w(b, dram_ap, nk, ncol, colbase, name):
    ecol, recol = b.const("ecol")
    st = b.sb([128, nk, ncol], F32, name + "_st")
    rst = Res()
    t = b.sb([128, nk, ncol], BF16, name)
    r = Res(name)
    for j in range(nk):
        b.dma("sp", st[:, j, :], dram_ap[j * 128:(j + 1) * 128, :], W=[rst])
    for j in range(nk):
        b.ts("dve", t[:, j, :], st[:, j, :], ecol[:, colbase + j:colbase + j + 1], None, ALU.mult, None, R=[rst, recol], W=[r])
    return t, r


def phase_proj_even(b, d):
    with b.phase():
        b.set_pools({"p": [0, 1, 2], "s": [3, 4], "w": [5, 6, 7]})
        ecol, recol = b.const("ecol")
        bo, rbo = b.const("blockones")
        ao, rao = b.const("allones")
        pA, rpA = b.const("permA")
        pB, rpB = b.const("permB")
        wev, rwev = load_w_bf16(b, d["wev"], 8, 1536, "wev")
        wuq, rwuq = load_scaled_w(b, d["wuq"], 3, 1024, 2, "wuq")
        wukv, rwukv = load_scaled_w(b, d["wukv"], 2, 1024, 5, "wukv")
        tmp = ProjTmp(b)
        hg = b.ring(2, [128, 8, 512], BF16, "hT")
        tabs = b.ring(2, [128, 4, 512], F32, "tabs")
        cqr = b.ring(2, [128, 3, 512], BF16, "cq")
        rtabr = b.ring(2, [128, 2, 512], F32, "rtab")
        ckvr = b.ring(2, [128, 2, 512], BF16, "ckv")
        sqkvr = b.ring(2, [128, 2, 512], BF16, "sqkv")
        var = b.ring(3, [128, 128], BF16, "va")
        vbr = b.ring(3, [128, 512], BF16, "vb")
        items = []
        mwr = b.ring(2, [128, 8, 512], F32, "wm")
        mbr = b.ring(2, [2, 512], F32, "bm")
        mor = b.ring(2, [2, 512], F32, "orow")
        side = mods_jobs(b, d, 1, mwr, mbr, mor)
        for gi, (isc, sc0, c0, W) in enumerate(groups_all()):
            G = {}
            for _ in range(2 if gi < 3 else 1):
                if side:
                    items.append([side.pop(0)])

            def s_load(G=G, isc=isc, sc0=sc0, c0=c0, W=W):
                hT, rhT = hg.next()
                src = d["HTc"] if isc else d["HTl"]
                b.dma("sp", hT[:, :, :W], src[:, sc0:sc0 + W].rearrange("(j p) t -> p j t", p=128), W=[rhT])
                tab, rtab = tabs.next()
                for i, nm in enumerate(("cosA", "sinA", "cosB", "sinB")):
                    b.dma("sp", tab[:, i, :W], d[nm][:, c0:c0 + W], W=[rtab])
                G["hT"] = (hT, rhT)
                G["tab"] = (tab, rtab)
            items.append([s_load])

            for cidx in range(5):
                c = {}

                def a0(c=c, G=G, cidx=cidx, W=W):
                    ps, rps = b.psum("p")
                    hT, rhT = G["hT"]
                    proj_fm(b, ps, rps, W, wev, rwev, 8, cidx * 128, hT, rhT)
                    c["ps"] = (ps, rps)

                def a1(c=c, W=W):
                    ps, rps = c["ps"]
                    sq, rsq = tmp.sq.next()
                    b.act(sq[:, :W], ps[:, :W], AF.Square, R=[rps], W=[rsq])
                    q32, rq32 = tmp.q32.next()
                    b.cp("dve", q32[:, :W], ps[:, :W], R=[rps], W=[rq32])
                    c["q32"] = (q32, rq32)
                    ssps, rssps = b.psum("s")
                    b.mm(ssps[:, :W], lhsT=bo[:], rhs=sq[:, :W], start=True, stop=True, R=[rbo, rsq], W=[rssps])
                    c["ss"] = (ssps, rssps)

                def a2(c=c, W=W):
                    ssps, rssps = c["ss"]
                    c["rs"] = rs_from_ss(b, tmp, ssps, rssps, W, 64)

                def store(ob, rob, cidx=cidx, c0=c0, W=W):
                    if cidx < 4:
                        b.dma("sp", d["QA"][cidx * 128:(cidx + 1) * 128, c0:c0 + W], ob[:, :W], R=[rob])
                    else:
                        b.dma("sp", d["KA"][:, c0:c0 + W], ob[:, :W], R=[rob])

                def tabA(G=G):
                    tab, rtab = G["tab"]
                    return tab[:, 0, :], tab[:, 1, :], rtab

                def a3(c=c, W=W, cidx=cidx, G=G, store=store):
                    q32, rq32 = c["q32"]
                    rs, rrs = c["rs"]
                    gcol = ecol[:, 0:1] if cidx < 4 else ecol[:, 1:2]
                    qn, rqn = tmp.qn.next()
                    b.stt("dve", qn[:, :W], q32[:, :W], gcol, rs[:, :W], ALU.mult, ALU.mult, R=[rq32, rrs, recol], W=[rqn])
                    c["qn"] = (qn, rqn)
                    tab, rtab = G["tab"]
                    sw_, cb_ = rope_stages(b, tmp, c, W, pA, rpA, tab[:, 0, :], tab[:, 1, :], rtab, store)
                    c["comb"] = cb_
                    sw_()

                def a4(c=c):
                    c["comb"]()
                items.append([a0, a1, a2, a3, a4])

            for sub in range(W // 128):
                c = {}

                def v0(c=c, G=G, sub=sub):
                    hT, rhT = G["hT"]
                    ps, rps = b.psum("p")
                    for j in range(8):
                        b.mm(ps[:, 0:128], lhsT=hT[:, j, sub * 128:(sub + 1) * 128], rhs=wev[:, j, 640:768], start=(j == 0), stop=(j == 7),
                             R=[rhT, rwev], W=[rps])
                    c["ps"] = (ps, rps)

                def v1(c=c, sub=sub, c0=c0):
                    ps, rps = c["ps"]
                    va, rva = var.next()
                    b.cp("act", va[:], ps[:, 0:128], R=[rps], W=[rva])
                    b.dma("sp", d["VA"][c0 + sub * 128:c0 + (sub + 1) * 128, :], va[:], R=[rva])
                items.append([v0, v1])

            for j in range(3):
                c = {}

                def q0(c=c, G=G, j=j, W=W):
                    hT, rhT = G["hT"]
                    ps, rps = b.psum("p")
                    proj_fm(b, ps, rps, W, wev, rwev, 8, 768 + j * 128, hT, rhT)
                    c["ps"] = (ps, rps)

                def q1(c=c, G=G, j=j, W=W):
                    ps, rps = c["ps"]
                    if j == 0:
                        G["cq"] = cqr.next()
                        G["ssq"] = b.psum("s")
                    cq, rcq = G["cq"]
                    ssq, rssq = G["ssq"]
                    b.cp("dve", cq[:, j, :W], ps[:, :W], R=[rps], W=[rcq])
                    sq, rsq = tmp.sq.next()
                    b.act(sq[:, :W], ps[:, :W], AF.Square, R=[rps], W=[rsq])
                    b.mm(ssq[:, :W], lhsT=ao[:], rhs=sq[:, :W], start=(j == 0), stop=(j == 2), R=[rao, rsq], W=[rssq])

                def q2(G=G, j=j, W=W):
                    if j == 2:
                        ssq, rssq = G["ssq"]
                        rs, rrs = rs_from_ss(b, tmp, ssq, rssq, W, 384)
                        tab, rtab = G["tab"]
                        rt, rrt = rtabr.next()
                        b.tt("dve", rt[:, 0, :W], rs[:, :W], tab[:, 2, :W], ALU.mult, R=[rrs, rtab], W=[rrt])
                        b.tt("pool", rt[:, 1, :W], rs[:, :W], tab[:, 3, :W], ALU.mult, R=[rrs, rtab], W=[rrt])
                        G["rtab"] = (rt, rrt)
                items.append([q0, q1, q2])
            for h in range(8):
                c = {}

                def h0(c=c, G=G, h=h, W=W):
                    cq, rcq = G["cq"]
                    ps, rps = b.psum("p")
                    for j in range(3):
                        b.mm(ps[:, :W], lhsT=wuq[:, j, h * 128:(h + 1) * 128], rhs=cq[:, j, :W], start=(j == 0), stop=(j == 2),
                             R=[rwuq, rcq], W=[rps])
                    c["ps"] = (ps, rps)

                def storeq(ob, rob, h=h, c0=c0, W=W):
                    b.dma("sp", d["QB"][h, :, c0:c0 + W], ob[0:96, :W], R=[rob])

                def h1(c=c, G=G, W=W, storeq=storeq):
                    c["xps"] = c["ps"]
                    rt, rrt = G["rtab"]
                    sw_, cb_ = rope_stages(b, tmp, c, W, pB, rpB, rt[:, 0, :], rt[:, 1, :], rrt, storeq)
                    c["comb"] = cb_
                    sw_()

                def h2(c=c):
                    c["comb"]()
                items.append([h0, h1, h2])

            for j in range(2):
                c = {}

                def k0(c=c, G=G, j=j, W=W):
                    hT, rhT = G["hT"]
                    ps, rps = b.psum("p")
                    proj_fm(b, ps, rps, W, wev, rwev, 8, 1152 + j * 128, hT, rhT)
                    c["ps"] = (ps, rps)

                def k1(c=c, G=G, j=j, W=W):
                    ps, rps = c["ps"]
                    if j == 0:
                        G["ckv"] = ckvr.next()
                        G["sqkv"] = sqkvr.next()
                        G["sskv"] = b.psum("s")
                    ckv, rckv = G["ckv"]
                    sqkv, rsqkv = G["sqkv"]
                    sskv, rsskv = G["sskv"]
                    b.cp("dve", ckv[:, j, :W], ps[:, :W], R=[rps], W=[rckv])
                    b.act(sqkv[:, j, :W], ps[:, :W], AF.Square, R=[rps], W=[rsqkv])
                    b.mm(sskv[:, :W], lhsT=ao[:], rhs=sqkv[:, j, :W], start=(j == 0), stop=(j == 1), R=[rao, rsqkv], W=[rsskv])

                def k2(G=G, j=j, W=W):
                    if j == 1:
                        sskv, rsskv = G["sskv"]
                        G["rskv"] = rs_from_ss(b, tmp, sskv, rsskv, W, 256)
                items.append([k0, k1, k2])
            for cc in range(4):
                c = {}

                def n0(c=c, G=G, cc=cc, W=W):
                    ckv, rckv = G["ckv"]
                    ps, rps = b.psum("p")
                    for j in range(2):
                        b.mm(ps[:, :W], lhsT=wukv[:, j, cc * 128:(cc + 1) * 128], rhs=ckv[:, j, :W], start=(j == 0), stop=(j == 1),
                             R=[rwukv, rckv], W=[rps])
                    c["ps"] = (ps, rps)

                def n1(c=c, G=G, cc=cc, W=W, c0=c0):
                    ps, rps = c["ps"]
                    rskv, rrskv = G["rskv"]
                    ob, rob = tmp.ob.next()
                    b.tt("dve", ob[:, :W], ps[:, :W], rskv[:, :W], ALU.mult, R=[rps, rrskv], W=[rob])
                    b.dma("sp", d["KBn"][cc * 128:(cc + 1) * 128, c0:c0 + W], ob[:, :W], R=[rob])
                items.append([n0, n1])
            for sub in range(W // 128):
                c = {}

                def w0(c=c, G=G, sub=sub):
                    ckv, rckv = G["ckv"]
                    sqkv, rsqkv = G["sqkv"]
                    ps, rps = b.psum("p")
                    pss, rpss = b.psum("s")
                    for j in range(2):
                        b.mm(ps[:, :], lhsT=ckv[:, j, sub * 128:(sub + 1) * 128], rhs=wukv[:, j, 512:1024], start=(j == 0), stop=(j == 1),
                             R=[rckv, rwukv], W=[rps])
                    for j in range(2):
                        b.mm(pss[:, 0:1], lhsT=sqkv[:, j, sub * 128:(sub + 1) * 128], rhs=ao[:, 0:1], start=(j == 0), stop=(j == 1),
                             R=[rsqkv, rao], W=[rpss])
                    c["ps"] = (ps, rps)
                    c["pss"] = (pss, rpss)

                def w1(c=c):
                    pss, rpss = c["pss"]
                    ss, rss = tmp.ss.next()
                    b.cp("dve", ss[:, 0:1], pss[:, 0:1], R=[rpss], W=[rss])
                    rstd_col(b, ss, rss, 256)
                    c["ss"] = (ss, rss)

                def w2(c=c, sub=sub, c0=c0):
                    ps, rps = c["ps"]
                    ss, rss = c["ss"]
                    vb, rvb = vbr.next()
                    b.ts("dve", vb[:], ps[:, :], ss[:, 2:3], None, ALU.mult, None, R=[rps, rss], W=[rvb])
                    b.dma("sp", d["VB"][c0 + sub * 128:c0 + (sub + 1) * 128, :], vb[:], R=[rvb])
                items.append([w0, w1, w2])
            c = {}

            def r0(c=c, G=G, W=W):
                hT, rhT = G["hT"]
                ps, rps = b.psum("p")
                proj_fm(b, ps, rps, W, wev, rwev, 8, 1408, hT, rhT)
                c["ps"] = (ps, rps)

            def storekr(ob, rob, c0=c0, W=W):
                b.dma("sp", d["KR"][:, c0:c0 + W], ob[64:96, :W], R=[rob])

            def r1(c=c, G=G, W=W, storekr=storekr):
                c["xps"] = c["ps"]
                tab, rtab = G["tab"]
                sw_, cb_ = rope_stages(b, tmp, c, W, pB, rpB, tab[:, 2, :], tab[:, 3, :], rtab, storekr)
                c["comb"] = cb_
                sw_()

            def r2(c=c):
                c["comb"]()
            items.append([r0, r1, r2])
        for job in side:
            items.append([job])
        run_pipeline(items)
        b.set_pools(None)


def qtiles_all(with_ctx=True):
    q = []
    if with_ctx:
        q.append((0, M, [0, 1]))
    for i in range(T // 512):
        q.append((M + 512 * i, 512, list(range(TA // 128))))
    return q


def phase_attn_even(b, d):
    with b.phase():
        import os
        nhead_dbg = int(os.environ.get("ATT_NH", "99"))
        Kr = b.ring(2, [128, TA], BF16, "Kt")
        Vr = b.ring(2, [128, TA // 128, 128], BF16, "Vt")
        for vt, rv in Vr.items:
            b.memset("pool", vt[:, :, 64:128], 1.0, [rv])
        Qr = b.ring(3, [128, 512], BF16, "Qt")
        Pr = b.ring(3, [128, 3, 512], BF16, "Pt")
        recr = b.ring(2, [64, 512], F32, "rec")
        outr = b.ring(3, [64, 512], BF16, "ao")
        sbanks = [0, 1, 2, 3, 4, 5]
        si = [0]
        oi = [0]

        stageA = []
        stageB = []

        def run_group(kparts, vsrc, dq, heads, scale):
            grp = {}

            def load_kv():
                Kt, rK = Kr.next()
                for ap, pb in kparts:
                    n = ap.shape[0]
                    b.dma("sp", Kt[pb:pb + n, :], ap, W=[rK])
                Vt, rV = Vr.next()
                b.dma("sp", Vt[:, :, 0:64], vsrc.rearrange("(c p) d -> p c d", p=128), W=[rV])
                grp["K"] = (Kt, rK)
                grp["V"] = (Vt, rV)
            first_in_group = [True]
            NB = 3
            for (qap, orow) in heads:
                for (c0, W, kcs) in qtiles_all():
                    qt = {}
                    batches = [kcs[i:i + NB] for i in range(0, len(kcs), NB)]
                    for bi_, batch in enumerate(batches):
                        ch = {}

                        def A(bi_=bi_, batch=batch, qt=qt, ch=ch, c0=c0, W=W, qap=qap, lk=(first_in_group[0] and bi_ == 0)):
                            if lk:
                                load_kv()
                            if bi_ == 0:
                                Qt, rQ = Qr.next()
                                b.dma("sp", Qt[0:dq, :W], qap[:, c0:c0 + W], W=[rQ])
                                qt["Q"] = (Qt, rQ)
                                ob = 6 + oi[0]
                                oi[0] ^= 1
                                qt["O"] = (b.ps[ob], b.psr[ob])
                                qt["KV"] = (grp["K"], grp["V"])
                            Qt, rQ = qt["Q"]
                            (Kt, rK), _ = qt["KV"]
                            base = 3 * si[0]
                            si[0] ^= 1
                            nb = len(batch)
                            for j, kc in enumerate(batch):
                                b.mm(b.ps[base + j][:, :W], lhsT=Kt[0:dq, kc * 128:(kc + 1) * 128], rhs=Qt[0:dq, :W], start=True, stop=True,
                                     R=[rK, rQ], W=[b.psr[base + j]])
                            Pt, rP = Pr.next()
                            src = b.psbig[:, base * 512:(base + nb) * 512].rearrange("p (n w) -> p n w", w=512)[:, :, :W]
                            b.act(Pt[:, 0:nb, :W], src, AF.Exp, R=[b.psr[base + j] for j in range(nb)], W=[rP], scale=scale)
                            ch["P"] = (Pt, rP)

                        def Bf(bi_=bi_, batch=batch, qt=qt, ch=ch, c0=c0, W=W, orow=orow, nbt=len(batches)):
                            ops_, rops = qt["O"]
                            _, (Vt, rV) = qt["KV"]
                            Pt, rP = ch["P"]
                            for j, kc in enumerate(batch):
                                b.mm(ops_[:, :W], lhsT=Vt[:, kc, :], rhs=Pt[:, j, :W], start=(bi_ == 0 and j == 0),
                                     stop=(bi_ == nbt - 1 and j == len(batch) - 1), R=[rV, rP], W=[rops])
                            if bi_ == nbt - 1:
                                rec, rrec = recr.next()
                                b.recip("dve", rec[:, :W], ops_[64:128, :W], R=[rops], W=[rrec])
                                ao, rao = outr.next()
                                b.tt("dve", ao[:, :W], ops_[0:64, :W], rec[:, :W], ALU.mult, R=[rops, rrec], W=[rao])
                                b.dma("sp", d["OT"][orow:orow + 64, c0:c0 + W], ao[:, :W], R=[rao])
                        stageA.append(A)
                        stageB.append(Bf)
                        first_in_group[0] = False

        nh = 0
        for g in range(2):
            heads = [(d["QA"][h * 64:(h + 1) * 64, :], h * 64) for h in range(4 * g, 4 * g + 4)]
            heads = heads[:max(0, nhead_dbg - nh)]
            nh += len(heads)
            if heads:
                run_group([(d["KA"][g * 64:(g + 1) * 64, :], 0)], d["VA"][:, g * 64:(g + 1) * 64], 64, heads, 0.125)
        for h in range(8):
            if nh >= nhead_dbg:
                break
            nh += 1
            run_group([(d["KBn"][h * 64:(h + 1) * 64, :], 0), (d["KR"][:, :], 64)], d["VB"][:, h * 64:(h + 1) * 64], 96,
                      [(d["QB"][h], 512 + h * 64)], 96 ** -0.5)
        LOOK = 1
        n = len(stageA)
        for i in range(n + LOOK):
            if i < n:
                stageA[i]()
            if i - LOOK >= 0:
                stageB[i - LOOK]()


def phase_outproj(b, d, l, wout_ap, src_lat, src_ctx, do_ctx):
    with b.phase():
        ident, rid = b.const("ident")
        wout, rwout = load_w_bf16(b, wout_ap, 8, 1024, "wout")
        nx = NormCtx(b)
        xr = b.ring(7, [128, 1024], F32, "xt")
        og = b.ring(3, [128, 8, 512], BF16, "OTg")
        hg = b.ring(3, [128, 8, 512], BF16, "hTg")
        x1r = b.ring(4, [128, 1024], F32, "x1")
        jobs = []
        if do_ctx:
            jobs.append((1, src_ctx, d["cres"], M // 128, 0, d["HTc"]))
        jobs.append((0, src_lat, d["xres"], T // 128, M, d["HTl"]))
        items = []
        for which, src, dst, ntile, otc0, HT in jobs:
            gt, rgt = load_mod(b, d, l, which, 2, "gt")
            mult2, rmult2 = load_mult(b, d, l, which, 4, d["norm_ffn"][l, :], "m2")
            sh2, rsh2 = load_mod(b, d, l, which, 3, "sh2")
            for g0 in range(0, ntile, 4):
                ng = min(4, ntile - g0)
                grp = {}
                for i in range(ng):
                    c = {"col": i * 128}

                    def s_ld(c=c, grp=grp, i=i, src=src, g0=g0, ng=ng, otc0=otc0):
                        if i == 0:
                            OTg, rOT = og.next()
                            b.dma("sp", OTg[:, :, 0:ng * 128],
                                  d["OT"][:, otc0 + g0 * 128: otc0 + (g0 + ng) * 128].rearrange("(j p) t -> p j t", p=128), W=[rOT])
                            grp["OT"] = (OTg, rOT)
                        r0 = (g0 + i) * 128
                        xt, rx = xr.next()
                        b.dma("sp", xt[:], src[r0:r0 + 128, :], W=[rx])
                        c["xt"] = (xt, rx)

                    def s_mm(c=c, grp=grp, i=i):
                        if i == 0:
                            grp["h"] = hg.next()
                        OTg, rOT = grp["OT"]
                        c["hTg"], c["rhTg"] = grp["h"]
                        c["ps"] = []
                        for half in range(2):
                            ps, rps = b.psum()
                            for j in range(8):
                                b.mm(ps[:, :], lhsT=OTg[:, j, i * 128:(i + 1) * 128], rhs=wout[:, j, half * 512:(half + 1) * 512],
                                     start=(j == 0), stop=(j == 7), R=[rOT, rwout], W=[rps])
                            c["ps"].append((ps, rps))

                    def s_res(c=c, i=i, g0=g0, dst=dst, gt=gt, rgt=rgt):
                        xt, rx = c["xt"]
                        x1, rx1 = x1r.next()
                        for half in range(2):
                            ps, rps = c["ps"][half]
                            b.tt("dve", x1[:, half * 512:(half + 1) * 512], ps[:, :], gt[:, half * 512:(half + 1) * 512], ALU.mult,
                                 R=[rps, rgt], W=[rx1])
                        b.tt("pool", x1[:], x1[:], xt[:], ALU.add, R=[rx1, rx], W=[rx1])
                        r0 = (g0 + i) * 128
                        b.dma("sp", dst[r0:r0 + 128, :], x1[:], R=[rx1])
                        c["x"], c["rx"] = x1[:], rx1
                    if i == ng - 1:
                        def after(c=c, HT=HT, g0=g0, ng=ng):
                            b.dma("sp", HT[:, 1 + g0 * 128: 1 + (g0 + ng) * 128].rearrange("(j p) t -> p j t", p=128),
                                  c["hTg"][:, :, 0:ng * 128], R=[c["rhTg"]])
                        c["after"] = after
                    st = norm_stages(b, nx, c, mult2, rmult2, sh2, rsh2, ident, rid)

                    def s_res_sq(s_res=s_res, sq=st[0]):
                        s_res()
                        sq()
                    items.append([s_ld, NOOP, NOOP, s_mm, s_res_sq, st[1], st[2]])
        run_pipeline(items)


FT = 384


def phase_ffn(b, d, l, do_ctx, final=False):
    with b.phase():
        if final:
            fng, rfng = load_bc(b, d["final_norm"][0, :], name="fn")
            fss = b.ring(4, [128, 4], F32, "fss")
        import os
        ntl = int(os.environ.get("FFN_NT", "99"))
        wup, rwup_g = load_w_bf16(b, d["ffn_up"][l * 1024:(l + 1) * 1024, :], 8, 5632, "wup", ngrp=4, order=[0, 2, 1, 3])
        wdn, rwdn = load_w_bf16(b, d["ffn_down"][l * 2816:(l + 1) * 2816, :], 22, 1024, "wdn")
        fc = b.sb([128, 176], F32, "fcol")
        rfc = Res()
        b.dma("sp", fc[:], d["fcol"][l * 128:(l + 1) * 128, :], W=[rfc])
        hTr = b.ring(2, [128, 8, FT + 2], BF16, "h2T")
        hid = b.sb([128, 22, FT], BF16, "hid")
        rhid = [Res("hid%d" % j) for j in range(22)]
        gar = b.ring(2, [128, FT], F32, "ga")
        var_ = b.ring(2, [128, FT], F32, "va")
        sgr = b.ring(2, [128, FT], F32, "sg")
        xr = b.ring(2, [128, 1024], F32, "xt")
        yr = b.ring(2, [128, 1024], F32, "yt")
        jobs = []
        if do_ctx:
            jobs.append((1, d["cres"], d["HTc"], [(0, M)]))
        lat_tiles = [(t0, min(FT, T - t0)) for t0 in range(0, T, FT)][:ntl]
        jobs.append((0, d["xres"], d["HTl"], lat_tiles))

        def conv(ps, rps, n, ch, dst, rdst, q2):
            w0 = fc[:, ch:ch + 1]
            w1 = fc[:, 44 + ch:44 + ch + 1]
            w2 = fc[:, 88 + ch:88 + ch + 1]
            bb = fc[:, 132 + ch:132 + ch + 1]
            b.act(dst[:, :n], ps[:, 1:n + 1], AF.Identity, R=[rps, rfc], W=[rdst], bias=bb, scale=w1)
            b.stt("dve", dst[:, :n], ps[:, 0:n], w0, dst[:, :n], ALU.mult, ALU.add, R=[rps, rfc, rdst], W=[rdst])
            b.stt("dve", dst[:, :n], ps[:, 2:n + 2], w2, dst[:, :n], ALU.mult, ALU.add, R=[rps, rfc, rdst], W=[rdst])

        for which, xdst, HT, tiles in jobs:
            gt2, rgt2 = load_mod(b, d, l, which, 5, "gt2")
            pre = {}

            def load_hT(ti, HT=HT, tiles=tiles, pre=pre):
                t0_, n_ = tiles[ti]
                hT_, rhT_ = hTr.next()
                b.dma("sp", hT_[:, :, 0:n_ + 2], HT[:, t0_:t0_ + n_ + 2].rearrange("(j p) t -> p j t", p=128), W=[rhT_])
                pre[ti] = (hT_, rhT_)
            load_hT(0)
            for ti, (t0, n) in enumerate(tiles):
                hT, rhT = pre.pop(ti)
                for j in range(22):
                    psg, rpsg = b.psum()
                    for k in range(8):
                        b.mm(psg[:, :n + 2], lhsT=wup[:, k, j * 128:(j + 1) * 128], rhs=hT[:, k, :n + 2], start=(k == 0), stop=(k == 7),
                             R=[rwup_g[j // 11], rhT], W=[rpsg])
                    psv, rpsv = b.psum()
                    for k in range(8):
                        b.mm(psv[:, :n + 2], lhsT=wup[:, k, (22 + j) * 128:(23 + j) * 128], rhs=hT[:, k, :n + 2], start=(k == 0), stop=(k == 7),
                             R=[rwup_g[(22 + j) // 11], rhT], W=[rpsv])
                    ga, rga = gar.next()
                    conv(psg, rpsg, n, j, ga, rga, "dve")
                    va, rva = var_.next()
                    conv(psv, rpsv, n, 22 + j, va, rva, "dve")
                    sg, rsg = sgr.next()
                    b.act(sg[:, :n], ga[:, :n], AF.Silu, R=[rga], W=[rsg])
                    b.tt("pool", hid[:, j, :n], sg[:, :n], va[:, :n], ALU.mult, R=[rsg, rva], W=[rhid[j]])
                if ti + 1 < len(tiles):
                    load_hT(ti + 1)
                for sub in range((n + 127) // 128):
                    m = min(128, n - sub * 128)
                    r0 = t0 + sub * 128
                    xt, rx = xr.next()
                    b.dma("sp", xt[:m, :], xdst[r0:r0 + m, :], W=[rx])
                    yt, ry = yr.next()
                    for half in range(2):
                        ps, rps = b.psum()
                        for j in range(22):
                            b.mm(ps[:m, :], lhsT=hid[:, j, sub * 128:sub * 128 + m], rhs=wdn[:, j, half * 512:(half + 1) * 512],
                                 start=(j == 0), stop=(j == 21), R=[rhid[j], rwdn], W=[rps])
                        b.tt("dve", yt[:m, half * 512:(half + 1) * 512], ps[:m, :], gt2[:m, half * 512:(half + 1) * 512], ALU.mult,
                             R=[rps, rgt2], W=[ry])
                    b.tt("pool", yt[:m, :], yt[:m, :], xt[:m, :], ALU.add, R=[ry, rx], W=[ry])
                    if final and which == 0:
                        ss, rss = fss.next()
                        b.act(xt[:m, :], yt[:m, :], AF.Square, R=[ry], W=[rx, rss], accum=ss[:m, 0:1])
                        rstd_col(b, ss, rss, 1024)
                        b.stt("dve", yt[:m, :], yt[:m, :], ss[:m, 2:3], fng[:m, :], ALU.mult, ALU.mult, R=[ry, rss, rfng], W=[ry])
                        b.dma("sp", d["out"][r0:r0 + m, :], yt[:m, :], R=[ry])
                    else:
                        b.dma("sp", xdst[r0:r0 + m, :], yt[:m, :], R=[ry])


HT_ = 510


def phase_proj_odd(b, d):
    with b.phase():
        b.set_pools({"p": [0, 1, 2], "s": [3, 4], "w": [5, 6, 7]})
        ocol, rocol = b.const("ocol")
        pA, rpA = b.const("permA")
        ident, rid = b.const("ident")
        wod, rwod = load_w_bf16(b, d["wod"], 8, 2304, "wod")
        tmp = ProjTmp(b)
        hg = b.ring(2, [128, 8, 512], BF16, "hT")
        tabs = b.ring(2, [128, 2, 512], F32, "tabs")
        var = b.ring(3, [128, 128], BF16, "va")
        items = []
        for (isc, sc0, c0, W) in groups_all():
            G = {}

            def s_load(G=G, isc=isc, sc0=sc0, c0=c0, W=W):
                hT, rhT = hg.next()
                src = d["HTc"] if isc else d["HTl"]
                b.dma("sp", hT[:, :, :W], src[:, sc0:sc0 + W].rearrange("(j p) t -> p j t", p=128), W=[rhT])
                tab, rtab = tabs.next()
                for i, nm in enumerate(("cosA", "sinA")):
                    b.dma("sp", tab[:, i, :W], d[nm][:, c0:c0 + W], W=[rtab])
                G["hT"] = (hT, rhT)
                G["tab"] = (tab, rtab)
            items.append([s_load])
            for cidx in range(5):
                if isc and cidx < 4:
                    continue
                c = {}

                def a0(c=c, G=G, cidx=cidx, W=W):
                    hT, rhT = G["hT"]
                    ps, rps = b.psum("p")
                    proj_fm(b, ps, rps, W, wod, rwod, 8, cidx * 128, hT, rhT)
                    c["ps"] = (ps, rps)

                def store(ob, rob, cidx=cidx, c0=c0, W=W):
                    if cidx < 4:
                        b.dma("sp", d["QA"][cidx * 128:(cidx + 1) * 128, c0:c0 + W], ob[:, :W], R=[rob])
                    else:
                        b.dma("sp", d["KA"][:, c0:c0 + W], ob[:, :W], R=[rob])

                def a1(c=c, G=G, W=W, store=store):
                    c["xps"] = c["ps"]
                    tab, rtab = G["tab"]
                    sw_, cb_ = rope_stages(b, tmp, c, W, pA, rpA, tab[:, 0, :], tab[:, 1, :], rtab, store)
                    c["comb"] = cb_
                    sw_()

                def a2(c=c):
                    c["comb"]()
                items.append([a0, a1, a2])
            for sub in range(W // 128):
                c = {}

                def v0(c=c, G=G, sub=sub):
                    hT, rhT = G["hT"]
                    ps, rps = b.psum("p")
                    for j in range(8):
                        b.mm(ps[:, 0:128], lhsT=hT[:, j, sub * 128:(sub + 1) * 128], rhs=wod[:, j, 640:768], start=(j == 0), stop=(j == 7),
                             R=[rhT, rwod], W=[rps])
                    c["ps"] = (ps, rps)

                def v1(c=c, sub=sub, c0=c0):
                    ps, rps = c["ps"]
                    va, rva = var.next()
                    b.cp("act", va[:], ps[:, 0:128], R=[rps], W=[rva])
                    b.dma("sp", d["VA"][c0 + sub * 128:c0 + (sub + 1) * 128, :], va[:], R=[rva])
                items.append([v0, v1])
        hh = b.ring(2, [128, 8, 512], BF16, "hTh")
        cvr = b.ring(8, [128, 512], F32, "cv")
        ur = b.ring(3, [128, 512], F32, "u")
        ubr = b.ring(3, [128, 512], BF16, "ub")
        utr = b.ring(2, [128, 4, 512], BF16, "utT")

        def conv(ps, rps, n, ch, dst, rdst):
            w0 = ocol[:, ch:ch + 1]
            w1 = ocol[:, 12 + ch:12 + ch + 1]
            w2 = ocol[:, 24 + ch:24 + ch + 1]
            bb = ocol[:, 36 + ch:36 + ch + 1]
            b.act(dst[:, :n], ps[:, 1:n + 1], AF.Identity, R=[rps, rocol], W=[rdst], bias=bb, scale=w1)
            b.stt("dve", dst[:, :n], ps[:, 0:n], w0, dst[:, :n], ALU.mult, ALU.add, R=[rps, rocol, rdst], W=[rdst])
            b.stt("dve", dst[:, :n], ps[:, 2:n + 2], w2, dst[:, :n], ALU.mult, ALU.add, R=[rps, rocol, rdst], W=[rdst])

        tiles = [(t0, min(HT_, T - t0)) for t0 in range(0, T, HT_)]
        for (t0, n) in tiles:
            G = {}
            nsub = (n + 127) // 128

            def h_load(G=G, t0=t0, n=n):
                hT, rhT = hh.next()
                b.dma("sp", hT[:, :, 0:n + 2], d["HTl"][:, t0:t0 + n + 2].rearrange("(j p) t -> p j t", p=128), W=[rhT])
                G["hT"] = (hT, rhT)
                G["ut"] = utr.next()
            items.append([h_load])
            for i in range(4):
                I = {}
                for part in range(3):
                    c = {}
                    ch = part * 4 + i

                    def p0(c=c, G=G, ch=ch, n=n):
                        hT, rhT = G["hT"]
                        ps, rps = b.psum("p")
                        proj_fm(b, ps, rps, n + 2, wod, rwod, 8, 768 + ch * 128, hT, rhT)
                        c["ps"] = (ps, rps)

                    def p1(c=c, I=I, ch=ch, part=part, n=n):
                        ps, rps = c["ps"]
                        cv, rcv = cvr.next()
                        conv(ps, rps, n, ch, cv, rcv)
                        I[part] = (cv, rcv)
                    st = [p0, p1]
                    if part == 2:
                        def p2(c=c, I=I, G=G, i=i, t0=t0, n=n, nsub=nsub):
                            (x0c, rx0), (x1c, rx1), (vc, rvc) = I[0], I[1], I[2]
                            b.dma("sp", d["X0"][i * 128:(i + 1) * 128, t0:t0 + n], x0c[:, :n], R=[rx0])
                            u, ru = ur.next()
                            b.tt("pool", u[:, :n], x1c[:, :n], vc[:, :n], ALU.mult, R=[rx1, rvc], W=[ru])
                            b.dma("sp", d["UF"][i * 128:(i + 1) * 128, t0:t0 + n], u[:, :n], R=[ru])
                            ub, rub = ubr.next()
                            b.cp("pool", ub[:, :n], u[:, :n], R=[ru], W=[rub])
                            ps, rps = b.psum("s")
                            for sub in range(nsub):
                                m = min(128, n - sub * 128)
                                b.mm(ps[:m, sub * 128:(sub + 1) * 128], lhsT=ub[:, sub * 128:sub * 128 + m], rhs=ident[:], start=True, stop=True,
                                     R=[rub, rid], W=[rps])
                            c["tp"] = (ps, rps)

                        def p3(c=c, G=G, i=i, t0=t0, n=n, nsub=nsub):
                            ps, rps = c["tp"]
                            utT, rut = G["ut"]
                            for sub in range(nsub):
                                m = min(128, n - sub * 128)
                                b.cp("act" if sub % 2 == 0 else "dve", utT[:m, sub, i * 128:(i + 1) * 128], ps[:m, sub * 128:(sub + 1) * 128],
                                     R=[rps], W=[rut])
                            if i == 3:
                                for sub in range(nsub):
                                    m = min(128, n - sub * 128)
                                    b.dma("sp", d["UT"][t0 + sub * 128:t0 + sub * 128 + m, :], utT[:m, sub, :], R=[rut])
                        st += [p2, p3]
                    items.append(st)
        run_pipeline(items)
        b.set_pools(None)


def phase_attn_odd(b, d):
    with b.phase():
        ocol, rocol = b.const("ocol")
        mP, rmP = b.const("maskP")
        mN, rmN = b.const("maskN")
        es = b.sb([128, 8], F32, "esink")
        res_ = Res()
        b.act(es[:], ocol[:, 52:60], AF.Exp, R=[rocol], W=[res_])
        ones512 = b.sb([128, 128], F32, "ones")
        b.memset("dve", ones512[:], 1.0, [res_])
        es4 = []
        for g_ in range(2):
            t_ = b.sb([128, 512], F32, "es4")
            for hh_ in range(4):
                b.ts("dve", t_[:, hh_ * 128:(hh_ + 1) * 128], ones512[:], es[:, 4 * g_ + hh_:4 * g_ + hh_ + 1], None, ALU.mult, None,
                     R=[res_], W=[res_])
            es4.append(t_)
        Kr = b.ring(2, [128, TA], BF16, "Kt")
        Vr = b.ring(2, [128, TA // 128, 128], BF16, "Vt")
        for vt, rv in Vr.items:
            b.memset("pool", vt[:, :, 64:128], 1.0, [rv])
        Qr = b.ring(2, [64, 4, 512], BF16, "Qt")
        Pr = b.ring(16, [128, 512], BF16, "Pt")
        recr = b.ring(3, [64, 512], F32, "rec")
        aor = b.ring(3, [64, 4, 512], BF16, "ao")
        items = []
        for g in range(2):
            Gk = {}
            for q4 in range(T // 512):
                Gq = {}
                for qi in range(4):
                    c = {}
                    qb = q4 * 4 + qi

                    def sA(c=c, Gk=Gk, Gq=Gq, g=g, q4=q4, qi=qi, qb=qb):
                        if q4 == 0 and qi == 0:
                            Kt, rK = Kr.next()
                            b.dma("sp", Kt[0:64, :], d["KA"][g * 64:(g + 1) * 64, :], W=[rK])
                            Vt, rV = Vr.next()
                            b.dma("sp", Vt[:, :, 0:64], d["VA"][:, g * 64:(g + 1) * 64].rearrange("(c p) d -> p c d", p=128), W=[rV])
                            Gk["K"] = (Kt, rK)
                            Gk["V"] = (Vt, rV)
                        if qi == 0:
                            Qt, rQ = Qr.next()
                            for hh_ in range(4):
                                h = 4 * g + hh_
                                b.dma("sp", Qt[:, hh_, :], d["QA"][h * 64:(h + 1) * 64, M + q4 * 512: M + (q4 + 1) * 512], W=[rQ])
                            Gq["Q"] = (Qt, rQ)
                            Gq["ao"] = aor.next()
                        Kt, rK = Gk["K"]
                        Qt, rQ = Gq["Q"]
                        chunks = [(0, None), (1, None)]
                        if qb >= 1:
                            chunks.append((2 + qb - 1, (mP, rmP)))
                        chunks.append((2 + qb, None))
                        if qb <= T // 128 - 2:
                            chunks.append((2 + qb + 1, (mN, rmN)))
                        pts = []
                        for (kc, mask) in chunks:
                            sps, rsps = b.psum()
                            for hh_ in range(4):
                                b.mm(sps[:, hh_ * 128:(hh_ + 1) * 128], lhsT=Kt[0:64, kc * 128:(kc + 1) * 128],
                                     rhs=Qt[:, hh_, qi * 128:(qi + 1) * 128], start=True, stop=True, R=[rK, rQ], W=[rsps])
                            Pt, rP = Pr.next()
                            b.act(Pt[:], sps[:], AF.Exp, R=[rsps], W=[rP], scale=0.125)
                            if mask is not None:
                                b.tt("pool", Pt[:], Pt[:], mask[0][:], ALU.mult, R=[rP, mask[1]], W=[rP])
                            pts.append((kc, Pt, rP))
                        c["pts"] = pts

                    def sB(c=c, Gk=Gk):
                        Vt, rV = Gk["V"]
                        pts = c["pts"]
                        ops_, rops = b.psum()
                        for hh_ in range(4):
                            for ci, (kc, Pt, rP) in enumerate(pts):
                                b.mm(ops_[:, hh_ * 128:(hh_ + 1) * 128], lhsT=Vt[:, kc, :], rhs=Pt[:, hh_ * 128:(hh_ + 1) * 128],
                                     start=(ci == 0), stop=(ci == len(pts) - 1), R=[rV, rP], W=[rops])
                        c["O"] = (ops_, rops)

                    def sC(c=c, Gq=Gq, g=g, q4=q4, qi=qi):
                        ops_, rops = c["O"]
                        ao, rao = Gq["ao"]
                        rec, rrec = recr.next()
                        b.tt("dve", rec[:], ops_[64:128, :], es4[g][64:128, :], ALU.add, R=[rops, res_], W=[rrec])
                        b.act(rec[:], rec[:], AF.Ln, R=[rrec], W=[rrec])
                        b.act(rec[:], rec[:], AF.Exp, R=[rrec], W=[rrec], scale=-1.0)
                        b.tt("dve", ao[:, :, qi * 128:(qi + 1) * 128], ops_[0:64, :].rearrange("p (h q) -> p h q", h=4),
                             rec[:].rearrange("p (h q) -> p h q", h=4), ALU.mult, R=[rops, rrec], W=[rao])
                        if qi == 3:
                            for hh_ in range(4):
                                h = 4 * g + hh_
                                b.dma("sp", d["OT"][h * 64:(h + 1) * 64, M + q4 * 512: M + (q4 + 1) * 512], ao[:, hh_, :], R=[rao])
                    items.append([sA, sB, sC])
        run_pipeline(items)


def phase_hyena_fwd(b, d):
    with b.phase():
        import os
        ocol, rocol = b.const("ocol")
        hw, rhw = b.const("hy_w")
        uT = b.sb([128, 32, 512], BF16, "uT")
        ruT = Res()
        for c4 in range(4):
            b.dma("sp", uT[:, c4 * 8:(c4 + 1) * 8, :], d["UT"][c4 * 1024:(c4 + 1) * 1024, :].rearrange("(c p) f -> p c f", p=128), W=[ruT])
        hsum = b.sb([128, 32, 512], BF16, "hsum")
        hdif = b.sb([128, 32, 512], BF16, "hdif")
        rhs_, rhd_ = Res(), Res()
        with contextlib.ExitStack() as st2:
            old = b.stack
            b.stack = st2
            zT = b.sb([5, T], F32, "zT")
            w1 = b.sb([5, 64], F32, "w1")
            w2 = b.sb([64, 64], F32, "w2")
            w3 = b.sb([64, 64], F32, "w3")
            w4 = b.sb([64, 1024], F32, "w4")
            rz, rw = Res(), Res()
            b.dma("sp", zT[:], d["hy_zT"], W=[rz])
            b.dma("sp", w1[:], d["hy_w1"], W=[rw])
            b.dma("sp", w2[:], d["hy_w2"], W=[rw])
            b.dma("sp", w3[:], d["hy_w3"], W=[rw])
            b.dma("sp", w4[:], d["hy_w4"], W=[rw])
            ar = b.ring(2, [64, 512], F32, "ha")
            kr_ = b.ring(2, [64, 512], F32, "hk")
            hr = b.ring(3, [64, 512], F32, "hh")
            decr = b.ring(2, [128, 1024], F32, "dec")
            hfr = b.ring(2, [128, 512], F32, "hf")
            hbr = b.ring(2, [128, 512], F32, "hb")

            def sin_layer(ps, rps, li):
                bcol = ocol[0:64, 60 + li:61 + li]
                fcol = ocol[0:64, 63 + li:64 + li]
                a, ra = ar.next()
                b.ts("dve", a[:], ps[0:64, :], bcol, fcol, ALU.add, ALU.mult, R=[rps, rocol], W=[ra])
                k, rk = kr_.next()
                b.ts("dve", k[:], a[:], I2P, MAGIC, ALU.mult, ALU.add, R=[ra], W=[rk])
                b.ts("dve", k[:], k[:], -MAGIC, None, ALU.add, None, R=[rk], W=[rk])
                b.stt("dve", a[:], a[:], I2P, k[:], ALU.mult, ALU.subtract, R=[ra, rk], W=[ra])
                h, rh = hr.next()
                b.act(h[:], a[:], AF.Sin, R=[ra], W=[rh], scale=6.283185)
                return h, rh

            for tt in range(8):
                ps, rps = b.psum()
                b.mm(ps[0:64, :], lhsT=w1[:], rhs=zT[:, tt * 512:(tt + 1) * 512], start=True, stop=True, R=[rw, rz], W=[rps])
                h, rh = sin_layer(ps, rps, 0)
                ps, rps = b.psum()
                b.mm(ps[0:64, :], lhsT=w2[:], rhs=h[:], start=True, stop=True, R=[rw, rh], W=[rps])
                h, rh = sin_layer(ps, rps, 1)
                ps, rps = b.psum()
                b.mm(ps[0:64, :], lhsT=w3[:], rhs=h[:], start=True, stop=True, R=[rw, rh], W=[rps])
                h3, rh3 = sin_layer(ps, rps, 2)
                for sub in range(4):
                    ck = tt * 4 + sub
                    dec, rdec = decr.next()
                    b.dma("sp", dec[:], d["hy_dec"][ck * 128:(ck + 1) * 128, :], W=[rdec])
                    psf, rpsf = b.psum()
                    b.mm(psf[:], lhsT=h3[:, sub * 128:(sub + 1) * 128], rhs=w4[:, 0:512], start=True, stop=True, R=[rh3, rw], W=[rpsf])
                    psb, rpsb = b.psum()
                    b.mm(psb[:], lhsT=h3[:, sub * 128:(sub + 1) * 128], rhs=w4[:, 512:1024], start=True, stop=True, R=[rh3, rw], W=[rpsb])
                    hf, rhf = hfr.next()
                    b.tt("dve", hf[:], psf[:], dec[:, 0:512], ALU.mult, R=[rpsf, rdec], W=[rhf])
                    hb, rhb = hbr.next()
                    b.tt("dve", hb[:], psb[:], dec[:, 512:1024], ALU.mult, R=[rpsb, rdec], W=[rhb])
                    b.tt("pool", hsum[:, ck, :], hf[:], hb[:], ALU.add, R=[rhf, rhb], W=[rhs_])
                    b.tt("pool", hdif[:, ck, :], hf[:], hb[:], ALU.subtract, R=[rhf, rhb], W=[rhd_])
            b.P.barrier()
            b.P.emit_block(b.sems)
            b.stack = old
        Cr = b.ring(2, [128, 33, 256], BF16, "Ct")
        Sr = b.ring(2, [128, 33, 256], BF16, "St")
        kcr = b.ring(2, [128, 512], F32, "kc")
        ksr = b.ring(2, [128, 512], F32, "ks")
        t1r = b.ring(2, [128, 512], F32, "t1")
        t2r = b.ring(2, [128, 512], F32, "t2")
        ycr = b.ring(2, [128, 512], BF16, "yc")
        ysr = b.ring(2, [128, 512], BF16, "ys")
        nfc = int(os.environ.get("HY_NF", "33"))
        Ct = St = rC = rS = None
        dtiles = {}

        def load_dft(jc):
            Ct, rC = Cr.next()
            b.dma("sp", Ct[:], d["dftC"][jc], W=[rC])
            St, rS = Sr.next()
            b.dma("sp", St[:], d["dftS"][jc], W=[rS])
            dtiles[jc] = (Ct, rC, St, rS)
        load_dft(0)
        for a in range(nfc):
            jc, off = a // 2, (a % 2) * 128
            if a % 2 == 0:
                Ct, rC, St, rS = dtiles.pop(jc)
                if 2 * (jc + 1) < nfc:
                    load_dft(jc + 1)
            pA, rpA_ = b.psum()
            pKc, rpKc = b.psum()
            pB, rpB_ = b.psum()
            pKs, rpKs = b.psum()
            for s in range(32):
                st_, sp_ = (s == 0), (s == 31)
                b.mm(pA[:], lhsT=Ct[:, s, off:off + 128], rhs=uT[:, s, :], start=st_, stop=sp_, R=[rC, ruT], W=[rpA_])
                b.mm(pKc[:], lhsT=Ct[:, s, off:off + 128], rhs=hsum[:, s, :], start=st_, stop=sp_, R=[rC, rhs_], W=[rpKc])
                b.mm(pB[:], lhsT=St[:, s, off:off + 128], rhs=uT[:, s, :], start=st_, stop=sp_, R=[rS, ruT], W=[rpB_])
                b.mm(pKs[:], lhsT=St[:, s, off:off + 128], rhs=hdif[:, s, :], start=st_, stop=sp_, R=[rS, rhd_], W=[rpKs])
            kc, rkc = kcr.next()
            b.act(kc[:], pKc[:], AF.Copy, R=[rpKc, rhw], W=[rkc], scale=hw[:, a:a + 1])
            ks, rks = ksr.next()
            b.act(ks[:], pKs[:], AF.Copy, R=[rpKs, rhw], W=[rks], scale=hw[:, a:a + 1])
            t1, rt1 = t1r.next()
            b.tt("dve", t1[:], pA[:], kc[:], ALU.mult, R=[rpA_, rkc], W=[rt1])
            t2, rt2 = t2r.next()
            b.tt("dve", t2[:], pB[:], ks[:], ALU.mult, R=[rpB_, rks], W=[rt2])
            yc, ryc = ycr.next()
            b.tt("pool", yc[:], t1[:], t2[:], ALU.subtract, R=[rt1, rt2], W=[ryc])
            b.dma("sp", d["YC"][a * 128:(a + 1) * 128, :], yc[:], R=[ryc])
            t1, rt1 = t1r.next()
            b.tt("dve", t1[:], pA[:], ks[:], ALU.mult, R=[rpA_, rks], W=[rt1])
            t2, rt2 = t2r.next()
            b.tt("dve", t2[:], pB[:], kc[:], ALU.mult, R=[rpB_, rkc], W=[rt2])
            ys, rys = ysr.next()
            b.tt("pool", ys[:], t1[:], t2[:], ALU.add, R=[rt1, rt2], W=[rys])
            b.dma("sp", d["YS"][a * 128:(a + 1) * 128, :], ys[:], R=[rys])


def phase_hyena_inv(b, d):
    with b.phase():
        import os
        ocol, rocol = b.const("ocol")
        Yc = b.sb([128, 33, 512], BF16, "Yc")
        Ys = b.sb([128, 33, 512], BF16, "Ys")
        rYc, rYs = Res(), Res()
        b.dma("sp", Yc[:], d["YC"].rearrange("(f p) c -> p f c", p=128), W=[rYc])
        b.dma("sp", Ys[:], d["YS"].rearrange("(f p) c -> p f c", p=128), W=[rYs])
        Cr = b.ring(2, [128, 33, 256], BF16, "Ct")
        Sr = b.ring(2, [128, 33, 256], BF16, "St")
        ur = b.ring(3, [128, 256], F32, "u")
        xr = b.ring(3, [128, 256], F32, "x0")
        tr = b.ring(2, [128, 256], F32, "t")
        ocr = b.ring(3, [128, 256], BF16, "oc")
        ntt = int(os.environ.get("HY_NT", "16"))
        dtiles = {}

        def load_dft(jc):
            Ct, rC = Cr.next()
            b.dma("sp", Ct[:], d["dftC"][jc], W=[rC])
            St, rS = Sr.next()
            b.dma("sp", St[:], d["dftS"][jc], W=[rS])
            dtiles[jc] = (Ct, rC, St, rS)
        load_dft(0)
        for jc in range(ntt):
            Ct, rC, St, rS = dtiles.pop(jc)
            if jc + 1 < ntt:
                load_dft(jc + 1)
            for i in range(4):
                u, ru = ur.next()
                b.dma("sp", u[:], d["UF"][i * 128:(i + 1) * 128, jc * 256:(jc + 1) * 256], W=[ru])
                x0, rx0 = xr.next()
                b.dma("sp", x0[:], d["X0"][i * 128:(i + 1) * 128, jc * 256:(jc + 1) * 256], W=[rx0])
                ps, rps = b.psum()
                for f in range(33):
                    b.mm(ps[:, 0:256], lhsT=Yc[:, f, i * 128:(i + 1) * 128], rhs=Ct[:, f, :], start=(f == 0), stop=False,
                         R=[rYc, rC], W=[rps])
                    b.mm(ps[:, 0:256], lhsT=Ys[:, f, i * 128:(i + 1) * 128], rhs=St[:, f, :], start=False, stop=(f == 32),
                         R=[rYs, rS], W=[rps])
                t, rt = tr.next()
                b.stt("dve", t[:], u[:], ocol[:, 48 + i:49 + i], ps[:, 0:256], ALU.mult, ALU.add, R=[ru, rocol, rps], W=[rt])
                oc, roc = ocr.next()
                b.tt("pool", oc[:], t[:], x0[:], ALU.mult, R=[rt, rx0], W=[roc])
                b.dma("sp", d["OT"][512 + i * 128:512 + (i + 1) * 128, M + jc * 256:M + (jc + 1) * 256], oc[:], R=[roc])


def phase_final(b, d):
    with b.phase():
        g, rg = load_bc(b, d["final_norm"][0, :], name="fn")
        xr = b.ring(7, [128, 1024], F32, "xt")
        jr = b.ring(2, [128, 1024], BF16, "junk")
        ssr = b.ring(6, [128, 4], F32, "ss")
        orr = b.ring(3, [128, 1024], F32, "o")
        items = []
        for i in range(T // 128):
            c = {}

            def s0(c=c, i=i):
                xt, rx = xr.next()
                b.dma("sp", xt[:], d["xres"][i * 128:(i + 1) * 128, :], W=[rx])
                c["x"] = (xt, rx)

            def s1(c=c):
                xt, rx = c["x"]
                junk, rj = jr.next()
                ss, rss = ssr.next()
                b.act(junk[:], xt[:], AF.Square, R=[rx], W=[rj, rss], accum=ss[:, 0:1])
                c["ss"] = (ss, rss)

            def s2(c=c, i=i):
                xt, rx = c["x"]
                ss, rss = c["ss"]
                rstd_col(b, ss, rss, 1024)
                o, ro = orr.next()
                b.stt("dve", o[:], xt[:], ss[:, 2:3], g[:], ALU.mult, ALU.mult, R=[rx, rss, rg], W=[ro])
                b.dma("sp", d["out"][i * 128:(i + 1) * 128, :], o[:], R=[ro])
            items.append([s0, NOOP, NOOP, s1, s2])
        run_pipeline(items)


_NC_CACHE = {}


def kernel(**inputs):
    inp = {k: np.asarray(v) for k, v in inputs.items()}
    if "nc" not in _NC_CACHE:
        _NC_CACHE["nc"] = build()
    nc = _NC_CACHE["nc"]
    _W_CACHE.clear()
    in_maps = [core_inputs(inp, bidx) for bidx in range(8)]
    res = run_bass_kernel_spmd(nc, in_maps, core_ids=list(range(8)))
    out = np.stack([np.asarray(r["out"], dtype=np.float32) for r in res.results], axis=0)
    return out
```

```python
import contextlib
import math
import numpy as np
import ml_dtypes
import concourse.bass as bass
import concourse.mybir as mybir
from concourse.bass_utils import run_bass_kernel_spmd

F32 = mybir.dt.float32
BF16 = mybir.dt.bfloat16
ALU = mybir.AluOpType
AF = mybir.ActivationFunctionType

T = 4096
M = 256
TA = T + M
D = 1024
FF = 2816
EPS = 1e-6
NFFT = 8192
MAGIC = 12582912.0
I2P = float(1.0 / (2 * math.pi))

COMPUTE = ("pe", "dve", "act", "pool")
QUEUES = ("pe", "dve", "act", "pool", "sp")
N_DMA_SEMS = {"sp": 12, "pool": 8}


class Res:
    __slots__ = ("name", "last_w", "readers", "excl")

    def __init__(self, name="", excl=False):
        self.name = name
        self.last_w = None
        self.readers = []
        self.excl = excl


class Op:
    __slots__ = ("q", "fn", "waits", "tok", "dma")

    def __init__(self, q, fn, dma):
        self.q = q
        self.fn = fn
        self.dma = dma
        self.waits = {}
        self.tok = None


class Prog:
    def __init__(self, nc):
        self.nc = nc
        self.ops = []
        self.cnt = {q: 0 for q in COMPUTE}
        self.dma_cnt = {}
        self.dma_rr = {q: 0 for q in N_DMA_SEMS}
        self.known = {q: {} for q in QUEUES}
        self.maxtok = {}

    def _deps(self, op, reads, writes):
        toks = []
        for r in reads:
            if r.last_w is not None:
                toks.append(r.last_w)
        for w in writes:
            if w.last_w is not None:
                toks.append(w.last_w)
            toks.extend(w.readers)
        kn = self.known[op.q]
        for (sk, v) in toks:
            if op.q == "pe" and sk == "pe":
                continue
            if kn.get(sk, 0) >= v:
                continue
            if op.waits.get(sk, 0) < v:
                op.waits[sk] = v
        for sk, v in op.waits.items():
            kn[sk] = max(kn.get(sk, 0), v)

    def _commit(self, op, reads, writes):
        for w in writes:
            w.last_w = op.tok
            w.readers = []
        for r in reads:
            if r not in writes:
                r.readers.append(op.tok)
                if len(r.readers) > 64:
                    best = {}
                    for (sk, v) in r.readers:
                        if best.get(sk, 0) < v:
                            best[sk] = v
                    r.readers = list(best.items())
        self.maxtok[op.tok[0]] = op.tok[1]
        self.ops.append(op)

    def op(self, q, fn, reads=(), writes=()):
        ex = [r for r in reads if r.excl and r not in writes]
        if ex:
            writes = list(writes) + ex
            reads = [r for r in reads if not r.excl]
        o = Op(q, fn, False)
        self._deps(o, reads, writes)
        self.cnt[q] += 1
        o.tok = (q, self.cnt[q])
        self._commit(o, reads, writes)
        return o

    def dma(self, q, fn, reads=(), writes=()):
        o = Op(q, fn, True)
        self._deps(o, reads, writes)
        k = self.dma_rr[q]
        self.dma_rr[q] = (k + 1) % N_DMA_SEMS[q]
        sk = ("dma", q, k)
        n = self.dma_cnt.get(sk, 0)
        if n > 0 and self.known[q].get(sk, 0) < 16 * n:
            o.waits[sk] = max(o.waits.get(sk, 0), 16 * n)
            self.known[q][sk] = 16 * n
        self.dma_cnt[sk] = n + 1
        o.tok = (sk, 16 * (n + 1))
        self._commit(o, reads, writes)
        return o

    def barrier(self):
        for q in QUEUES:
            o = Op(q, None, False)
            for sk, v in self.maxtok.items():
                if q == "pe" and sk == "pe":
                    continue
                if self.known[q].get(sk, 0) < v:
                    o.waits[sk] = v
                    self.known[q][sk] = v
            if o.waits:
                self.ops.append(o)

    def emit_block(self, sems):
        nc = self.nc
        ops = self.ops
        self.ops = []
        byq = {q: [o for o in ops if o.q == q] for q in QUEUES}
        with nc.Block() as block:
            def run(eng, q):
                for o in byq[q]:
                    for sk, v in o.waits.items():
                        eng.wait_ge(sems[sk], v)
                    if o.fn is None:
                        continue
                    ins = o.fn(eng)
                    ins.then_inc(sems[o.tok[0]], 16 if o.dma else 1)

            @block.tensor
            def _(e):
                run(e, "pe")

            @block.vector
            def _(e):
                run(e, "dve")

            @block.scalar
            def _(e):
                run(e, "act")

            @block.gpsimd
            def _(e):
                run(e, "pool")

            @block.sync
            def _(e):
                run(e, "sp")


class B:
    def __init__(self, nc, dbg, gstack):
        self.nc = nc
        self.P = Prog(nc)
        self.dbg = dbg
        self.gstack = gstack
        self.stack = gstack
        self.uid = 0
        self.consts = {}
        self.sems = {}
        for q in COMPUTE:
            self.sems[q] = gstack.enter_context(nc.semaphore("s_" + q))
        for q, n in N_DMA_SEMS.items():
            for k in range(n):
                self.sems[("dma", q, k)] = gstack.enter_context(nc.semaphore("d_%s%d" % (q, k)))
        self.psbig = gstack.enter_context(nc.psum_tensor("psbig", [128, 4096], F32))
        self.ps = [self.psbig[:, i * 512:(i + 1) * 512] for i in range(8)]
        self.psr = [Res("ps%d" % i, excl=True) for i in range(8)]
        self.psi = 0
        self.pools = None
        self.pool_i = {}

    def const(self, name):
        return self.consts[name]

    def load_consts(self, d, names_bf, names_f32):
        for n in names_bf:
            ap = d[n]
            t = self.gstack.enter_context(self.nc.sbuf_tensor("c_" + n, list(ap.shape), BF16))
            r = Res(n)
            self.dma("sp", t[:], ap, W=[r])
            self.consts[n] = (t, r)
        for n in names_f32:
            ap = d[n]
            t = self.gstack.enter_context(self.nc.sbuf_tensor("c_" + n, list(ap.shape), F32))
            r = Res(n)
            self.dma("sp", t[:], ap, W=[r])
            self.consts[n] = (t, r)
        t = self.gstack.enter_context(self.nc.sbuf_tensor("c_eps", [128, 2], F32))
        r = Res("eps")
        self.memset("dve", t[:, 0:1], EPS, [r])
        self.memset("dve", t[:, 1:2], 0.0, [r])
        self.eps_col = t
        self.eps_res = r
        self.P.barrier()
        self.P.emit_block(self.sems)

    def sb(self, shape, dtype, name=None):
        self.uid += 1
        t = self.stack.enter_context(self.nc.sbuf_tensor("%s_%d" % (name or "t", self.uid), list(shape), dtype))
        return t

    def ring(self, n, shape, dtype, name=None):
        return Ring([(self.sb(shape, dtype, name), Res(name)) for _ in range(n)])

    def psum(self, role=None):
        if role is None or self.pools is None:
            i = self.psi
            self.psi = (i + 1) % len(self.ps)
            return self.ps[i], self.psr[i]
        banks = self.pools[role]
        k = self.pool_i.get(role, 0)
        self.pool_i[role] = (k + 1) % len(banks)
        i = banks[k]
        return self.ps[i], self.psr[i]

    def set_pools(self, pools):
        self.pools = pools
        self.pool_i = {}

    @contextlib.contextmanager
    def phase(self):
        old = self.stack
        with contextlib.ExitStack() as st:
            self.stack = st
            yield
            self.P.barrier()
            self.P.emit_block(self.sems)
        self.stack = old

    def mm(self, out, lhsT, rhs, start, stop, R, W):
        self.P.op("pe", lambda e: e.matmul(out, lhsT=lhsT, rhs=rhs, start=start, stop=stop), R, W)

    def ts(self, q, out, in0, s1, s2, op0, op1, R, W):
        if op1 is None:
            self.P.op(q, lambda e: e.tensor_scalar(out=out, in0=in0, scalar1=s1, scalar2=None, op0=op0), R, W)
        else:
            self.P.op(q, lambda e: e.tensor_scalar(out=out, in0=in0, scalar1=s1, scalar2=s2, op0=op0, op1=op1), R, W)

    def tt(self, q, out, in0, in1, op, R, W):
        self.P.op(q, lambda e: e.tensor_tensor(out=out, in0=in0, in1=in1, op=op), R, W)

    def stt(self, q, out, in0, scalar, in1, op0, op1, R, W):
        self.P.op(q, lambda e: e.scalar_tensor_tensor(out=out, in0=in0, scalar=scalar, in1=in1, op0=op0, op1=op1), R, W)

    def act(self, out, in_, func, R, W, bias=None, scale=1.0, accum=None):
        def f(e):
            kw = {}
            if bias is not None:
                kw["bias"] = bias
            if accum is not None:
                kw["accum_out"] = accum
            return e.activation(out=out, in_=in_, func=func, scale=scale, **kw)
        self.P.op("act", f, R, W)

    def cp(self, q, out, in_, R, W):
        if q == "act":
            self.act(out, in_, AF.Copy, R, W)
        else:
            self.P.op(q, lambda e: e.tensor_copy(out=out, in_=in_), R, W)

    def recip(self, q, out, in_, R, W):
        self.P.op(q, lambda e: e.reciprocal(out=out, in_=in_), R, W)

    def memset(self, q, ap, val, W):
        self.P.op(q, lambda e: e.memset(ap, val), (), W)

    def dma(self, q, out, in_, R=(), W=(), slow=False):
        if slow:
            self.P.dma(q, lambda e: e.dma_start(out=out, in_=in_, allow_slow_non_contiguous=True), R, W)
        else:
            self.P.dma(q, lambda e: e.dma_start(out=out, in_=in_), R, W)


class Ring:
    def __init__(self, items):
        self.items = items
        self.i = 0

    def next(self):
        it = self.items[self.i]
        self.i = (self.i + 1) % len(self.items)
        return it


def _bf(a):
    return np.ascontiguousarray(np.asarray(a, np.float32)).astype(ml_dtypes.bfloat16)


def _rope_angles(rope_dim):
    rows = T // 64
    row_idx = np.repeat(np.arange(rows), 64).astype(np.float32)
    col_idx = np.tile(np.arange(64), rows).astype(np.float32)
    d_axis = rope_dim // 2
    inv_freq = (np.float32(10000.0) ** (-np.arange(0, d_axis, 2, dtype=np.float32) / np.float32(d_axis))).astype(np.float32)
    ang = np.concatenate([row_idx[:, None] * inv_freq, col_idx[:, None] * inv_freq], axis=-1).astype(np.float32)
    return ang


_CONST_CACHE = {}


def host_constants():
    if _CONST_CACHE:
        return _CONST_CACHE
    c = {}
    c["ident"] = _bf(np.eye(128))
    bo = np.zeros((128, 128), np.float32)
    bo[:64, :64] = 1
    bo[64:, 64:] = 1
    c["blockones"] = _bf(bo)
    c["allones"] = _bf(np.ones((128, 128)))
    angA = _rope_angles(64)
    cosA = np.ones((128, TA), np.float32)
    sinA = np.zeros((128, TA), np.float32)
    for p in range(128):
        cosA[p, M:] = np.cos(angA[:, p % 32])
        sinA[p, M:] = np.sin(angA[:, p % 32])
    c["cosA"], c["sinA"] = cosA, sinA
    pa = np.zeros((128, 128), np.float32)
    for hb in (0, 64):
        for i in range(32):
            pa[hb + 32 + i, hb + i] = -1.0
            pa[hb + i, hb + 32 + i] = 1.0
    c["permA"] = _bf(pa)
    angB = _rope_angles(32)
    cosB = np.ones((128, TA), np.float32)
    sinB = np.zeros((128, TA), np.float32)
    for p in range(64, 96):
        cosB[p, M:] = np.cos(angB[:, (p - 64) % 16])
        sinB[p, M:] = np.sin(angB[:, (p - 64) % 16])
    c["cosB"], c["sinB"] = cosB, sinB
    pb = np.zeros((128, 128), np.float32)
    for i in range(16):
        pb[80 + i, 64 + i] = -1.0
        pb[64 + i, 80 + i] = 1.0
    c["permB"] = _bf(pb)
    jj = np.arange(128)[:, None]
    rr = np.arange(128)[None, :]
    c["maskP"] = _bf(np.tile((jj >= rr).astype(np.float32), (1, 4)))
    c["maskN"] = _bf(np.tile((jj <= rr).astype(np.float32), (1, 4)))
    t = np.linspace(0.0, 1.0, T, dtype=np.float32)[:, None]
    w = (2 * np.float32(math.pi) * np.arange(T, dtype=np.float32)[:, None] / np.float32(T)).astype(np.float32)
    f = np.linspace(1e-4, 1.0, 2, dtype=np.float32)[None, :]
    z = np.concatenate([t, np.cos(f * w), -np.sin(f * w)], axis=-1).astype(np.float32)
    c["hy_zT"] = np.ascontiguousarray(z.T)
    cmin = math.log(1e-2) / 1.5
    cmax = math.log(1e-2) / 0.3
    deltas = np.abs(np.linspace(cmin, cmax, 512, dtype=np.float32))
    dec = np.exp(-t * deltas[None, :]).astype(np.float32)
    decb = dec.copy()
    decb[0, :] = 0.0
    c["hy_dec"] = np.ascontiguousarray(np.concatenate([dec, decb], axis=1))
    NE = 4224
    a = np.arange(4096, dtype=np.int64)
    prod = (a[:, None] * a[None, :]) % NFFT
    angd = prod.astype(np.float64) * (2 * math.pi / NFFT)
    C = np.zeros((NE, NE), np.float32)
    S = np.zeros((NE, NE), np.float32)
    C[:4096, :4096] = np.cos(angd)
    S[:4096, :4096] = np.sin(angd)
    alt = np.where(a % 2 == 0, 1.0, -1.0).astype(np.float32)
    C[:4096, 4096] = alt
    C[4096, :4096] = alt
    del angd, prod

    def tiles(Mx):
        Mb = _bf(Mx)
        out = np.zeros((17, 128, 33, 256), ml_dtypes.bfloat16)
        for jc in range(17):
            wd = min(256, NE - jc * 256)
            blk = Mb[:, jc * 256: jc * 256 + wd].reshape(33, 128, wd).transpose(1, 0, 2)
            out[jc, :, :, :wd] = blk
        return out
    c["dftC"] = tiles(C)
    c["dftS"] = tiles(S)
    del C, S
    wt = np.full((128, 33), 2.0 / NFFT, np.float32)
    wt[0, 0] = 1.0 / NFFT
    wt[:, 32] = 1.0 / NFFT
    c["hy_w"] = wt
    _CONST_CACHE.update(c)
    return c


PERM64 = np.concatenate([np.arange(0, 64, 2), np.arange(1, 64, 2)])
PERM32 = np.concatenate([np.arange(0, 32, 2), np.arange(1, 32, 2)])


def colvec(v, n):
    return np.ascontiguousarray(np.asarray(v, np.float32).reshape(n, 128).T)


def host_weights(inp):
    w = {}
    f32 = np.float32
    w["w_mod"] = np.ascontiguousarray(inp["w_mod"].reshape(2 * 1024, 6144))
    w["b_mod"] = np.ascontiguousarray(inp["b_mod"])
    w["norm_mix"] = np.ascontiguousarray(inp["norm_mix"])
    w["norm_ffn"] = np.ascontiguousarray(inp["norm_ffn"])
    w["final_norm"] = np.ascontiguousarray(inp["final_norm"].reshape(1, 1024))
    wi = inp["ev_w_in"][0]
    cols = []
    for h in range(8):
        cols.append(0 + h * 64 + PERM64)
    aq = np.concatenate(cols)
    ak = np.concatenate([896 + g * 64 + PERM64 for g in range(2)])
    av = 1024 + np.arange(128)
    bcq = 512 + np.arange(384)
    bckv = 1152 + np.arange(256)
    kr = 1408 + PERM32
    wev = np.zeros((1024, 1536), f32)
    wev[:, 0:512] = wi[:, aq]
    wev[:, 512:640] = wi[:, ak]
    wev[:, 640:768] = wi[:, av]
    wev[:, 768:1152] = wi[:, bcq]
    wev[:, 1152:1408] = wi[:, bckv]
    wev[:, 1408 + 64:1408 + 96] = wi[:, kr]
    w["wev"] = wev
    wuq = np.zeros((384, 1024), f32)
    uq = inp["b_w_uq"][0]
    for h in range(8):
        wuq[:, h * 128: h * 128 + 64] = uq[:, h * 96: h * 96 + 64]
        wuq[:, h * 128 + 64: h * 128 + 96] = uq[:, h * 96 + 64 + PERM32]
    w["wuq"] = wuq
    ukv = inp["b_w_ukv"][0]
    wukv = np.zeros((256, 1024), f32)
    for h in range(8):
        wukv[:, h * 64:(h + 1) * 64] = ukv[:, h * 128: h * 128 + 64]
        wukv[:, 512 + h * 64: 512 + (h + 1) * 64] = ukv[:, h * 128 + 64: h * 128 + 128]
    w["wukv"] = wukv
    w["ev_wout"] = np.ascontiguousarray(inp["ev_w_out"][0])
    ecol = np.zeros((128, 8), f32)
    ecol[:, 0] = np.tile(inp["a_q_norm"][0][PERM64], 2)
    ecol[:, 1] = np.tile(inp["a_k_norm"][0][PERM64], 2)
    ecol[:, 2:5] = colvec(inp["b_q_norm"][0], 3)
    ecol[:, 5:7] = colvec(inp["b_kv_norm"][0], 2)
    w["ecol"] = ecol
    wo = inp["od_w_in"][0]
    dq = np.concatenate([h * 64 + PERM64 for h in range(8)])
    dk = np.concatenate([2048 + g * 64 + PERM64 for g in range(2)])
    wod = np.zeros((1024, 2304), f32)
    wod[:, 0:512] = wo[:, dq]
    wod[:, 512:640] = wo[:, dk]
    wod[:, 640:768] = wo[:, 2176:2304]
    wod[:, 768:2304] = wo[:, 512:2048]
    w["wod"] = wod
    w["od_wout"] = np.ascontiguousarray(inp["od_w_out"][0])
    ocol = np.zeros((128, 80), f32)
    cw = inp["c_conv_w"][0]
    for k in range(3):
        ocol[:, k * 12:(k + 1) * 12] = colvec(cw[k], 12)
    ocol[:, 36:48] = colvec(inp["c_conv_b"][0], 12)
    ocol[:, 48:52] = colvec(inp["c_bias"][0], 4)
    ocol[:, 52:60] = np.broadcast_to(inp["d_sink"][0][None, :], (128, 8))
    ocol[:64, 60] = inp["c_filt_b1"][0]
    ocol[:64, 61] = inp["c_filt_b2"][0]
    ocol[:64, 62] = inp["c_filt_b3"][0]
    ocol[:64, 63:66] = inp["c_filt_freq"][0].T
    w["ocol"] = ocol
    w["hy_w1"] = np.ascontiguousarray(inp["c_filt_w1"][0])
    w["hy_w2"] = np.ascontiguousarray(inp["c_filt_w2"][0])
    w["hy_w3"] = np.ascontiguousarray(inp["c_filt_w3"][0])
    w["hy_w4"] = np.ascontiguousarray(inp["c_filt_w4"][0])
    w["ffn_up"] = np.ascontiguousarray(inp["ffn_w_up"].reshape(2 * 1024, 5632))
    w["ffn_down"] = np.ascontiguousarray(inp["ffn_w_down"].reshape(2 * 2816, 1024))
    fcol = np.zeros((2, 128, 176), f32)
    for l in range(2):
        for k in range(3):
            fcol[l, :, k * 44:(k + 1) * 44] = colvec(inp["ffn_conv_w"][l, k], 44)
        fcol[l, :, 132:176] = colvec(inp["ffn_conv_b"][l], 44)
    w["fcol"] = fcol.reshape(256, 176)
    return w


def load_bc(b, row_ap, q="sp", name="bc"):
    F = row_ap.shape[-1]
    t = b.sb([128, F], F32, name)
    r = Res(name)
    b.dma(q, t[:], row_ap.partition_broadcast(128), W=[r])
    return t, r


def load_w_bf16(b, dram_ap, nk, ncol, name, ngrp=1, order=None):
    t = b.sb([128, nk, ncol], BF16, name)
    if ngrp == 1:
        r = Res(name)
        for j in range(nk):
            b.dma("pool", t[:, j, :], dram_ap[j * 128:(j + 1) * 128, :], W=[r])
        return t, r
    gw = ncol // ngrp
    rs = [Res("%s%d" % (name, g)) for g in range(ngrp)]
    for g in (order or range(ngrp)):
        for j in range(nk):
            b.dma("pool", t[:, j, g * gw:(g + 1) * gw], dram_ap[j * 128:(j + 1) * 128, g * gw:(g + 1) * gw], W=[rs[g]])
    return t, rs


class NormCtx:
    def __init__(self, b):
        self.junk = b.ring(2, [128, 1024], BF16, "junk")
        self.ss = b.ring(6, [128, 4], F32, "ss")
        self.h32 = b.ring(3, [128, 1024], F32, "h32")
        self.hb = b.ring(3, [128, 1024], BF16, "hb")


def rstd_col(b, ss, rss, n):
    b.act(ss[:, 1:2], ss[:, 0:1], AF.Ln, R=[rss, b.eps_res], W=[rss], bias=b.eps_col[:, 0:1], scale=1.0 / n)
    b.act(ss[:, 2:3], ss[:, 1:2], AF.Exp, R=[rss], W=[rss], scale=-0.5)


def NOOP():
    pass


def run_pipeline(items):
    n = len(items)
    if n == 0:
        return
    S = max(len(it) for it in items)
    for step in range(n + S - 1):
        for s_ in reversed(range(S)):
            t = step - s_
            if 0 <= t < n and s_ < len(items[t]):
                items[t][s_]()


def norm_stages(b, nx, c, mult, rmult, sh, rsh, ident, rid):
    def s_sq():
        junk, rj = nx.junk.next()
        ss, rss = nx.ss.next()
        b.act(junk[:], c["x"], AF.Square, R=[c["rx"]], W=[rj, rss], accum=ss[:, 0:1])
        c["ss"] = (ss, rss)

    def s_scale():
        ss, rss = c["ss"]
        rstd_col(b, ss, rss, 1024)
        h32, rh = nx.h32.next()
        b.stt("dve", h32[:], c["x"], ss[:, 2:3], mult[:], ALU.mult, ALU.mult, R=[c["rx"], rss, rmult], W=[rh])
        hb, rhb = nx.hb.next()
        b.tt("pool", hb[:], h32[:], sh[:], ALU.add, R=[rh, rsh], W=[rhb])
        c["hb"] = (hb, rhb)

    def s_tr():
        hb, rhb = c["hb"]
        hTg, rhTg, col = c["hTg"], c["rhTg"], c["col"]
        for half in range(2):
            ps, rps = b.psum()
            for jj in range(4):
                j = half * 4 + jj
                b.mm(ps[:, jj * 128:(jj + 1) * 128], lhsT=hb[:, j * 128:(j + 1) * 128], rhs=ident[:], start=True, stop=True,
                     R=[rhb, rid], W=[rps])
            dst = hTg[:, half * 4:(half + 1) * 4, col:col + 128]
            src = ps[:, :].rearrange("p (j t) -> p j t", j=4)
            b.cp("act" if half == 0 else "dve", dst, src, R=[rps], W=[rhTg])
        if c.get("after") is not None:
            c["after"]()
    return [s_sq, s_scale, s_tr]


def mods_jobs(b, d, l, wring, bmr, orow):
    scl, rscl = b.const("scl")
    jobs = []
    for t in range(12):
        st = {}

        def load(t=t, st=st):
            wt, rw = wring.next()
            b.dma("sp", wt[:], d["w_mod"][l * 1024:(l + 1) * 1024, t * 512:(t + 1) * 512].rearrange("(j p) c -> p j c", p=128), W=[rw])
            bm, rbm = bmr.next()
            b.dma("sp", bm[:], d["b_mod"][l, t * 512:(t + 1) * 512].partition_broadcast(2), W=[rbm])
            st["w"] = (wt, rw, bm, rbm)

        def comp(t=t, st=st):
            wt, rw, bm, rbm = st["w"]
            ps, rps = b.psum("s")
            for j in range(8):
                b.mm(ps[0:2, :], lhsT=scl[:, 2 * j:2 * j + 2], rhs=wt[:, j, :], start=(j == 0), stop=(j == 7), R=[rscl, rw], W=[rps])
            o, ro = orow.next()
            b.tt("dve", o[:], ps[0:2, :], bm[:], ALU.add, R=[rps, rbm], W=[ro])
            b.dma("sp", d["mods"][l, :, t * 512:(t + 1) * 512], o[:], R=[ro])
        jobs.append((load, comp))
    return jobs


def phase_mods(b, d, layers=(0,)):
    scl = b.gstack.enter_context(b.nc.sbuf_tensor("c_scl", [128, 16], F32))
    rscl = Res("scl")
    b.consts["scl"] = (scl, rscl)
    with b.phase():
        cv = b.sb([128, 16], F32, "cv")
        rcv = Res()
        b.dma("sp", cv[:], d["cvec"], W=[rcv])
        b.act(scl[:], cv[:], AF.Silu, R=[rcv], W=[rscl])
        wring = b.ring(7, [128, 8, 512], F32, "wm")
        bmr = b.ring(7, [2, 512], F32, "bm")
        orow = b.ring(4, [2, 512], F32, "orow")
        jobs = []
        for l in layers:
            jobs += mods_jobs(b, d, l, wring, bmr, orow)
        AHEAD = 5
        for i in range(len(jobs) + AHEAD):
            if i < len(jobs):
                jobs[i][0]()
            if i - AHEAD >= 0:
                jobs[i - AHEAD][1]()


def load_mod(b, d, l, which, m, name):
    return load_bc(b, d["mods"][l, which, m * 1024:(m + 1) * 1024], name=name)


def load_mult(b, d, l, which, m, gain_row, name):
    sc, rsc = load_mod(b, d, l, which, m, name + "_sc")
    g, rg = load_bc(b, gain_row, name=name + "_g")
    b.stt("dve", sc[:], sc[:], 1.0, g[:], ALU.add, ALU.mult, R=[rsc, rg], W=[rsc])
    return sc, rsc


def phase_norm(b, d, l, src_lat, src_ctx):
    with b.phase():
        ident, rid = b.const("ident")
        nx = NormCtx(b)
        xr = b.ring(7, [128, 1024], F32, "xt")
        hg = b.ring(4, [128, 8, 512], BF16, "hTg")
        items = []
        for which, src, ntile, dst in ((1, src_ctx, M // 128, d["HTc"]), (0, src_lat, T // 128, d["HTl"])):
            mult, rmult = load_mult(b, d, l, which, 1, d["norm_mix"][l, :], "m1")
            sh, rsh = load_mod(b, d, l, which, 0, "sh1")
            for g0 in range(0, ntile, 4):
                ng = min(4, ntile - g0)
                grp = {}
                for i in range(ng):
                    c = {"col": i * 128}

                    def s_load(c=c, grp=grp, i=i, src=src, g0=g0):
                        if i == 0:
                            grp["h"] = hg.next()
                        c["hTg"], c["rhTg"] = grp["h"]
                        xt, rx = xr.next()
                        b.dma("sp", xt[:], src[(g0 + i) * 128:(g0 + i + 1) * 128, :], W=[rx])
                        c["x"], c["rx"] = xt[:], rx
                    if i == ng - 1:
                        def after(c=c, dst=dst, g0=g0, ng=ng):
                            b.dma("sp", dst[:, 1 + g0 * 128: 1 + (g0 + ng) * 128].rearrange("(j p) t -> p j t", p=128),
                                  c["hTg"][:, :, 0:ng * 128], R=[c["rhTg"]])
                        c["after"] = after
                    items.append([s_load, NOOP, NOOP] + norm_stages(b, nx, c, mult, rmult, sh, rsh, ident, rid))
        run_pipeline(items)


IN_SPECS = [
    ("x", [T, D], F32), ("ctx", [M, D], F32), ("cvec", [128, 16], F32),
    ("w_mod", [2048, 6144], F32), ("b_mod", [2, 6144], F32),
    ("norm_mix", [2, D], F32), ("norm_ffn", [2, D], F32), ("final_norm", [1, D], F32),
    ("wev", [1024, 1536], F32), ("wuq", [384, 1024], F32), ("wukv", [256, 1024], F32),
    ("ev_wout", [1024, 1024], F32), ("ecol", [128, 8], F32),
    ("wod", [1024, 2304], F32), ("od_wout", [1024, 1024], F32), ("ocol", [128, 80], F32),
    ("hy_w1", [5, 64], F32), ("hy_w2", [64, 64], F32), ("hy_w3", [64, 64], F32), ("hy_w4", [64, 1024], F32),
    ("ffn_up", [2048, 5632], F32), ("ffn_down", [5632, 1024], F32), ("fcol", [256, 176], F32),
    ("ident", [128, 128], BF16), ("blockones", [128, 128], BF16), ("allones", [128, 128], BF16),
    ("permA", [128, 128], BF16), ("permB", [128, 128], BF16), ("maskP", [128, 512], BF16), ("maskN", [128, 512], BF16),
    ("cosA", [128, TA], F32), ("sinA", [128, TA], F32), ("cosB", [128, TA], F32), ("sinB", [128, TA], F32),
    ("hy_zT", [5, T], F32), ("hy_dec", [T, 1024], F32), ("hy_w", [128, 33], F32),
    ("dftC", [17, 128, 33, 256], BF16), ("dftS", [17, 128, 33, 256], BF16),
]

SCRATCH = [
    ("mods", [2, 2, 6144], F32),
    ("HTl", [D, T + 2], BF16), ("HTc", [D, M + 2], BF16),
    ("xres", [T, D], F32), ("cres", [M, D], F32),
    ("QA", [512, TA], BF16), ("KA", [128, TA], BF16), ("VA", [TA, 128], BF16),
    ("QB", [8, 96, TA], BF16), ("KBn", [512, TA], BF16), ("KR", [32, TA], BF16), ("VB", [TA, 512], BF16),
    ("OT", [D, TA], BF16),
    ("X0", [512, T], F32), ("UF", [512, T], F32), ("UT", [T, 512], BF16),
    ("YC", [33 * 128, 512], BF16), ("YS", [33 * 128, 512], BF16),
]


def build(dbg=(), stop=None):
    nc = bass.Bass("TRN2", target_bir_lowering=False)
    d = {}
    for name, shape, dt in IN_SPECS:
        d[name] = nc.dram_tensor(name, list(shape), dt, kind="ExternalInput").ap()
    for name, shape, dt in SCRATCH:
        if name in dbg:
            d[name] = nc.dram_tensor(name, list(shape), dt, kind="ExternalOutput").ap()
        else:
            d[name] = nc.dram_tensor(name, list(shape), dt).ap()
    d["out"] = nc.dram_tensor("out", [T, D], F32, kind="ExternalOutput").ap()
    with contextlib.ExitStack() as gstack:
        b = B(nc, dbg, gstack)
        b.load_consts(d, ["ident", "blockones", "allones", "permA", "permB", "maskP", "maskN"], ["ecol", "ocol", "hy_w"])
        emit_program(b, d, stop)
    return nc


def emit_program(b, d, stop):
    with b.phase():
        z = b.sb([128, 8, 2], BF16, "z")
        rz = Res()
        b.memset("dve", z[:], 0.0, [rz])
        for t, n in ((d["HTl"], T), (d["HTc"], M)):
            for c in (0, n + 1):
                b.dma("sp", t[:, c:c + 1].rearrange("(j p) t -> p j t", p=128), z[:, :, 0:1], R=[rz], slow=True)
    phase_mods(b, d)
    if stop == "mods":
        return
    phase_norm(b, d, 0, d["x"], d["ctx"])
    if stop == "norm0":
        return
    phase_proj_even(b, d)
    if stop == "proj0":
        return
    phase_attn_even(b, d)
    if stop == "attn0":
        return
    phase_outproj(b, d, 0, d["ev_wout"], d["x"], d["ctx"], True)
    if stop == "outproj0":
        return
    phase_ffn(b, d, 0, True)
    if stop == "ffn0":
        return
    phase_norm(b, d, 1, d["xres"], d["cres"])
    phase_proj_odd(b, d)
    if stop == "proj1":
        return
    phase_attn_odd(b, d)
    if stop == "attn1":
        return
    phase_hyena_fwd(b, d)
    phase_hyena_inv(b, d)
    if stop == "hyena":
        return
    phase_outproj(b, d, 1, d["od_wout"], d["xres"], None, False)
    phase_ffn(b, d, 1, False, final=True)


_W_CACHE = {}


def core_inputs(inp, bidx):
    c = host_constants()
    if "w" not in _W_CACHE:
        _W_CACHE["w"] = host_weights(inp)
    w = _W_CACHE["w"]
    m = dict(c)
    m.update(w)
    m["x"] = np.ascontiguousarray(inp["x"][bidx])
    m["ctx"] = np.ascontiguousarray(inp["ctx"][bidx])
    cv = np.zeros((128, 16), np.float32)
    cv[:, 0::2] = colvec(inp["c"][bidx], 8)
    cv[:, 1::2] = colvec(inp["c_ctx"], 8)
    m["cvec"] = cv
    return m


class ProjTmp:
    def __init__(self, b):
        self.sq = b.ring(3, [128, 512], BF16, "sq")
        self.q32 = b.ring(4, [128, 512], F32, "q32")
        self.lnt = b.ring(2, [128, 512], F32, "lnt")
        self.rs = b.ring(3, [128, 512], F32, "rs")
        self.qn = b.ring(3, [128, 512], BF16, "qn")
        self.t1 = b.ring(3, [128, 512], BF16, "t1")
        self.t2 = b.ring(3, [128, 512], BF16, "t2")
        self.ob = b.ring(4, [128, 512], BF16, "ob")
        self.ss = b.ring(6, [128, 4], F32, "sstok")


def rs_from_ss(b, tmp, ssps, rssps, W, n):
    lnt, rl = tmp.lnt.next()
    b.act(lnt[:, :W], ssps[:, :W], AF.Ln, R=[rssps, b.eps_res], W=[rl], bias=b.eps_col[:, 0:1], scale=1.0 / n)
    rs, rrs = tmp.rs.next()
    b.act(rs[:, :W], lnt[:, :W], AF.Exp, R=[rl], W=[rrs], scale=-0.5)
    return rs, rrs


def proj_fm(b, ps, rps, W, wt, rw, nk, c0, hT, rhT):
    for j in range(nk):
        b.mm(ps[:, :W], lhsT=wt[:, j, c0:c0 + 128], rhs=hT[:, j, :W], start=(j == 0), stop=(j == nk - 1), R=[rw, rhT], W=[rps])


def rope_stages(b, tmp, c, W, perm, rperm, cos, sin, rtab, store):
    ident, rid = b.const("ident")

    def s_swap():
        t1, rt1 = tmp.t1.next()
        t2, rt2 = tmp.t2.next()
        if c.get("xps") is not None:
            ps, rps = c["xps"]
            b.tt("dve", t1[:, :W], ps[:, :W], cos[:, :W], ALU.mult, R=[rps, rtab], W=[rt1])
            b.tt("dve", t2[:, :W], ps[:, :W], sin[:, :W], ALU.mult, R=[rps, rtab], W=[rt2])
        else:
            qn, rqn = c["qn"]
            b.tt("dve", t1[:, :W], qn[:, :W], cos[:, :W], ALU.mult, R=[rqn, rtab], W=[rt1])
            b.tt("pool", t2[:, :W], qn[:, :W], sin[:, :W], ALU.mult, R=[rqn, rtab], W=[rt2])
        c["t12"] = (t1, rt1, t2, rt2)

    def s_comb():
        t1, rt1, t2, rt2 = c["t12"]
        sw, rsw = b.psum("w")
        b.mm(sw[:, :W], lhsT=ident[:], rhs=t1[:, :W], start=True, stop=False, R=[rid, rt1], W=[rsw])
        b.mm(sw[:, :W], lhsT=perm[:], rhs=t2[:, :W], start=False, stop=True, R=[rperm, rt2], W=[rsw])
        ob, rob = tmp.ob.next()
        b.cp("act", ob[:, :W], sw[:, :W], R=[rsw], W=[rob])
        store(ob, rob)
    return s_swap, s_comb


def groups_all():
    g = [(True, 1, 0, M)]
    for i in range(T // 512):
        g.append((False, 1 + 512 * i, M + 512 * i, 512))
    return g


def load_scaled_w(b, dram_ap, nk, ncol, colbase, name):
    ecol, recol = b.const("ecol")
    st = b.sb([128, nk, ncol], F32, name + "_st")
    rst = Res()
    t = b.sb([128, nk, ncol], BF16, name)
    r = Res(name)
    for j in range(nk):
        b.dma("sp", st[:, j, :], dram_ap[j * 128:(j + 1) * 128, :], W=[rst])
    for j in range(nk):
        b.ts("dve", t[:, j, :], st[:, j, :], ecol[:, colbase + j:colbase + j + 1], None, ALU.mult, None, R=[rst, recol], W=[r])
    return t, r


def phase_proj_even(b, d):
    with b.phase():
        b.set_pools({"p": [0, 1, 2], "s": [3, 4], "w": [5, 6, 7]})
        ecol, recol = b.const("ecol")
        bo, rbo = b.const("blockones")
        ao, rao = b.const("allones")
        pA, rpA = b.const("permA")
        pB, rpB = b.const("permB")
        wev, rwev = load_w_bf16(b, d["wev"], 8, 1536, "wev")
        wuq, rwuq = load_scaled_w(b, d["wuq"], 3, 1024, 2, "wuq")
        wukv, rwukv = load_scaled_w(b, d["wukv"], 2, 1024, 5, "wukv")
        tmp = ProjTmp(b)
        hg = b.ring(2, [128, 8, 512], BF16, "hT")
        tabs = b.ring(2, [128, 4, 512], F32, "tabs")
        cqr = b.ring(2, [128, 3, 512], BF16, "cq")
        rtabr = b.ring(2, [128, 2, 512], F32, "rtab")
        ckvr = b.ring(2, [128, 2, 512], BF16, "ckv")
        sqkvr = b.ring(2, [128, 2, 512], BF16, "sqkv")
        var = b.ring(3, [128, 128], BF16, "va")
        vbr = b.ring(3, [128, 512], BF16, "vb")
        items = []
        mwr = b.ring(2, [128, 8, 512], F32, "wm")
        mbr = b.ring(2, [2, 512], F32, "bm")
        mor = b.ring(2, [2, 512], F32, "orow")
        side = mods_jobs(b, d, 1, mwr, mbr, mor)
        for gi, (isc, sc0, c0, W) in enumerate(groups_all()):
            G = {}
            for _ in range(2 if gi < 3 else 1):
                if side:
                    ld_, cp_ = side.pop(0)
                    items.append([ld_] + [NOOP] * 7 + [cp_])

            def s_load(G=G, isc=isc, sc0=sc0, c0=c0, W=W):
                hT, rhT = hg.next()
                src = d["HTc"] if isc else d["HTl"]
                b.dma("sp", hT[:, :, :W], src[:, sc0:sc0 + W].rearrange("(j p) t -> p j t", p=128), W=[rhT])
                tab, rtab = tabs.next()
                for i, nm in enumerate(("cosA", "sinA", "cosB", "sinB")):
                    b.dma("sp", tab[:, i, :W], d[nm][:, c0:c0 + W], W=[rtab])
                G["hT"] = (hT, rhT)
                G["tab"] = (tab, rtab)
            items.append([s_load])

            for cidx in range(5):
                c = {}

                def a0(c=c, G=G, cidx=cidx, W=W):
                    ps, rps = b.psum("p")
                    hT, rhT = G["hT"]
                    proj_fm(b, ps, rps, W, wev, rwev, 8, cidx * 128, hT, rhT)
                    c["ps"] = (ps, rps)

                def a1(c=c, W=W):
                    ps, rps = c["ps"]
                    sq, rsq = tmp.sq.next()
                    b.act(sq[:, :W], ps[:, :W], AF.Square, R=[rps], W=[rsq])
                    q32, rq32 = tmp.q32.next()
                    b.cp("dve", q32[:, :W], ps[:, :W], R=[rps], W=[rq32])
                    c["q32"] = (q32, rq32)
                    ssps, rssps = b.psum("s")
                    b.mm(ssps[:, :W], lhsT=bo[:], rhs=sq[:, :W], start=True, stop=True, R=[rbo, rsq], W=[rssps])
                    c["ss"] = (ssps, rssps)

                def a2(c=c, W=W):
                    ssps, rssps = c["ss"]
                    c["rs"] = rs_from_ss(b, tmp, ssps, rssps, W, 64)

                def store(ob, rob, cidx=cidx, c0=c0, W=W):
                    if cidx < 4:
                        b.dma("sp", d["QA"][cidx * 128:(cidx + 1) * 128, c0:c0 + W], ob[:, :W], R=[rob])
                    else:
                        b.dma("sp", d["KA"][:, c0:c0 + W], ob[:, :W], R=[rob])

                def tabA(G=G):
                    tab, rtab = G["tab"]
                    return tab[:, 0, :], tab[:, 1, :], rtab

                def a3(c=c, W=W, cidx=cidx, G=G, store=store):
                    q32, rq32 = c["q32"]
                    rs, rrs = c["rs"]
                    gcol = ecol[:, 0:1] if cidx < 4 else ecol[:, 1:2]
                    qn, rqn = tmp.qn.next()
                    b.stt("dve", qn[:, :W], q32[:, :W], gcol, rs[:, :W], ALU.mult, ALU.mult, R=[rq32, rrs, recol], W=[rqn])
                    c["qn"] = (qn, rqn)
                    tab, rtab = G["tab"]
                    sw_, cb_ = rope_stages(b, tmp, c, W, pA, rpA, tab[:, 0, :], tab[:, 1, :], rtab, store)
                    c["comb"] = cb_
                    sw_()

                def a4(c=c):
                    c["comb"]()
                items.append([a0, a1, a2, a3, a4])

            for sub in range(W // 128):
                c = {}

                def v0(c=c, G=G, sub=sub):
                    hT, rhT = G["hT"]
                    ps, rps = b.psum("p")
                    for j in range(8):
                        b.mm(ps[:, 0:128], lhsT=hT[:, j, sub * 128:(sub + 1) * 128], rhs=wev[:, j, 640:768], start=(j == 0), stop=(j == 7),
                             R=[rhT, rwev], W=[rps])
                    c["ps"] = (ps, rps)

                def v1(c=c, sub=sub, c0=c0):
                    ps, rps = c["ps"]
                    va, rva = var.next()
                    b.cp("act", va[:], ps[:, 0:128], R=[rps], W=[rva])
                    b.dma("sp", d["VA"][c0 + sub * 128:c0 + (sub + 1) * 128, :], va[:], R=[rva])
                items.append([v0, v1])

            for j in range(3):
                c = {}

                def q0(c=c, G=G, j=j, W=W):
                    hT, rhT = G["hT"]
                    ps, rps = b.psum("p")
                    proj_fm(b, ps, rps, W, wev, rwev, 8, 768 + j * 128, hT, rhT)
                    c["ps"] = (ps, rps)

                def q1(c=c, G=G, j=j, W=W):
                    ps, rps = c["ps"]
                    if j == 0:
                        G["cq"] = cqr.next()
                        G["ssq"] = b.psum("s")
                    cq, rcq = G["cq"]
                    ssq, rssq = G["ssq"]
                    b.cp("dve", cq[:, j, :W], ps[:, :W], R=[rps], W=[rcq])
                    sq, rsq = tmp.sq.next()
                    b.act(sq[:, :W], ps[:, :W], AF.Square, R=[rps], W=[rsq])
                    b.mm(ssq[:, :W], lhsT=ao[:], rhs=sq[:, :W], start=(j == 0), stop=(j == 2), R=[rao, rsq], W=[rssq])

                def q2(G=G, j=j, W=W):
                    if j == 2:
                        ssq, rssq = G["ssq"]
                        rs, rrs = rs_from_ss(b, tmp, ssq, rssq, W, 384)
                        tab, rtab = G["tab"]
                        rt, rrt = rtabr.next()
                        b.tt("dve", rt[:, 0, :W], rs[:, :W], tab[:, 2, :W], ALU.mult, R=[rrs, rtab], W=[rrt])
                        b.tt("pool", rt[:, 1, :W], rs[:, :W], tab[:, 3, :W], ALU.mult, R=[rrs, rtab], W=[rrt])
                        G["rtab"] = (rt, rrt)
                items.append([q0, q1, q2])
            for h in range(8):
                c = {}

                def h0(c=c, G=G, h=h, W=W):
                    cq, rcq = G["cq"]
                    ps, rps = b.psum("p")
                    for j in range(3):
                        b.mm(ps[:, :W], lhsT=wuq[:, j, h * 128:(h + 1) * 128], rhs=cq[:, j, :W], start=(j == 0), stop=(j == 2),
                             R=[rwuq, rcq], W=[rps])
                    c["ps"] = (ps, rps)

                def storeq(ob, rob, h=h, c0=c0, W=W):
                    b.dma("sp", d["QB"][h, :, c0:c0 + W], ob[0:96, :W], R=[rob])

                def h1(c=c, G=G, W=W, storeq=storeq):
                    c["xps"] = c["ps"]
                    rt, rrt = G["rtab"]
                    sw_, cb_ = rope_stages(b, tmp, c, W, pB, rpB, rt[:, 0, :], rt[:, 1, :], rrt, storeq)
                    c["comb"] = cb_
                    sw_()

                def h2(c=c):
                    c["comb"]()
                items.append([h0, h1, h2])

            for j in range(2):
                c = {}

                def k0(c=c, G=G, j=j, W=W):
                    hT, rhT = G["hT"]
                    ps, rps = b.psum("p")
                    proj_fm(b, ps, rps, W, wev, rwev, 8, 1152 + j * 128, hT, rhT)
                    c["ps"] = (ps, rps)

                def k1(c=c, G=G, j=j, W=W):
                    ps, rps = c["ps"]
                    if j == 0:
                        G["ckv"] = ckvr.next()
                        G["sqkv"] = sqkvr.next()
                        G["sskv"] = b.psum("s")
                    ckv, rckv = G["ckv"]
                    sqkv, rsqkv = G["sqkv"]
                    sskv, rsskv = G["sskv"]
                    b.cp("dve", ckv[:, j, :W], ps[:, :W], R=[rps], W=[rckv])
                    b.act(sqkv[:, j, :W], ps[:, :W], AF.Square, R=[rps], W=[rsqkv])
                    b.mm(sskv[:, :W], lhsT=ao[:], rhs=sqkv[:, j, :W], start=(j == 0), stop=(j == 1), R=[rao, rsqkv], W=[rsskv])

                def k2(G=G, j=j, W=W):
                    if j == 1:
                        sskv, rsskv = G["sskv"]
                        G["rskv"] = rs_from_ss(b, tmp, sskv, rsskv, W, 256)
                items.append([k0, k1, k2])
            for cc in range(4):
                c = {}

                def n0(c=c, G=G, cc=cc, W=W):
                    ckv, rckv = G["ckv"]
                    ps, rps = b.psum("p")
                    for j in range(2):
                        b.mm(ps[:, :W], lhsT=wukv[:, j, cc * 128:(cc + 1) * 128], rhs=ckv[:, j, :W], start=(j == 0), stop=(j == 1),
                             R=[rwukv, rckv], W=[rps])
                    c["ps"] = (ps, rps)

                def n1(c=c, G=G, cc=cc, W=W, c0=c0):
                    ps, rps = c["ps"]
                    rskv, rrskv = G["rskv"]
                    ob, rob = tmp.ob.next()
                    b.tt("dve", ob[:, :W], ps[:, :W], rskv[:, :W], ALU.mult, R=[rps, rrskv], W=[rob])
                    b.dma("sp", d["KBn"][cc * 128:(cc + 1) * 128, c0:c0 + W], ob[:, :W], R=[rob])
                items.append([n0, n1])
            for sub in range(W // 128):
                c = {}

                def w0(c=c, G=G, sub=sub):
                    ckv, rckv = G["ckv"]
                    sqkv, rsqkv = G["sqkv"]
                    ps, rps = b.psum("p")
                    pss, rpss = b.psum("s")
                    for j in range(2):
                        b.mm(ps[:, :], lhsT=ckv[:, j, sub * 128:(sub + 1) * 128], rhs=wukv[:, j, 512:1024], start=(j == 0), stop=(j == 1),
                             R=[rckv, rwukv], W=[rps])
                    for j in range(2):
                        b.mm(pss[:, 0:1], lhsT=sqkv[:, j, sub * 128:(sub + 1) * 128], rhs=ao[:, 0:1], start=(j == 0), stop=(j == 1),
                             R=[rsqkv, rao], W=[rpss])
                    c["ps"] = (ps, rps)
                    c["pss"] = (pss, rpss)

                def w1(c=c):
                    pss, rpss = c["pss"]
                    ss, rss = tmp.ss.next()
                    b.cp("dve", ss[:, 0:1], pss[:, 0:1], R=[rpss], W=[rss])
                    rstd_col(b, ss, rss, 256)
                    c["ss"] = (ss, rss)

                def w2(c=c, sub=sub, c0=c0):
                    ps, rps = c["ps"]
                    ss, rss = c["ss"]
                    vb, rvb = vbr.next()
                    b.ts("dve", vb[:], ps[:, :], ss[:, 2:3], None, ALU.mult, None, R=[rps, rss], W=[rvb])
                    b.dma("sp", d["VB"][c0 + sub * 128:c0 + (sub + 1) * 128, :], vb[:], R=[rvb])
                items.append([w0, w1, w2])
            c = {}

            def r0(c=c, G=G, W=W):
                hT, rhT = G["hT"]
                ps, rps = b.psum("p")
                proj_fm(b, ps, rps, W, wev, rwev, 8, 1408, hT, rhT)
                c["ps"] = (ps, rps)

            def storekr(ob, rob, c0=c0, W=W):
                b.dma("sp", d["KR"][:, c0:c0 + W], ob[64:96, :W], R=[rob])

            def r1(c=c, G=G, W=W, storekr=storekr):
                c["xps"] = c["ps"]
                tab, rtab = G["tab"]
                sw_, cb_ = rope_stages(b, tmp, c, W, pB, rpB, tab[:, 2, :], tab[:, 3, :], rtab, storekr)
                c["comb"] = cb_
                sw_()

            def r2(c=c):
                c["comb"]()
            items.append([r0, r1, r2])
        for (ld_, cp_) in side:
            items.append([ld_, cp_])
        run_pipeline(items)
        b.set_pools(None)


def qtiles_all(with_ctx=True):
    q = []
    if with_ctx:
        q.append((0, M, [0, 1]))
    for i in range(T // 512):
        q.append((M + 512 * i, 512, list(range(TA // 128))))
    return q


def phase_attn_even(b, d):
    with b.phase():
        import os
        nhead_dbg = int(os.environ.get("ATT_NH", "99"))
        Kr = b.ring(2, [128, TA], BF16, "Kt")
        Vr = b.ring(2, [128, TA // 128, 128], BF16, "Vt")
        for vt, rv in Vr.items:
            b.memset("pool", vt[:, :, 64:128], 1.0, [rv])
        Qr = b.ring(3, [128, 512], BF16, "Qt")
        Pr = b.ring(3, [128, 3, 512], BF16, "Pt")
        recr = b.ring(2, [64, 512], F32, "rec")
        outr = b.ring(3, [64, 512], BF16, "ao")
        sbanks = [0, 1, 2, 3, 4, 5]
        si = [0]
        oi = [0]

        stageA = []
        stageB = []

        def run_group(kparts, vsrc, dq, heads, scale):
            grp = {}

            def load_kv():
                Kt, rK = Kr.next()
                for ap, pb in kparts:
                    n = ap.shape[0]
                    b.dma("sp", Kt[pb:pb + n, :], ap, W=[rK])
                Vt, rV = Vr.next()
                b.dma("sp", Vt[:, :, 0:64], vsrc.rearrange("(c p) d -> p c d", p=128), W=[rV])
                grp["K"] = (Kt, rK)
                grp["V"] = (Vt, rV)
            first_in_group = [True]
            NB = 3
            for (qap, orow) in heads:
                for (c0, W, kcs) in qtiles_all():
                    qt = {}
                    batches = [kcs[i:i + NB] for i in range(0, len(kcs), NB)]
                    for bi_, batch in enumerate(batches):
                        ch = {}

                        def A(bi_=bi_, batch=batch, qt=qt, ch=ch, c0=c0, W=W, qap=qap, lk=(first_in_group[0] and bi_ == 0)):
                            if lk:
                                load_kv()
                            if bi_ == 0:
                                Qt, rQ = Qr.next()
                                b.dma("sp", Qt[0:dq, :W], qap[:, c0:c0 + W], W=[rQ])
                                qt["Q"] = (Qt, rQ)
                                ob = 6 + oi[0]
                                oi[0] ^= 1
                                qt["O"] = (b.ps[ob], b.psr[ob])
                                qt["KV"] = (grp["K"], grp["V"])
                            Qt, rQ = qt["Q"]
                            (Kt, rK), _ = qt["KV"]
                            base = 3 * si[0]
                            si[0] ^= 1
                            nb = len(batch)
                            for j, kc in enumerate(batch):
                                b.mm(b.ps[base + j][:, :W], lhsT=Kt[0:dq, kc * 128:(kc + 1) * 128], rhs=Qt[0:dq, :W], start=True, stop=True,
                                     R=[rK, rQ], W=[b.psr[base + j]])
                            Pt, rP = Pr.next()
                            src = b.psbig[:, base * 512:(base + nb) * 512].rearrange("p (n w) -> p n w", w=512)[:, :, :W]
                            b.act(Pt[:, 0:nb, :W], src, AF.Exp, R=[b.psr[base + j] for j in range(nb)], W=[rP], scale=scale)
                            ch["P"] = (Pt, rP)

                        def Bf(bi_=bi_, batch=batch, qt=qt, ch=ch, c0=c0, W=W, orow=orow, nbt=len(batches)):
                            ops_, rops = qt["O"]
                            _, (Vt, rV) = qt["KV"]
                            Pt, rP = ch["P"]
                            for j, kc in enumerate(batch):
                                b.mm(ops_[:, :W], lhsT=Vt[:, kc, :], rhs=Pt[:, j, :W], start=(bi_ == 0 and j == 0),
                                     stop=(bi_ == nbt - 1 and j == len(batch) - 1), R=[rV, rP], W=[rops])
                            if bi_ == nbt - 1:
                                rec, rrec = recr.next()
                                b.recip("dve", rec[:, :W], ops_[64:128, :W], R=[rops], W=[rrec])
                                ao, rao = outr.next()
                                b.tt("dve", ao[:, :W], ops_[0:64, :W], rec[:, :W], ALU.mult, R=[rops, rrec], W=[rao])
                                b.dma("sp", d["OT"][orow:orow + 64, c0:c0 + W], ao[:, :W], R=[rao])
                        stageA.append(A)
                        stageB.append(Bf)
                        first_in_group[0] = False

        nh = 0
        for g in range(2):
            heads = [(d["QA"][h * 64:(h + 1) * 64, :], h * 64) for h in range(4 * g, 4 * g + 4)]
            heads = heads[:max(0, nhead_dbg - nh)]
            nh += len(heads)
            if heads:
                run_group([(d["KA"][g * 64:(g + 1) * 64, :], 0)], d["VA"][:, g * 64:(g + 1) * 64], 64, heads, 0.125)
        for h in range(8):
            if nh >= nhead_dbg:
                break
            nh += 1
            run_group([(d["KBn"][h * 64:(h + 1) * 64, :], 0), (d["KR"][:, :], 64)], d["VB"][:, h * 64:(h + 1) * 64], 96,
                      [(d["QB"][h], 512 + h * 64)], 96 ** -0.5)
        LOOK = 1
        n = len(stageA)
        for i in range(n + LOOK):
            if i < n:
                stageA[i]()
            if i - LOOK >= 0:
                stageB[i - LOOK]()


def phase_outproj(b, d, l, wout_ap, src_lat, src_ctx, do_ctx):
    with b.phase():
        ident, rid = b.const("ident")
        wout, rwout = load_w_bf16(b, wout_ap, 8, 1024, "wout")
        nx = NormCtx(b)
        xr = b.ring(7, [128, 1024], F32, "xt")
        og = b.ring(3, [128, 8, 512], BF16, "OTg")
        hg = b.ring(3, [128, 8, 512], BF16, "hTg")
        x1r = b.ring(4, [128, 1024], F32, "x1")
        jobs = []
        if do_ctx:
            jobs.append((1, src_ctx, d["cres"], M // 128, 0, d["HTc"]))
        jobs.append((0, src_lat, d["xres"], T // 128, M, d["HTl"]))
        items = []
        for which, src, dst, ntile, otc0, HT in jobs:
            gt, rgt = load_mod(b, d, l, which, 2, "gt")
            mult2, rmult2 = load_mult(b, d, l, which, 4, d["norm_ffn"][l, :], "m2")
            sh2, rsh2 = load_mod(b, d, l, which, 3, "sh2")
            for g0 in range(0, ntile, 4):
                ng = min(4, ntile - g0)
                grp = {}
                for i in range(ng):
                    c = {"col": i * 128}

                    def s_ld(c=c, grp=grp, i=i, src=src, g0=g0, ng=ng, otc0=otc0):
                        if i == 0:
                            OTg, rOT = og.next()
                            b.dma("sp", OTg[:, :, 0:ng * 128],
                                  d["OT"][:, otc0 + g0 * 128: otc0 + (g0 + ng) * 128].rearrange("(j p) t -> p j t", p=128), W=[rOT])
                            grp["OT"] = (OTg, rOT)
                        r0 = (g0 + i) * 128
                        xt, rx = xr.next()
                        b.dma("sp", xt[:], src[r0:r0 + 128, :], W=[rx])
                        c["xt"] = (xt, rx)

                    def s_mm(c=c, grp=grp, i=i):
                        if i == 0:
                            grp["h"] = hg.next()
                        OTg, rOT = grp["OT"]
                        c["hTg"], c["rhTg"] = grp["h"]
                        c["ps"] = []
                        for half in range(2):
                            ps, rps = b.psum()
                            for j in range(8):
                                b.mm(ps[:, :], lhsT=OTg[:, j, i * 128:(i + 1) * 128], rhs=wout[:, j, half * 512:(half + 1) * 512],
                                     start=(j == 0), stop=(j == 7), R=[rOT, rwout], W=[rps])
                            c["ps"].append((ps, rps))

                    def s_res(c=c, i=i, g0=g0, dst=dst, gt=gt, rgt=rgt):
                        xt, rx = c["xt"]
                        x1, rx1 = x1r.next()
                        for half in range(2):
                            ps, rps = c["ps"][half]
                            b.tt("dve", x1[:, half * 512:(half + 1) * 512], ps[:, :], gt[:, half * 512:(half + 1) * 512], ALU.mult,
                                 R=[rps, rgt], W=[rx1])
                        b.tt("pool", x1[:], x1[:], xt[:], ALU.add, R=[rx1, rx], W=[rx1])
                        r0 = (g0 + i) * 128
                        b.dma("sp", dst[r0:r0 + 128, :], x1[:], R=[rx1])
                        c["x"], c["rx"] = x1[:], rx1
                    if i == ng - 1:
                        def after(c=c, HT=HT, g0=g0, ng=ng):
                            b.dma("sp", HT[:, 1 + g0 * 128: 1 + (g0 + ng) * 128].rearrange("(j p) t -> p j t", p=128),
                                  c["hTg"][:, :, 0:ng * 128], R=[c["rhTg"]])
                        c["after"] = after
                    st = norm_stages(b, nx, c, mult2, rmult2, sh2, rsh2, ident, rid)

                    def s_res_sq(s_res=s_res, sq=st[0]):
                        s_res()
                        sq()
                    items.append([s_ld, NOOP, NOOP, s_mm, s_res_sq, st[1], st[2]])
        run_pipeline(items)


FT = 384


def phase_ffn(b, d, l, do_ctx, final=False):
    with b.phase():
        if final:
            fng, rfng = load_bc(b, d["final_norm"][0, :], name="fn")
            fss = b.ring(4, [128, 4], F32, "fss")
        import os
        ntl = int(os.environ.get("FFN_NT", "99"))
        wup, rwup_g = load_w_bf16(b, d["ffn_up"][l * 1024:(l + 1) * 1024, :], 8, 5632, "wup", ngrp=4, order=[0, 2, 1, 3])
        wdn, rwdn = load_w_bf16(b, d["ffn_down"][l * 2816:(l + 1) * 2816, :], 22, 1024, "wdn")
        fc = b.sb([128, 176], F32, "fcol")
        rfc = Res()
        b.dma("sp", fc[:], d["fcol"][l * 128:(l + 1) * 128, :], W=[rfc])
        hTr = b.ring(2, [128, 8, FT + 2], BF16, "h2T")
        hid = b.sb([128, 22, FT], BF16, "hid")
        rhid = [Res("hid%d" % j) for j in range(22)]
        gar = b.ring(2, [128, FT], F32, "ga")
        var_ = b.ring(2, [128, FT], F32, "va")
        sgr = b.ring(2, [128, FT], F32, "sg")
        xr = b.ring(2, [128, 1024], F32, "xt")
        yr = b.ring(2, [128, 1024], F32, "yt")
        jobs = []
        if do_ctx:
            jobs.append((1, d["cres"], d["HTc"], [(0, M)]))
        lat_tiles = [(t0, min(FT, T - t0)) for t0 in range(0, T, FT)][:ntl]
        jobs.append((0, d["xres"], d["HTl"], lat_tiles))

        def conv(ps, rps, n, ch, dst, rdst, q2):
            w0 = fc[:, ch:ch + 1]
            w1 = fc[:, 44 + ch:44 + ch + 1]
            w2 = fc[:, 88 + ch:88 + ch + 1]
            bb = fc[:, 132 + ch:132 + ch + 1]
            b.act(dst[:, :n], ps[:, 1:n + 1], AF.Identity, R=[rps, rfc], W=[rdst], bias=bb, scale=w1)
            b.stt("dve", dst[:, :n], ps[:, 0:n], w0, dst[:, :n], ALU.mult, ALU.add, R=[rps, rfc, rdst], W=[rdst])
            b.stt("dve", dst[:, :n], ps[:, 2:n + 2], w2, dst[:, :n], ALU.mult, ALU.add, R=[rps, rfc, rdst], W=[rdst])

        for which, xdst, HT, tiles in jobs:
            gt2, rgt2 = load_mod(b, d, l, which, 5, "gt2")
            pre = {}

            def load_hT(ti, HT=HT, tiles=tiles, pre=pre):
                t0_, n_ = tiles[ti]
                hT_, rhT_ = hTr.next()
                b.dma("sp", hT_[:, :, 0:n_ + 2], HT[:, t0_:t0_ + n_ + 2].rearrange("(j p) t -> p j t", p=128), W=[rhT_])
                pre[ti] = (hT_, rhT_)
            load_hT(0)
            for ti, (t0, n) in enumerate(tiles):
                hT, rhT = pre.pop(ti)
                for j in range(22):
                    psg, rpsg = b.psum()
                    for k in range(8):
                        b.mm(psg[:, :n + 2], lhsT=wup[:, k, j * 128:(j + 1) * 128], rhs=hT[:, k, :n + 2], start=(k == 0), stop=(k == 7),
                             R=[rwup_g[j // 11], rhT], W=[rpsg])
                    psv, rpsv = b.psum()
                    for k in range(8):
                        b.mm(psv[:, :n + 2], lhsT=wup[:, k, (22 + j) * 128:(23 + j) * 128], rhs=hT[:, k, :n + 2], start=(k == 0), stop=(k == 7),
                             R=[rwup_g[(22 + j) // 11], rhT], W=[rpsv])
                    ga, rga = gar.next()
                    conv(psg, rpsg, n, j, ga, rga, "dve")
                    va, rva = var_.next()
                    conv(psv, rpsv, n, 22 + j, va, rva, "dve")
                    sg, rsg = sgr.next()
                    b.act(sg[:, :n], ga[:, :n], AF.Silu, R=[rga], W=[rsg])
                    b.tt("pool", hid[:, j, :n], sg[:, :n], va[:, :n], ALU.mult, R=[rsg, rva], W=[rhid[j]])
                if ti + 1 < len(tiles):
                    load_hT(ti + 1)
                for sub in range((n + 127) // 128):
                    m = min(128, n - sub * 128)
                    r0 = t0 + sub * 128
                    xt, rx = xr.next()
                    b.dma("sp", xt[:m, :], xdst[r0:r0 + m, :], W=[rx])
                    yt, ry = yr.next()
                    for half in range(2):
                        ps, rps = b.psum()
                        for j in range(22):
                            b.mm(ps[:m, :], lhsT=hid[:, j, sub * 128:sub * 128 + m], rhs=wdn[:, j, half * 512:(half + 1) * 512],
                                 start=(j == 0), stop=(j == 21), R=[rhid[j], rwdn], W=[rps])
                        b.tt("dve", yt[:m, half * 512:(half + 1) * 512], ps[:m, :], gt2[:m, half * 512:(half + 1) * 512], ALU.mult,
                             R=[rps, rgt2], W=[ry])
                    b.tt("pool", yt[:m, :], yt[:m, :], xt[:m, :], ALU.add, R=[ry, rx], W=[ry])
                    if final and which == 0:
                        ss, rss = fss.next()
                        b.act(xt[:m, :], yt[:m, :], AF.Square, R=[ry], W=[rx, rss], accum=ss[:m, 0:1])
                        rstd_col(b, ss, rss, 1024)
                        b.stt("dve", yt[:m, :], yt[:m, :], ss[:m, 2:3], fng[:m, :], ALU.mult, ALU.mult, R=[ry, rss, rfng], W=[ry])
                        b.dma("sp", d["out"][r0:r0 + m, :], yt[:m, :], R=[ry])
                    else:
                        b.dma("sp", xdst[r0:r0 + m, :], yt[:m, :], R=[ry])


HT_ = 510


def phase_proj_odd(b, d):
    with b.phase():
        b.set_pools({"p": [0, 1, 2], "s": [3, 4], "w": [5, 6, 7]})
        ocol, rocol = b.const("ocol")
        pA, rpA = b.const("permA")
        ident, rid = b.const("ident")
        wod, rwod = load_w_bf16(b, d["wod"], 8, 2304, "wod")
        tmp = ProjTmp(b)
        hg = b.ring(2, [128, 8, 512], BF16, "hT")
        tabs = b.ring(2, [128, 2, 512], F32, "tabs")
        var = b.ring(3, [128, 128], BF16, "va")
        items = []
        for (isc, sc0, c0, W) in groups_all():
            G = {}

            def s_load(G=G, isc=isc, sc0=sc0, c0=c0, W=W):
                hT, rhT = hg.next()
                src = d["HTc"] if isc else d["HTl"]
                b.dma("sp", hT[:, :, :W], src[:, sc0:sc0 + W].rearrange("(j p) t -> p j t", p=128), W=[rhT])
                tab, rtab = tabs.next()
                for i, nm in enumerate(("cosA", "sinA")):
                    b.dma("sp", tab[:, i, :W], d[nm][:, c0:c0 + W], W=[rtab])
                G["hT"] = (hT, rhT)
                G["tab"] = (tab, rtab)
            items.append([s_load])
            for cidx in range(5):
                if isc and cidx < 4:
                    continue
                c = {}

                def a0(c=c, G=G, cidx=cidx, W=W):
                    hT, rhT = G["hT"]
                    ps, rps = b.psum("p")
                    proj_fm(b, ps, rps, W, wod, rwod, 8, cidx * 128, hT, rhT)
                    c["ps"] = (ps, rps)

                def store(ob, rob, cidx=cidx, c0=c0, W=W):
                    if cidx < 4:
                        b.dma("sp", d["QA"][cidx * 128:(cidx + 1) * 128, c0:c0 + W], ob[:, :W], R=[rob])
                    else:
                        b.dma("sp", d["KA"][:, c0:c0 + W], ob[:, :W], R=[rob])

                def a1(c=c, G=G, W=W, store=store):
                    c["xps"] = c["ps"]
                    tab, rtab = G["tab"]
                    sw_, cb_ = rope_stages(b, tmp, c, W, pA, rpA, tab[:, 0, :], tab[:, 1, :], rtab, store)
                    c["comb"] = cb_
                    sw_()

                def a2(c=c):
                    c["comb"]()
                items.append([a0, a1, a2])
            for sub in range(W // 128):
                c = {}

                def v0(c=c, G=G, sub=sub):
                    hT, rhT = G["hT"]
                    ps, rps = b.psum("p")
                    for j in range(8):
                        b.mm(ps[:, 0:128], lhsT=hT[:, j, sub * 128:(sub + 1) * 128], rhs=wod[:, j, 640:768], start=(j == 0), stop=(j == 7),
                             R=[rhT, rwod], W=[rps])
                    c["ps"] = (ps, rps)

                def v1(c=c, sub=sub, c0=c0):
                    ps, rps = c["ps"]
                    va, rva = var.next()
                    b.cp("act", va[:], ps[:, 0:128], R=[rps], W=[rva])
                    b.dma("sp", d["VA"][c0 + sub * 128:c0 + (sub + 1) * 128, :], va[:], R=[rva])
                items.append([v0, v1])
        hh = b.ring(2, [128, 8, 512], BF16, "hTh")
        cvr = b.ring(8, [128, 512], F32, "cv")
        ur = b.ring(3, [128, 512], F32, "u")
        ubr = b.ring(3, [128, 512], BF16, "ub")
        utr = b.ring(2, [128, 4, 512], BF16, "utT")

        def conv(ps, rps, n, ch, dst, rdst):
            w0 = ocol[:, ch:ch + 1]
            w1 = ocol[:, 12 + ch:12 + ch + 1]
            w2 = ocol[:, 24 + ch:24 + ch + 1]
            bb = ocol[:, 36 + ch:36 + ch + 1]
            b.act(dst[:, :n], ps[:, 1:n + 1], AF.Identity, R=[rps, rocol], W=[rdst], bias=bb, scale=w1)
            b.stt("dve", dst[:, :n], ps[:, 0:n], w0, dst[:, :n], ALU.mult, ALU.add, R=[rps, rocol, rdst], W=[rdst])
            b.stt("dve", dst[:, :n], ps[:, 2:n + 2], w2, dst[:, :n], ALU.mult, ALU.add, R=[rps, rocol, rdst], W=[rdst])

        tiles = [(t0, min(HT_, T - t0)) for t0 in range(0, T, HT_)]
        for (t0, n) in tiles:
            G = {}
            nsub = (n + 127) // 128

            def h_load(G=G, t0=t0, n=n):
                hT, rhT = hh.next()
                b.dma("sp", hT[:, :, 0:n + 2], d["HTl"][:, t0:t0 + n + 2].rearrange("(j p) t -> p j t", p=128), W=[rhT])
                G["hT"] = (hT, rhT)
                G["ut"] = utr.next()
            items.append([h_load])
            for i in range(4):
                I = {}
                for part in range(3):
                    c = {}
                    ch = part * 4 + i

                    def p0(c=c, G=G, ch=ch, n=n):
                        hT, rhT = G["hT"]
                        ps, rps = b.psum("p")
                        proj_fm(b, ps, rps, n + 2, wod, rwod, 8, 768 + ch * 128, hT, rhT)
                        c["ps"] = (ps, rps)

                    def p1(c=c, I=I, ch=ch, part=part, n=n):
                        ps, rps = c["ps"]
                        cv, rcv = cvr.next()
                        conv(ps, rps, n, ch, cv, rcv)
                        I[part] = (cv, rcv)
                    st = [p0, p1]
                    if part == 2:
                        def p2(c=c, I=I, G=G, i=i, t0=t0, n=n, nsub=nsub):
                            (x0c, rx0), (x1c, rx1), (vc, rvc) = I[0], I[1], I[2]
                            b.dma("sp", d["X0"][i * 128:(i + 1) * 128, t0:t0 + n], x0c[:, :n], R=[rx0])
                            u, ru = ur.next()
                            b.tt("pool", u[:, :n], x1c[:, :n], vc[:, :n], ALU.mult, R=[rx1, rvc], W=[ru])
                            b.dma("sp", d["UF"][i * 128:(i + 1) * 128, t0:t0 + n], u[:, :n], R=[ru])
                            ub, rub = ubr.next()
                            b.cp("pool", ub[:, :n], u[:, :n], R=[ru], W=[rub])
                            ps, rps = b.psum("s")
                            for sub in range(nsub):
                                m = min(128, n - sub * 128)
                                b.mm(ps[:m, sub * 128:(sub + 1) * 128], lhsT=ub[:, sub * 128:sub * 128 + m], rhs=ident[:], start=True, stop=True,
                                     R=[rub, rid], W=[rps])
                            c["tp"] = (ps, rps)

                        def p3(c=c, G=G, i=i, t0=t0, n=n, nsub=nsub):
                            ps, rps = c["tp"]
                            utT, rut = G["ut"]
                            for sub in range(nsub):
                                m = min(128, n - sub * 128)
                                b.cp("act" if sub % 2 == 0 else "dve", utT[:m, sub, i * 128:(i + 1) * 128], ps[:m, sub * 128:(sub + 1) * 128],
                                     R=[rps], W=[rut])
                            if i == 3:
                                for sub in range(nsub):
                                    m = min(128, n - sub * 128)
                                    b.dma("sp", d["UT"][t0 + sub * 128:t0 + sub * 128 + m, :], utT[:m, sub, :], R=[rut])
                        st += [p2, p3]
                    items.append(st)
        run_pipeline(items)
        b.set_pools(None)


def phase_attn_odd(b, d):
    with b.phase():
        ocol, rocol = b.const("ocol")
        mP, rmP = b.const("maskP")
        mN, rmN = b.const("maskN")
        es = b.sb([128, 8], F32, "esink")
        res_ = Res()
        b.act(es[:], ocol[:, 52:60], AF.Exp, R=[rocol], W=[res_])
        ones512 = b.sb([128, 128], F32, "ones")
        b.memset("dve", ones512[:], 1.0, [res_])
        es4 = []
        for g_ in range(2):
            t_ = b.sb([128, 512], F32, "es4")
            for hh_ in range(4):
                b.ts("dve", t_[:, hh_ * 128:(hh_ + 1) * 128], ones512[:], es[:, 4 * g_ + hh_:4 * g_ + hh_ + 1], None, ALU.mult, None,
                     R=[res_], W=[res_])
            es4.append(t_)
        Kr = b.ring(2, [128, TA], BF16, "Kt")
        Vr = b.ring(2, [128, TA // 128, 128], BF16, "Vt")
        for vt, rv in Vr.items:
            b.memset("pool", vt[:, :, 64:128], 1.0, [rv])
        Qr = b.ring(2, [64, 4, 512], BF16, "Qt")
        Pr = b.ring(16, [128, 512], BF16, "Pt")
        recr = b.ring(3, [64, 512], F32, "rec")
        aor = b.ring(3, [64, 4, 512], BF16, "ao")
        items = []
        for g in range(2):
            Gk = {}
            for q4 in range(T // 512):
                Gq = {}
                for qi in range(4):
                    c = {}
                    qb = q4 * 4 + qi

                    def sA(c=c, Gk=Gk, Gq=Gq, g=g, q4=q4, qi=qi, qb=qb):
                        if q4 == 0 and qi == 0:
                            Kt, rK = Kr.next()
                            b.dma("sp", Kt[0:64, :], d["KA"][g * 64:(g + 1) * 64, :], W=[rK])
                            Vt, rV = Vr.next()
                            b.dma("sp", Vt[:, :, 0:64], d["VA"][:, g * 64:(g + 1) * 64].rearrange("(c p) d -> p c d", p=128), W=[rV])
                            Gk["K"] = (Kt, rK)
                            Gk["V"] = (Vt, rV)
                        if qi == 0:
                            Qt, rQ = Qr.next()
                            for hh_ in range(4):
                                h = 4 * g + hh_
                                b.dma("sp", Qt[:, hh_, :], d["QA"][h * 64:(h + 1) * 64, M + q4 * 512: M + (q4 + 1) * 512], W=[rQ])
                            Gq["Q"] = (Qt, rQ)
                            Gq["ao"] = aor.next()
                        Kt, rK = Gk["K"]
                        Qt, rQ = Gq["Q"]
                        chunks = [(0, None), (1, None)]
                        if qb >= 1:
                            chunks.append((2 + qb - 1, (mP, rmP)))
                        chunks.append((2 + qb, None))
                        if qb <= T // 128 - 2:
                            chunks.append((2 + qb + 1, (mN, rmN)))
                        pts = []
                        for (kc, mask) in chunks:
                            sps, rsps = b.psum()
                            for hh_ in range(4):
                                b.mm(sps[:, hh_ * 128:(hh_ + 1) * 128], lhsT=Kt[0:64, kc * 128:(kc + 1) * 128],
                                     rhs=Qt[:, hh_, qi * 128:(qi + 1) * 128], start=True, stop=True, R=[rK, rQ], W=[rsps])
                            Pt, rP = Pr.next()
                            b.act(Pt[:], sps[:], AF.Exp, R=[rsps], W=[rP], scale=0.125)
                            if mask is not None:
                                b.tt("pool", Pt[:], Pt[:], mask[0][:], ALU.mult, R=[rP, mask[1]], W=[rP])
                            pts.append((kc, Pt, rP))
                        c["pts"] = pts

                    def sB(c=c, Gk=Gk):
                        Vt, rV = Gk["V"]
                        pts = c["pts"]
                        ops_, rops = b.psum()
                        for hh_ in range(4):
                            for ci, (kc, Pt, rP) in enumerate(pts):
                                b.mm(ops_[:, hh_ * 128:(hh_ + 1) * 128], lhsT=Vt[:, kc, :], rhs=Pt[:, hh_ * 128:(hh_ + 1) * 128],
                                     start=(ci == 0), stop=(ci == len(pts) - 1), R=[rV, rP], W=[rops])
                        c["O"] = (ops_, rops)

                    def sC(c=c, Gq=Gq, g=g, q4=q4, qi=qi):
                        ops_, rops = c["O"]
                        ao, rao = Gq["ao"]
                        rec, rrec = recr.next()
                        b.tt("dve", rec[:], ops_[64:128, :], es4[g][64:128, :], ALU.add, R=[rops, res_], W=[rrec])
                        b.act(rec[:], rec[:], AF.Ln, R=[rrec], W=[rrec])
                        b.act(rec[:], rec[:], AF.Exp, R=[rrec], W=[rrec], scale=-1.0)
                        b.tt("dve", ao[:, :, qi * 128:(qi + 1) * 128], ops_[0:64, :].rearrange("p (h q) -> p h q", h=4),
                             rec[:].rearrange("p (h q) -> p h q", h=4), ALU.mult, R=[rops, rrec], W=[rao])
                        if qi == 3:
                            for hh_ in range(4):
                                h = 4 * g + hh_
                                b.dma("sp", d["OT"][h * 64:(h + 1) * 64, M + q4 * 512: M + (q4 + 1) * 512], ao[:, hh_, :], R=[rao])
                    items.append([sA, sB, sC])
        run_pipeline(items)


def phase_hyena_fwd(b, d):
    with b.phase():
        import os
        ocol, rocol = b.const("ocol")
        hw, rhw = b.const("hy_w")
        uT = b.sb([128, 32, 512], BF16, "uT")
        ruT = Res()
        for c4 in range(4):
            b.dma("sp", uT[:, c4 * 8:(c4 + 1) * 8, :], d["UT"][c4 * 1024:(c4 + 1) * 1024, :].rearrange("(c p) f -> p c f", p=128), W=[ruT])
        hsum = b.sb([128, 32, 512], BF16, "hsum")
        hdif = b.sb([128, 32, 512], BF16, "hdif")
        rhs_, rhd_ = Res(), Res()
        with contextlib.ExitStack() as st2:
            old = b.stack
            b.stack = st2
            zT = b.sb([5, T], F32, "zT")
            w1 = b.sb([5, 64], F32, "w1")
            w2 = b.sb([64, 64], F32, "w2")
            w3 = b.sb([64, 64], F32, "w3")
            w4 = b.sb([64, 1024], F32, "w4")
            rz, rw = Res(), Res()
            b.dma("sp", zT[:], d["hy_zT"], W=[rz])
            b.dma("sp", w1[:], d["hy_w1"], W=[rw])
            b.dma("sp", w2[:], d["hy_w2"], W=[rw])
            b.dma("sp", w3[:], d["hy_w3"], W=[rw])
            b.dma("sp", w4[:], d["hy_w4"], W=[rw])
            ar = b.ring(2, [64, 512], F32, "ha")
            kr_ = b.ring(2, [64, 512], F32, "hk")
            hr = b.ring(3, [64, 512], F32, "hh")
            decr = b.ring(2, [128, 1024], F32, "dec")
            hfr = b.ring(2, [128, 512], F32, "hf")
            hbr = b.ring(2, [128, 512], F32, "hb")

            def sin_layer(ps, rps, li):
                bcol = ocol[0:64, 60 + li:61 + li]
                fcol = ocol[0:64, 63 + li:64 + li]
                a, ra = ar.next()
                b.ts("dve", a[:], ps[0:64, :], bcol, fcol, ALU.add, ALU.mult, R=[rps, rocol], W=[ra])
                k, rk = kr_.next()
                b.ts("dve", k[:], a[:], I2P, MAGIC, ALU.mult, ALU.add, R=[ra], W=[rk])
                b.ts("dve", k[:], k[:], -MAGIC, None, ALU.add, None, R=[rk], W=[rk])
                b.stt("dve", a[:], a[:], I2P, k[:], ALU.mult, ALU.subtract, R=[ra, rk], W=[ra])
                h, rh = hr.next()
                b.act(h[:], a[:], AF.Sin, R=[ra], W=[rh], scale=6.283185)
                return h, rh

            for tt in range(8):
                ps, rps = b.psum()
                b.mm(ps[0:64, :], lhsT=w1[:], rhs=zT[:, tt * 512:(tt + 1) * 512], start=True, stop=True, R=[rw, rz], W=[rps])
                h, rh = sin_layer(ps, rps, 0)
                ps, rps = b.psum()
                b.mm(ps[0:64, :], lhsT=w2[:], rhs=h[:], start=True, stop=True, R=[rw, rh], W=[rps])
                h, rh = sin_layer(ps, rps, 1)
                ps, rps = b.psum()
                b.mm(ps[0:64, :], lhsT=w3[:], rhs=h[:], start=True, stop=True, R=[rw, rh], W=[rps])
                h3, rh3 = sin_layer(ps, rps, 2)
                for sub in range(4):
                    ck = tt * 4 + sub
                    dec, rdec = decr.next()
                    b.dma("sp", dec[:], d["hy_dec"][ck * 128:(ck + 1) * 128, :], W=[rdec])
                    psf, rpsf = b.psum()
                    b.mm(psf[:], lhsT=h3[:, sub * 128:(sub + 1) * 128], rhs=w4[:, 0:512], start=True, stop=True, R=[rh3, rw], W=[rpsf])
                    psb, rpsb = b.psum()
                    b.mm(psb[:], lhsT=h3[:, sub * 128:(sub + 1) * 128], rhs=w4[:, 512:1024], start=True, stop=True, R=[rh3, rw], W=[rpsb])
                    hf, rhf = hfr.next()
                    b.tt("dve", hf[:], psf[:], dec[:, 0:512], ALU.mult, R=[rpsf, rdec], W=[rhf])
                    hb, rhb = hbr.next()
                    b.tt("dve", hb[:], psb[:], dec[:, 512:1024], ALU.mult, R=[rpsb, rdec], W=[rhb])
                    b.tt("pool", hsum[:, ck, :], hf[:], hb[:], ALU.add, R=[rhf, rhb], W=[rhs_])
                    b.tt("pool", hdif[:, ck, :], hf[:], hb[:], ALU.subtract, R=[rhf, rhb], W=[rhd_])
            b.P.barrier()
            b.P.emit_block(b.sems)
            b.stack = old
        Cr = b.ring(2, [128, 33, 256], BF16, "Ct")
        Sr = b.ring(2, [128, 33, 256], BF16, "St")
        kcr = b.ring(2, [128, 512], F32, "kc")
        ksr = b.ring(2, [128, 512], F32, "ks")
        t1r = b.ring(2, [128, 512], F32, "t1")
        t2r = b.ring(2, [128, 512], F32, "t2")
        ycr = b.ring(2, [128, 512], BF16, "yc")
        ysr = b.ring(2, [128, 512], BF16, "ys")
        nfc = int(os.environ.get("HY_NF", "33"))
        Ct = St = rC = rS = None
        dtiles = {}

        def load_dft(jc):
            Ct, rC = Cr.next()
            b.dma("sp", Ct[:], d["dftC"][jc], W=[rC])
            St, rS = Sr.next()
            b.dma("sp", St[:], d["dftS"][jc], W=[rS])
            dtiles[jc] = (Ct, rC, St, rS)
        load_dft(0)
        for a in range(nfc):
            jc, off = a // 2, (a % 2) * 128
            if a % 2 == 0:
                Ct, rC, St, rS = dtiles.pop(jc)
                if 2 * (jc + 1) < nfc:
                    load_dft(jc + 1)
            pA, rpA_ = b.psum()
            pKc, rpKc = b.psum()
            pB, rpB_ = b.psum()
            pKs, rpKs = b.psum()
            for s in range(32):
                st_, sp_ = (s == 0), (s == 31)
                b.mm(pA[:], lhsT=Ct[:, s, off:off + 128], rhs=uT[:, s, :], start=st_, stop=sp_, R=[rC, ruT], W=[rpA_])
                b.mm(pKc[:], lhsT=Ct[:, s, off:off + 128], rhs=hsum[:, s, :], start=st_, stop=sp_, R=[rC, rhs_], W=[rpKc])
                b.mm(pB[:], lhsT=St[:, s, off:off + 128], rhs=uT[:, s, :], start=st_, stop=sp_, R=[rS, ruT], W=[rpB_])
                b.mm(pKs[:], lhsT=St[:, s, off:off + 128], rhs=hdif[:, s, :], start=st_, stop=sp_, R=[rS, rhd_], W=[rpKs])
            kc, rkc = kcr.next()
            b.act(kc[:], pKc[:], AF.Copy, R=[rpKc, rhw], W=[rkc], scale=hw[:, a:a + 1])
            ks, rks = ksr.next()
            b.act(ks[:], pKs[:], AF.Copy, R=[rpKs, rhw], W=[rks], scale=hw[:, a:a + 1])
            t1, rt1 = t1r.next()
            b.tt("dve", t1[:], pA[:], kc[:], ALU.mult, R=[rpA_, rkc], W=[rt1])
            t2, rt2 = t2r.next()
            b.tt("dve", t2[:], pB[:], ks[:], ALU.mult, R=[rpB_, rks], W=[rt2])
            yc, ryc = ycr.next()
            b.tt("pool", yc[:], t1[:], t2[:], ALU.subtract, R=[rt1, rt2], W=[ryc])
            b.dma("sp", d["YC"][a * 128:(a + 1) * 128, :], yc[:], R=[ryc])
            t1, rt1 = t1r.next()
            b.tt("dve", t1[:], pA[:], ks[:], ALU.mult, R=[rpA_, rks], W=[rt1])
            t2, rt2 = t2r.next()
            b.tt("dve", t2[:], pB[:], kc[:], ALU.mult, R=[rpB_, rkc], W=[rt2])
            ys, rys = ysr.next()
            b.tt("pool", ys[:], t1[:], t2[:], ALU.add, R=[rt1, rt2], W=[rys])
            b.dma("sp", d["YS"][a * 128:(a + 1) * 128, :], ys[:], R=[rys])


def phase_hyena_inv(b, d):
    with b.phase():
        import os
        ocol, rocol = b.const("ocol")
        Yc = b.sb([128, 33, 512], BF16, "Yc")
        Ys = b.sb([128, 33, 512], BF16, "Ys")
        rYc, rYs = Res(), Res()
        b.dma("sp", Yc[:], d["YC"].rearrange("(f p) c -> p f c", p=128), W=[rYc])
        b.dma("sp", Ys[:], d["YS"].rearrange("(f p) c -> p f c", p=128), W=[rYs])
        Cr = b.ring(2, [128, 33, 256], BF16, "Ct")
        Sr = b.ring(2, [128, 33, 256], BF16, "St")
        ur = b.ring(3, [128, 256], F32, "u")
        xr = b.ring(3, [128, 256], F32, "x0")
        tr = b.ring(2, [128, 256], F32, "t")
        ocr = b.ring(3, [128, 256], BF16, "oc")
        ntt = int(os.environ.get("HY_NT", "16"))
        dtiles = {}

        def load_dft(jc):
            Ct, rC = Cr.next()
            b.dma("sp", Ct[:], d["dftC"][jc], W=[rC])
            St, rS = Sr.next()
            b.dma("sp", St[:], d["dftS"][jc], W=[rS])
            dtiles[jc] = (Ct, rC, St, rS)
        load_dft(0)
        for jc in range(ntt):
            Ct, rC, St, rS = dtiles.pop(jc)
            if jc + 1 < ntt:
                load_dft(jc + 1)
            for i in range(4):
                u, ru = ur.next()
                b.dma("sp", u[:], d["UF"][i * 128:(i + 1) * 128, jc * 256:(jc + 1) * 256], W=[ru])
                x0, rx0 = xr.next()
                b.dma("sp", x0[:], d["X0"][i * 128:(i + 1) * 128, jc * 256:(jc + 1) * 256], W=[rx0])
                ps, rps = b.psum()
                for f in range(33):
                    b.mm(ps[:, 0:256], lhsT=Yc[:, f, i * 128:(i + 1) * 128], rhs=Ct[:, f, :], start=(f == 0), stop=False,
                         R=[rYc, rC], W=[rps])
                    b.mm(ps[:, 0:256], lhsT=Ys[:, f, i * 128:(i + 1) * 128], rhs=St[:, f, :], start=False, stop=(f == 32),
                         R=[rYs, rS], W=[rps])
                t, rt = tr.next()
                b.stt("dve", t[:], u[:], ocol[:, 48 + i:49 + i], ps[:, 0:256], ALU.mult, ALU.add, R=[ru, rocol, rps], W=[rt])
                oc, roc = ocr.next()
                b.tt("pool", oc[:], t[:], x0[:], ALU.mult, R=[rt, rx0], W=[roc])
                b.dma("sp", d["OT"][512 + i * 128:512 + (i + 1) * 128, M + jc * 256:M + (jc + 1) * 256], oc[:], R=[roc])


def phase_final(b, d):
    with b.phase():
        g, rg = load_bc(b, d["final_norm"][0, :], name="fn")
        xr = b.ring(7, [128, 1024], F32, "xt")
        jr = b.ring(2, [128, 1024], BF16, "junk")
        ssr = b.ring(6, [128, 4], F32, "ss")
        orr = b.ring(3, [128, 1024], F32, "o")
        items = []
        for i in range(T // 128):
            c = {}

            def s0(c=c, i=i):
                xt, rx = xr.next()
                b.dma("sp", xt[:], d["xres"][i * 128:(i + 1) * 128, :], W=[rx])
                c["x"] = (xt, rx)

            def s1(c=c):
                xt, rx = c["x"]
                junk, rj = jr.next()
                ss, rss = ssr.next()
                b.act(junk[:], xt[:], AF.Square, R=[rx], W=[rj, rss], accum=ss[:, 0:1])
                c["ss"] = (ss, rss)

            def s2(c=c, i=i):
                xt, rx = c["x"]
                ss, rss = c["ss"]
                rstd_col(b, ss, rss, 1024)
                o, ro = orr.next()
                b.stt("dve", o[:], xt[:], ss[:, 2:3], g[:], ALU.mult, ALU.mult, R=[rx, rss, rg], W=[ro])
                b.dma("sp", d["out"][i * 128:(i + 1) * 128, :], o[:], R=[ro])
            items.append([s0, NOOP, NOOP, s1, s2])
        run_pipeline(items)


_NC_CACHE = {}


def kernel(**inputs):
    inp = {k: np.asarray(v) for k, v in inputs.items()}
    if "nc" not in _NC_CACHE:
        _NC_CACHE["nc"] = build()
    nc = _NC_CACHE["nc"]
    _W_CACHE.clear()
    in_maps = [core_inputs(inp, bidx) for bidx in range(8)]
    res = run_bass_kernel_spmd(nc, in_maps, core_ids=list(range(8)))
    out = np.stack([np.asarray(r["out"], dtype=np.float32) for r in res.results], axis=0)
    return out
```

```python
import contextlib
import math
import numpy as np
import ml_dtypes
import concourse.bass as bass
import concourse.mybir as mybir
from concourse.bass_utils import run_bass_kernel_spmd

F32 = mybir.dt.float32
BF16 = mybir.dt.bfloat16
ALU = mybir.AluOpType
AF = mybir.ActivationFunctionType

T = 4096
M = 256
TA = T + M
D = 1024
FF = 2816
EPS = 1e-6
NFFT = 8192
MAGIC = 12582912.0
I2P = float(1.0 / (2 * math.pi))

COMPUTE = ("pe", "dve", "act", "pool")
QUEUES = ("pe", "dve", "act", "pool", "sp")
N_DMA_SEMS = {"sp": 12, "pool": 8}


class Res:
    __slots__ = ("name", "last_w", "readers", "excl")

    def __init__(self, name="", excl=False):
        self.name = name
        self.last_w = None
        self.readers = []
        self.excl = excl


class Op:
    __slots__ = ("q", "fn", "waits", "tok", "dma")

    def __init__(self, q, fn, dma):
        self.q = q
        self.fn = fn
        self.dma = dma
        self.waits = {}
        self.tok = None


class Prog:
    def __init__(self, nc):
        self.nc = nc
        self.ops = []
        self.cnt = {q: 0 for q in COMPUTE}
        self.dma_cnt = {}
        self.dma_rr = {q: 0 for q in N_DMA_SEMS}
        self.known = {q: {} for q in QUEUES}
        self.maxtok = {}

    def _deps(self, op, reads, writes):
        toks = []
        for r in reads:
            if r.last_w is not None:
                toks.append(r.last_w)
        for w in writes:
            if w.last_w is not None:
                toks.append(w.last_w)
            toks.extend(w.readers)
        kn = self.known[op.q]
        for (sk, v) in toks:
            if op.q == "pe" and sk == "pe":
                continue
            if kn.get(sk, 0) >= v:
                continue
            if op.waits.get(sk, 0) < v:
                op.waits[sk] = v
        for sk, v in op.waits.items():
            kn[sk] = max(kn.get(sk, 0), v)

    def _commit(self, op, reads, writes):
        for w in writes:
            w.last_w = op.tok
            w.readers = []
        for r in reads:
            if r not in writes:
                r.readers.append(op.tok)
                if len(r.readers) > 64:
                    best = {}
                    for (sk, v) in r.readers:
                        if best.get(sk, 0) < v:
                            best[sk] = v
                    r.readers = list(best.items())
        self.maxtok[op.tok[0]] = op.tok[1]
        self.ops.append(op)

    def op(self, q, fn, reads=(), writes=()):
        ex = [r for r in reads if r.excl and r not in writes]
        if ex:
            writes = list(writes) + ex
            reads = [r for r in reads if not r.excl]
        o = Op(q, fn, False)
        self._deps(o, reads, writes)
        self.cnt[q] += 1
        o.tok = (q, self.cnt[q])
        self._commit(o, reads, writes)
        return o

    def dma(self, q, fn, reads=(), writes=()):
        o = Op(q, fn, True)
        self._deps(o, reads, writes)
        k = self.dma_rr[q]
        self.dma_rr[q] = (k + 1) % N_DMA_SEMS[q]
        sk = ("dma", q, k)
        n = self.dma_cnt.get(sk, 0)
        if n > 0 and self.known[q].get(sk, 0) < 16 * n:
            o.waits[sk] = max(o.waits.get(sk, 0), 16 * n)
            self.known[q][sk] = 16 * n
        self.dma_cnt[sk] = n + 1
        o.tok = (sk, 16 * (n + 1))
        self._commit(o, reads, writes)
        return o

    def barrier(self):
        for q in QUEUES:
            o = Op(q, None, False)
            for sk, v in self.maxtok.items():
                if q == "pe" and sk == "pe":
                    continue
                if self.known[q].get(sk, 0) < v:
                    o.waits[sk] = v
                    self.known[q][sk] = v
            if o.waits:
                self.ops.append(o)

    def emit_block(self, sems):
        nc = self.nc
        ops = self.ops
        self.ops = []
        byq = {q: [o for o in ops if o.q == q] for q in QUEUES}
        with nc.Block() as block:
            def run(eng, q):
                for o in byq[q]:
                    for sk, v in o.waits.items():
                        eng.wait_ge(sems[sk], v)
                    if o.fn is None:
                        continue
                    ins = o.fn(eng)
                    ins.then_inc(sems[o.tok[0]], 16 if o.dma else 1)

            @block.tensor
            def _(e):
                run(e, "pe")

            @block.vector
            def _(e):
                run(e, "dve")

            @block.scalar
            def _(e):
                run(e, "act")

            @block.gpsimd
            def _(e):
                run(e, "pool")

            @block.sync
            def _(e):
                run(e, "sp")


class B:
    def __init__(self, nc, dbg, gstack):
        self.nc = nc
        self.P = Prog(nc)
        self.dbg = dbg
        self.gstack = gstack
        self.stack = gstack
        self.uid = 0
        self.consts = {}
        self.sems = {}
        for q in COMPUTE:
            self.sems[q] = gstack.enter_context(nc.semaphore("s_" + q))
        for q, n in N_DMA_SEMS.items():
            for k in range(n):
                self.sems[("dma", q, k)] = gstack.enter_context(nc.semaphore("d_%s%d" % (q, k)))
        self.psbig = gstack.enter_context(nc.psum_tensor("psbig", [128, 4096], F32))
        self.ps = [self.psbig[:, i * 512:(i + 1) * 512] for i in range(8)]
        self.psr = [Res("ps%d" % i, excl=True) for i in range(8)]
        self.psi = 0
        self.pools = None
        self.pool_i = {}

    def const(self, name):
        return self.consts[name]

    def load_consts(self, d, names_bf, names_f32):
        for n in names_bf:
            ap = d[n]
            t = self.gstack.enter_context(self.nc.sbuf_tensor("c_" + n, list(ap.shape), BF16))
            r = Res(n)
            self.dma("sp", t[:], ap, W=[r])
            self.consts[n] = (t, r)
        for n in names_f32:
            ap = d[n]
            t = self.gstack.enter_context(self.nc.sbuf_tensor("c_" + n, list(ap.shape), F32))
            r = Res(n)
            self.dma("sp", t[:], ap, W=[r])
            self.consts[n] = (t, r)
        t = self.gstack.enter_context(self.nc.sbuf_tensor("c_eps", [128, 2], F32))
        r = Res("eps")
        self.memset("dve", t[:, 0:1], EPS, [r])
        self.memset("dve", t[:, 1:2], 0.0, [r])
        self.eps_col = t
        self.eps_res = r
        self.P.barrier()
        self.P.emit_block(self.sems)

    def sb(self, shape, dtype, name=None):
        self.uid += 1
        t = self.stack.enter_context(self.nc.sbuf_tensor("%s_%d" % (name or "t", self.uid), list(shape), dtype))
        return t

    def ring(self, n, shape, dtype, name=None):
        return Ring([(self.sb(shape, dtype, name), Res(name)) for _ in range(n)])

    def psum(self, role=None):
        if role is None or self.pools is None:
            i = self.psi
            self.psi = (i + 1) % len(self.ps)
            return self.ps[i], self.psr[i]
        banks = self.pools[role]
        k = self.pool_i.get(role, 0)
        self.pool_i[role] = (k + 1) % len(banks)
        i = banks[k]
        return self.ps[i], self.psr[i]

    def set_pools(self, pools):
        self.pools = pools
        self.pool_i = {}

    @contextlib.contextmanager
    def phase(self):
        old = self.stack
        with contextlib.ExitStack() as st:
            self.stack = st
            yield
            self.P.barrier()
            self.P.emit_block(self.sems)
        self.stack = old

    def mm(self, out, lhsT, rhs, start, stop, R, W):
        self.P.op("pe", lambda e: e.matmul(out, lhsT=lhsT, rhs=rhs, start=start, stop=stop), R, W)

    def ts(self, q, out, in0, s1, s2, op0, op1, R, W):
        if op1 is None:
            self.P.op(q, lambda e: e.tensor_scalar(out=out, in0=in0, scalar1=s1, scalar2=None, op0=op0), R, W)
        else:
            self.P.op(q, lambda e: e.tensor_scalar(out=out, in0=in0, scalar1=s1, scalar2=s2, op0=op0, op1=op1), R, W)

    def tt(self, q, out, in0, in1, op, R, W):
        self.P.op(q, lambda e: e.tensor_tensor(out=out, in0=in0, in1=in1, op=op), R, W)

    def stt(self, q, out, in0, scalar, in1, op0, op1, R, W):
        self.P.op(q, lambda e: e.scalar_tensor_tensor(out=out, in0=in0, scalar=scalar, in1=in1, op0=op0, op1=op1), R, W)

    def act(self, out, in_, func, R, W, bias=None, scale=1.0, accum=None):
        def f(e):
            kw = {}
            if bias is not None:
                kw["bias"] = bias
            if accum is not None:
                kw["accum_out"] = accum
            return e.activation(out=out, in_=in_, func=func, scale=scale, **kw)
        self.P.op("act", f, R, W)

    def cp(self, q, out, in_, R, W):
        if q == "act":
            self.act(out, in_, AF.Copy, R, W)
        else:
            self.P.op(q, lambda e: e.tensor_copy(out=out, in_=in_), R, W)

    def recip(self, q, out, in_, R, W):
        self.P.op(q, lambda e: e.reciprocal(out=out, in_=in_), R, W)

    def memset(self, q, ap, val, W):
        self.P.op(q, lambda e: e.memset(ap, val), (), W)

    def dma(self, q, out, in_, R=(), W=(), slow=False):
        if slow:
            self.P.dma(q, lambda e: e.dma_start(out=out, in_=in_, allow_slow_non_contiguous=True), R, W)
        else:
            self.P.dma(q, lambda e: e.dma_start(out=out, in_=in_), R, W)


class Ring:
    def __init__(self, items):
        self.items = items
        self.i = 0

    def next(self):
        it = self.items[self.i]
        self.i = (self.i + 1) % len(self.items)
        return it


def _bf(a):
    return np.ascontiguousarray(np.asarray(a, np.float32)).astype(ml_dtypes.bfloat16)


def _rope_angles(rope_dim):
    rows = T // 64
    row_idx = np.repeat(np.arange(rows), 64).astype(np.float32)
    col_idx = np.tile(np.arange(64), rows).astype(np.float32)
    d_axis = rope_dim // 2
    inv_freq = (np.float32(10000.0) ** (-np.arange(0, d_axis, 2, dtype=np.float32) / np.float32(d_axis))).astype(np.float32)
    ang = np.concatenate([row_idx[:, None] * inv_freq, col_idx[:, None] * inv_freq], axis=-1).astype(np.float32)
    return ang


_CONST_CACHE = {}


def host_constants():
    if _CONST_CACHE:
        return _CONST_CACHE
    c = {}
    c["ident"] = _bf(np.eye(128))
    bo = np.zeros((128, 128), np.float32)
    bo[:64, :64] = 1
    bo[64:, 64:] = 1
    c["blockones"] = _bf(bo)
    c["allones"] = _bf(np.ones((128, 128)))
    angA = _rope_angles(64)
    cosA = np.ones((128, TA), np.float32)
    sinA = np.zeros((128, TA), np.float32)
    for p in range(128):
        cosA[p, M:] = np.cos(angA[:, p % 32])
        sinA[p, M:] = np.sin(angA[:, p % 32])
    c["cosA"], c["sinA"] = cosA, sinA
    pa = np.zeros((128, 128), np.float32)
    for hb in (0, 64):
        for i in range(32):
            pa[hb + 32 + i, hb + i] = -1.0
            pa[hb + i, hb + 32 + i] = 1.0
    c["permA"] = _bf(pa)
    angB = _rope_angles(32)
    cosB = np.ones((128, TA), np.float32)
    sinB = np.zeros((128, TA), np.float32)
    for p in range(64, 96):
        cosB[p, M:] = np.cos(angB[:, (p - 64) % 16])
        sinB[p, M:] = np.sin(angB[:, (p - 64) % 16])
    c["cosB"], c["sinB"] = cosB, sinB
    pb = np.zeros((128, 128), np.float32)
    for i in range(16):
        pb[80 + i, 64 + i] = -1.0
        pb[64 + i, 80 + i] = 1.0
    c["permB"] = _bf(pb)
    jj = np.arange(128)[:, None]
    rr = np.arange(128)[None, :]
    c["maskP"] = _bf(np.tile((jj >= rr).astype(np.float32), (1, 4)))
    c["maskN"] = _bf(np.tile((jj <= rr).astype(np.float32), (1, 4)))
    t = np.linspace(0.0, 1.0, T, dtype=np.float32)[:, None]
    w = (2 * np.float32(math.pi) * np.arange(T, dtype=np.float32)[:, None] / np.float32(T)).astype(np.float32)
    f = np.linspace(1e-4, 1.0, 2, dtype=np.float32)[None, :]
    z = np.concatenate([t, np.cos(f * w), -np.sin(f * w)], axis=-1).astype(np.float32)
    c["hy_zT"] = np.ascontiguousarray(z.T)
    cmin = math.log(1e-2) / 1.5
    cmax = math.log(1e-2) / 0.3
    deltas = np.abs(np.linspace(cmin, cmax, 512, dtype=np.float32))
    dec = np.exp(-t * deltas[None, :]).astype(np.float32)
    decb = dec.copy()
    decb[0, :] = 0.0
    c["hy_dec"] = np.ascontiguousarray(np.concatenate([dec, decb], axis=1))
    NE = 4224
    a = np.arange(4096, dtype=np.int64)
    prod = (a[:, None] * a[None, :]) % NFFT
    angd = prod.astype(np.float64) * (2 * math.pi / NFFT)
    C = np.zeros((NE, NE), np.float32)
    S = np.zeros((NE, NE), np.float32)
    C[:4096, :4096] = np.cos(angd)
    S[:4096, :4096] = np.sin(angd)
    alt = np.where(a % 2 == 0, 1.0, -1.0).astype(np.float32)
    C[:4096, 4096] = alt
    C[4096, :4096] = alt
    del angd, prod

    def tiles(Mx):
        Mb = _bf(Mx)
        out = np.zeros((17, 128, 33, 256), ml_dtypes.bfloat16)
        for jc in range(17):
            wd = min(256, NE - jc * 256)
            blk = Mb[:, jc * 256: jc * 256 + wd].reshape(33, 128, wd).transpose(1, 0, 2)
            out[jc, :, :, :wd] = blk
        return out
    c["dftC"] = tiles(C)
    c["dftS"] = tiles(S)
    del C, S
    wt = np.full((128, 33), 2.0 / NFFT, np.float32)
    wt[0, 0] = 1.0 / NFFT
    wt[:, 32] = 1.0 / NFFT
    c["hy_w"] = wt
    _CONST_CACHE.update(c)
    return c


PERM64 = np.concatenate([np.arange(0, 64, 2), np.arange(1, 64, 2)])
PERM32 = np.concatenate([np.arange(0, 32, 2), np.arange(1, 32, 2)])


def colvec(v, n):
    return np.ascontiguousarray(np.asarray(v, np.float32).reshape(n, 128).T)


def host_weights(inp):
    w = {}
    f32 = np.float32
    w["w_mod"] = np.ascontiguousarray(inp["w_mod"].reshape(2 * 1024, 6144))
    w["b_mod"] = np.ascontiguousarray(inp["b_mod"])
    w["norm_mix"] = np.ascontiguousarray(inp["norm_mix"])
    w["norm_ffn"] = np.ascontiguousarray(inp["norm_ffn"])
    w["final_norm"] = np.ascontiguousarray(inp["final_norm"].reshape(1, 1024))
    wi = inp["ev_w_in"][0]
    cols = []
    for h in range(8):
        cols.append(0 + h * 64 + PERM64)
    aq = np.concatenate(cols)
    ak = np.concatenate([896 + g * 64 + PERM64 for g in range(2)])
    av = 1024 + np.arange(128)
    bcq = 512 + np.arange(384)
    bckv = 1152 + np.arange(256)
    kr = 1408 + PERM32
    wev = np.zeros((1024, 1536), f32)
    wev[:, 0:512] = wi[:, aq]
    wev[:, 512:640] = wi[:, ak]
    wev[:, 640:768] = wi[:, av]
    wev[:, 768:1152] = wi[:, bcq]
    wev[:, 1152:1408] = wi[:, bckv]
    wev[:, 1408 + 64:1408 + 96] = wi[:, kr]
    w["wev"] = wev
    wuq = np.zeros((384, 1024), f32)
    uq = inp["b_w_uq"][0]
    for h in range(8):
        wuq[:, h * 128: h * 128 + 64] = uq[:, h * 96: h * 96 + 64]
        wuq[:, h * 128 + 64: h * 128 + 96] = uq[:, h * 96 + 64 + PERM32]
    w["wuq"] = wuq
    ukv = inp["b_w_ukv"][0]
    wukv = np.zeros((256, 1024), f32)
    for h in range(8):
        wukv[:, h * 64:(h + 1) * 64] = ukv[:, h * 128: h * 128 + 64]
        wukv[:, 512 + h * 64: 512 + (h + 1) * 64] = ukv[:, h * 128 + 64: h * 128 + 128]
    w["wukv"] = wukv
    w["ev_wout"] = np.ascontiguousarray(inp["ev_w_out"][0])
    ecol = np.zeros((128, 8), f32)
    ecol[:, 0] = np.tile(inp["a_q_norm"][0][PERM64], 2)
    ecol[:, 1] = np.tile(inp["a_k_norm"][0][PERM64], 2)
    ecol[:, 2:5] = colvec(inp["b_q_norm"][0], 3)
    ecol[:, 5:7] = colvec(inp["b_kv_norm"][0], 2)
    w["ecol"] = ecol
    wo = inp["od_w_in"][0]
    dq = np.concatenate([h * 64 + PERM64 for h in range(8)])
    dk = np.concatenate([2048 + g * 64 + PERM64 for g in range(2)])
    wod = np.zeros((1024, 2304), f32)
    wod[:, 0:512] = wo[:, dq]
    wod[:, 512:640] = wo[:, dk]
    wod[:, 640:768] = wo[:, 2176:2304]
    wod[:, 768:2304] = wo[:, 512:2048]
    w["wod"] = wod
    w["od_wout"] = np.ascontiguousarray(inp["od_w_out"][0])
    ocol = np.zeros((128, 80), f32)
    cw = inp["c_conv_w"][0]
    for k in range(3):
        ocol[:, k * 12:(k + 1) * 12] = colvec(cw[k], 12)
    ocol[:, 36:48] = colvec(inp["c_conv_b"][0], 12)
    ocol[:, 48:52] = colvec(inp["c_bias"][0], 4)
    ocol[:, 52:60] = np.broadcast_to(inp["d_sink"][0][None, :], (128, 8))
    ocol[:64, 60] = inp["c_filt_b1"][0]
    ocol[:64, 61] = inp["c_filt_b2"][0]
    ocol[:64, 62] = inp["c_filt_b3"][0]
    ocol[:64, 63:66] = inp["c_filt_freq"][0].T
    w["ocol"] = ocol
    w["hy_w1"] = np.ascontiguousarray(inp["c_filt_w1"][0])
    w["hy_w2"] = np.ascontiguousarray(inp["c_filt_w2"][0])
    w["hy_w3"] = np.ascontiguousarray(inp["c_filt_w3"][0])
    w["hy_w4"] = np.ascontiguousarray(inp["c_filt_w4"][0])
    w["ffn_up"] = np.ascontiguousarray(inp["ffn_w_up"].reshape(2 * 1024, 5632))
    w["ffn_down"] = np.ascontiguousarray(inp["ffn_w_down"].reshape(2 * 2816, 1024))
    fcol = np.zeros((2, 128, 176), f32)
    for l in range(2):
        for k in range(3):
            fcol[l, :, k * 44:(k + 1) * 44] = colvec(inp["ffn_conv_w"][l, k], 44)
        fcol[l, :, 132:176] = colvec(inp["ffn_conv_b"][l], 44)
    w["fcol"] = fcol.reshape(256, 176)
    return w


def load_bc(b, row_ap, q="sp", name="bc"):
    F = row_ap.shape[-1]
    t = b.sb([128, F], F32, name)
    r = Res(name)
    b.dma(q, t[:], row_ap.partition_broadcast(128), W=[r])
    return t, r


def load_w_bf16(b, dram_ap, nk, ncol, name, ngrp=1, order=None):
    t = b.sb([128, nk, ncol], BF16, name)
    if ngrp == 1:
        r = Res(name)
        for j in range(nk):
            b.dma("pool", t[:, j, :], dram_ap[j * 128:(j + 1) * 128, :], W=[r])
        return t, r
    gw = ncol // ngrp
    rs = [Res("%s%d" % (name, g)) for g in range(ngrp)]
    for g in (order or range(ngrp)):
        for j in range(nk):
            b.dma("pool", t[:, j, g * gw:(g + 1) * gw], dram_ap[j * 128:(j + 1) * 128, g * gw:(g + 1) * gw], W=[rs[g]])
    return t, rs


class NormCtx:
    def __init__(self, b):
        self.junk = b.ring(2, [128, 1024], BF16, "junk")
        self.ss = b.ring(8, [128, 4], F32, "ss")
        self.h32 = b.ring(3, [128, 1024], F32, "h32")
        self.hb = b.ring(5, [128, 1024], BF16, "hb")


def rstd_col(b, ss, rss, n):
    b.act(ss[:, 1:2], ss[:, 0:1], AF.Ln, R=[rss, b.eps_res], W=[rss], bias=b.eps_col[:, 0:1], scale=1.0 / n)
    b.act(ss[:, 2:3], ss[:, 1:2], AF.Exp, R=[rss], W=[rss], scale=-0.5)


def NOOP():
    pass


def run_pipeline(items):
    n = len(items)
    if n == 0:
        return
    S = max(len(it) for it in items)
    for step in range(n + S - 1):
        for s_ in reversed(range(S)):
            t = step - s_
            if 0 <= t < n and s_ < len(items[t]):
                items[t][s_]()


def norm_stages(b, nx, c, mult, rmult, sh, rsh, ident, rid):
    def s_sq():
        junk, rj = nx.junk.next()
        ss, rss = nx.ss.next()
        b.act(junk[:], c["x"], AF.Square, R=[c["rx"]], W=[rj, rss], accum=ss[:, 0:1])
        c["ss"] = (ss, rss)

    def s_scale():
        ss, rss = c["ss"]
        rstd_col(b, ss, rss, 1024)
        h32, rh = nx.h32.next()
        b.stt("dve", h32[:], c["x"], ss[:, 2:3], mult[:], ALU.mult, ALU.mult, R=[c["rx"], rss, rmult], W=[rh])
        hb, rhb = nx.hb.next()
        b.tt("pool", hb[:], h32[:], sh[:], ALU.add, R=[rh, rsh], W=[rhb])
        c["hb"] = (hb, rhb)

    def s_tr():
        hb, rhb = c["hb"]
        hTg, rhTg, col = c["hTg"], c["rhTg"], c["col"]
        for half in range(2):
            ps, rps = b.psum("t")
            for jj in range(4):
                j = half * 4 + jj
                b.mm(ps[:, jj * 128:(jj + 1) * 128], lhsT=hb[:, j * 128:(j + 1) * 128], rhs=ident[:], start=True, stop=True,
                     R=[rhb, rid], W=[rps])
            dst = hTg[:, half * 4:(half + 1) * 4, col:col + 128]
            src = ps[:, :].rearrange("p (j t) -> p j t", j=4)
            b.cp("act" if half == 0 else "dve", dst, src, R=[rps], W=[rhTg])
        if c.get("after") is not None:
            c["after"]()
    return [s_sq, s_scale, s_tr]


def mods_jobs(b, d, l, wring, bmr, orow):
    scl, rscl = b.const("scl")
    jobs = []
    for t in range(12):
        st = {}

        def load(t=t, st=st):
            wt, rw = wring.next()
            b.dma("sp", wt[:], d["w_mod"][l * 1024:(l + 1) * 1024, t * 512:(t + 1) * 512].rearrange("(j p) c -> p j c", p=128), W=[rw])
            bm, rbm = bmr.next()
            b.dma("sp", bm[:], d["b_mod"][l, t * 512:(t + 1) * 512].partition_broadcast(2), W=[rbm])
            st["w"] = (wt, rw, bm, rbm)

        def comp(t=t, st=st):
            wt, rw, bm, rbm = st["w"]
            ps, rps = b.psum("s")
            for j in range(8):
                b.mm(ps[0:2, :], lhsT=scl[:, 2 * j:2 * j + 2], rhs=wt[:, j, :], start=(j == 0), stop=(j == 7), R=[rscl, rw], W=[rps])
            o, ro = orow.next()
            b.tt("dve", o[:], ps[0:2, :], bm[:], ALU.add, R=[rps, rbm], W=[ro])
            b.dma("sp", d["mods"][l, :, t * 512:(t + 1) * 512], o[:], R=[ro])
        jobs.append((load, comp))
    return jobs


def phase_mods(b, d, layers=(0,)):
    scl = b.gstack.enter_context(b.nc.sbuf_tensor("c_scl", [128, 16], F32))
    rscl = Res("scl")
    b.consts["scl"] = (scl, rscl)
    with b.phase():
        cv = b.sb([128, 16], F32, "cv")
        rcv = Res()
        b.dma("sp", cv[:], d["cvec"], W=[rcv])
        b.act(scl[:], cv[:], AF.Silu, R=[rcv], W=[rscl])
        wring = b.ring(7, [128, 8, 512], F32, "wm")
        bmr = b.ring(7, [2, 512], F32, "bm")
        orow = b.ring(4, [2, 512], F32, "orow")
        jobs = []
        for l in layers:
            jobs += mods_jobs(b, d, l, wring, bmr, orow)
        AHEAD = 5
        for i in range(len(jobs) + AHEAD):
            if i < len(jobs):
                jobs[i][0]()
            if i - AHEAD >= 0:
                jobs[i - AHEAD][1]()


def load_mod(b, d, l, which, m, name):
    return load_bc(b, d["mods"][l, which, m * 1024:(m + 1) * 1024], name=name)


def load_mult(b, d, l, which, m, gain_row, name):
    sc, rsc = load_mod(b, d, l, which, m, name + "_sc")
    g, rg = load_bc(b, gain_row, name=name + "_g")
    b.stt("dve", sc[:], sc[:], 1.0, g[:], ALU.add, ALU.mult, R=[rsc, rg], W=[rsc])
    return sc, rsc


def phase_norm(b, d, l, src_lat, src_ctx):
    with b.phase():
        ident, rid = b.const("ident")
        nx = NormCtx(b)
        xr = b.ring(7, [128, 1024], F32, "xt")
        hg = b.ring(4, [128, 8, 512], BF16, "hTg")
        items = []
        for which, src, ntile, dst in ((1, src_ctx, M // 128, d["HTc"]), (0, src_lat, T // 128, d["HTl"])):
            mult, rmult = load_mult(b, d, l, which, 1, d["norm_mix"][l, :], "m1")
            sh, rsh = load_mod(b, d, l, which, 0, "sh1")
            for g0 in range(0, ntile, 4):
                ng = min(4, ntile - g0)
                grp = {}
                for i in range(ng):
                    c = {"col": i * 128}

                    def s_load(c=c, grp=grp, i=i, src=src, g0=g0):
                        if i == 0:
                            grp["h"] = hg.next()
                        c["hTg"], c["rhTg"] = grp["h"]
                        xt, rx = xr.next()
                        b.dma("sp", xt[:], src[(g0 + i) * 128:(g0 + i + 1) * 128, :], W=[rx])
                        c["x"], c["rx"] = xt[:], rx
                    if i == ng - 1:
                        def after(c=c, dst=dst, g0=g0, ng=ng):
                            b.dma("sp", dst[:, 1 + g0 * 128: 1 + (g0 + ng) * 128].rearrange("(j p) t -> p j t", p=128),
                                  c["hTg"][:, :, 0:ng * 128], R=[c["rhTg"]])
                        c["after"] = after
                    st = norm_stages(b, nx, c, mult, rmult, sh, rsh, ident, rid)
                    items.append([s_load, NOOP, NOOP, st[0], st[1], NOOP, st[2]])
        run_pipeline(items)


IN_SPECS = [
    ("x", [T, D], F32), ("ctx", [M, D], F32), ("cvec", [128, 16], F32),
    ("w_mod", [2048, 6144], F32), ("b_mod", [2, 6144], F32),
    ("norm_mix", [2, D], F32), ("norm_ffn", [2, D], F32), ("final_norm", [1, D], F32),
    ("wev", [1024, 1536], F32), ("wuq", [384, 1024], F32), ("wukv", [256, 1024], F32),
    ("ev_wout", [1024, 1024], F32), ("ecol", [128, 8], F32),
    ("wod", [1024, 2304], F32), ("od_wout", [1024, 1024], F32), ("ocol", [128, 80], F32),
    ("hy_w1", [5, 64], F32), ("hy_w2", [64, 64], F32), ("hy_w3", [64, 64], F32), ("hy_w4", [64, 1024], F32),
    ("ffn_up", [2048, 5632], F32), ("ffn_down", [5632, 1024], F32), ("fcol", [256, 176], F32),
    ("ident", [128, 128], BF16), ("blockones", [128, 128], BF16), ("allones", [128, 128], BF16),
    ("permA", [128, 128], BF16), ("permB", [128, 128], BF16), ("maskP", [128, 512], BF16), ("maskN", [128, 512], BF16),
    ("cosA", [128, TA], F32), ("sinA", [128, TA], F32), ("cosB", [128, TA], F32), ("sinB", [128, TA], F32),
    ("hy_zT", [5, T], F32), ("hy_dec", [T, 1024], F32), ("hy_w", [128, 33], F32),
    ("dftC", [17, 128, 33, 256], BF16), ("dftS", [17, 128, 33, 256], BF16),
]

SCRATCH = [
    ("mods", [2, 2, 6144], F32),
    ("HTl", [D, T + 2], BF16), ("HTc", [D, M + 2], BF16),
    ("xres", [T, D], F32), ("cres", [M, D], F32),
    ("QA", [512, TA], BF16), ("KA", [128, TA], BF16), ("VA", [TA, 128], BF16),
    ("QB", [8, 96, TA], BF16), ("KBn", [512, TA], BF16), ("KR", [32, TA], BF16), ("VB", [TA, 512], BF16),
    ("OT", [D, TA], BF16),
    ("X0", [512, T], F32), ("UF", [512, T], F32), ("UT", [T, 512], BF16),
    ("YC", [33 * 128, 512], BF16), ("YS", [33 * 128, 512], BF16),
]


def build(dbg=(), stop=None):
    nc = bass.Bass("TRN2", target_bir_lowering=False)
    d = {}
    for name, shape, dt in IN_SPECS:
        d[name] = nc.dram_tensor(name, list(shape), dt, kind="ExternalInput").ap()
    for name, shape, dt in SCRATCH:
        if name in dbg:
            d[name] = nc.dram_tensor(name, list(shape), dt, kind="ExternalOutput").ap()
        else:
            d[name] = nc.dram_tensor(name, list(shape), dt).ap()
    d["out"] = nc.dram_tensor("out", [T, D], F32, kind="ExternalOutput").ap()
    with contextlib.ExitStack() as gstack:
        b = B(nc, dbg, gstack)
        b.load_consts(d, ["ident", "blockones", "allones", "permA", "permB", "maskP", "maskN"], ["ecol", "ocol", "hy_w"])
        emit_program(b, d, stop)
    return nc


def emit_program(b, d, stop):
    with b.phase():
        z = b.sb([128, 8, 2], BF16, "z")
        rz = Res()
        b.memset("dve", z[:], 0.0, [rz])
        for t, n in ((d["HTl"], T), (d["HTc"], M)):
            for c in (0, n + 1):
                b.dma("sp", t[:, c:c + 1].rearrange("(j p) t -> p j t", p=128), z[:, :, 0:1], R=[rz], slow=True)
    phase_mods(b, d, layers=(0, 1))
    if stop == "mods":
        return
    phase_norm(b, d, 0, d["x"], d["ctx"])
    if stop == "norm0":
        return
    phase_proj_even(b, d)
    if stop == "proj0":
        return
    phase_attn_even(b, d)
    if stop == "attn0":
        return
    phase_outproj(b, d, 0, d["ev_wout"], d["x"], d["ctx"], True)
    if stop == "outproj0":
        return
    phase_ffn(b, d, 0, True)
    if stop == "ffn0":
        return
    phase_norm(b, d, 1, d["xres"], d["cres"])
    phase_proj_odd(b, d)
    if stop == "proj1":
        return
    phase_attn_odd(b, d)
    if stop == "attn1":
        return
    phase_hyena_fwd(b, d)
    phase_hyena_inv(b, d)
    if stop == "hyena":
        return
    phase_outproj(b, d, 1, d["od_wout"], d["xres"], None, False)
    phase_ffn(b, d, 1, False, final=True)


_W_CACHE = {}


def core_inputs(inp, bidx):
    c = host_constants()
    if "w" not in _W_CACHE:
        _W_CACHE["w"] = host_weights(inp)
    w = _W_CACHE["w"]
    m = dict(c)
    m.update(w)
    m["x"] = np.ascontiguousarray(inp["x"][bidx])
    m["ctx"] = np.ascontiguousarray(inp["ctx"][bidx])
    cv = np.zeros((128, 16), np.float32)
    cv[:, 0::2] = colvec(inp["c"][bidx], 8)
    cv[:, 1::2] = colvec(inp["c_ctx"], 8)
    m["cvec"] = cv
    return m


class ProjTmp:
    def __init__(self, b):
        self.sq = b.ring(3, [128, 512], BF16, "sq")
        self.q32 = b.ring(4, [128, 512], F32, "q32")
        self.lnt = b.ring(2, [128, 512], F32, "lnt")
        self.rs = b.ring(3, [128, 512], F32, "rs")
        self.qn = b.ring(3, [128, 512], BF16, "qn")
        self.t1 = b.ring(3, [128, 512], BF16, "t1")
        self.t2 = b.ring(3, [128, 512], BF16, "t2")
        self.ob = b.ring(4, [128, 512], BF16, "ob")
        self.ss = b.ring(6, [128, 4], F32, "sstok")


def rs_from_ss(b, tmp, ssps, rssps, W, n):
    lnt, rl = tmp.lnt.next()
    b.act(lnt[:, :W], ssps[:, :W], AF.Ln, R=[rssps, b.eps_res], W=[rl], bias=b.eps_col[:, 0:1], scale=1.0 / n)
    rs, rrs = tmp.rs.next()
    b.act(rs[:, :W], lnt[:, :W], AF.Exp, R=[rl], W=[rrs], scale=-0.5)
    return rs, rrs


def proj_fm(b, ps, rps, W, wt, rw, nk, c0, hT, rhT):
    for j in range(nk):
        b.mm(ps[:, :W], lhsT=wt[:, j, c0:c0 + 128], rhs=hT[:, j, :W], start=(j == 0), stop=(j == nk - 1), R=[rw, rhT], W=[rps])


def rope_stages(b, tmp, c, W, perm, rperm, cos, sin, rtab, store):
    ident, rid = b.const("ident")

    def s_swap():
        t1, rt1 = tmp.t1.next()
        t2, rt2 = tmp.t2.next()
        if c.get("xps") is not None:
            ps, rps = c["xps"]
            b.tt("dve", t1[:, :W], ps[:, :W], cos[:, :W], ALU.mult, R=[rps, rtab], W=[rt1])
            b.tt("dve", t2[:, :W], ps[:, :W], sin[:, :W], ALU.mult, R=[rps, rtab], W=[rt2])
        else:
            qn, rqn = c["qn"]
            b.tt("dve", t1[:, :W], qn[:, :W], cos[:, :W], ALU.mult, R=[rqn, rtab], W=[rt1])
            b.tt("pool", t2[:, :W], qn[:, :W], sin[:, :W], ALU.mult, R=[rqn, rtab], W=[rt2])
        c["t12"] = (t1, rt1, t2, rt2)

    def s_comb():
        t1, rt1, t2, rt2 = c["t12"]
        sw, rsw = b.psum("w")
        b.mm(sw[:, :W], lhsT=ident[:], rhs=t1[:, :W], start=True, stop=False, R=[rid, rt1], W=[rsw])
        b.mm(sw[:, :W], lhsT=perm[:], rhs=t2[:, :W], start=False, stop=True, R=[rperm, rt2], W=[rsw])
        ob, rob = tmp.ob.next()
        b.cp("act", ob[:, :W], sw[:, :W], R=[rsw], W=[rob])
        store(ob, rob)
    return s_swap, s_comb


def groups_all():
    g = [(True, 1, 0, M)]
    for i in range(T // 512):
        g.append((False, 1 + 512 * i, M + 512 * i, 512))
    return g


def load_scaled_w(b, dram_ap, nk, ncol, colbase, name):
    ecol, recol = b.const("ecol")
    st = b.sb([128, nk, ncol], F32, name + "_st")
    rst = Res()
    t = b.sb([128, nk, ncol], BF16, name)
    r = Res(name)
    for j in range(nk):
        b.dma("sp", st[:, j, :], dram_ap[j * 128:(j + 1) * 128, :], W=[rst])
    for j in range(nk):
        b.ts("dve", t[:, j, :], st[:, j, :], ecol[:, colbase + j:colbase + j + 1], None, ALU.mult, None, R=[rst, recol], W=[r])
    return t, r


def phase_proj_even(b, d):
    with b.phase():
        b.set_pools({"p": [0, 1, 2], "s": [3, 4], "w": [5, 6, 7]})
        ecol, recol = b.const("ecol")
        bo, rbo = b.const("blockones")
        ao, rao = b.const("allones")
        pA, rpA = b.const("permA")
        pB, rpB = b.const("permB")
        wev, rwev = load_w_bf16(b, d["wev"], 8, 1536, "wev")
        wuq, rwuq = load_scaled_w(b, d["wuq"], 3, 1024, 2, "wuq")
        wukv, rwukv = load_scaled_w(b, d["wukv"], 2, 1024, 5, "wukv")
        tmp = ProjTmp(b)
        hg = b.ring(2, [128, 8, 512], BF16, "hT")
        tabs = b.ring(2, [128, 4, 512], F32, "tabs")
        cqr = b.ring(2, [128, 3, 512], BF16, "cq")
        rtabr = b.ring(2, [128, 2, 512], F32, "rtab")
        ckvr = b.ring(2, [128, 2, 512], BF16, "ckv")
        sqkvr = b.ring(2, [128, 2, 512], BF16, "sqkv")
        var = b.ring(3, [128, 128], BF16, "va")
        vbr = b.ring(3, [128, 512], BF16, "vb")
        items = []
        side = []
        for gi, (isc, sc0, c0, W) in enumerate(groups_all()):
            G = {}
            for _ in range(2 if gi < 3 else 1):
                if side:
                    ld_, cp_ = side.pop(0)
                    items.append([ld_] + [NOOP] * 7 + [cp_])

            def s_load(G=G, isc=isc, sc0=sc0, c0=c0, W=W):
                hT, rhT = hg.next()
                src = d["HTc"] if isc else d["HTl"]
                b.dma("sp", hT[:, :, :W], src[:, sc0:sc0 + W].rearrange("(j p) t -> p j t", p=128), W=[rhT])
                tab, rtab = tabs.next()
                for i, nm in enumerate(("cosA", "sinA", "cosB", "sinB")):
                    b.dma("sp", tab[:, i, :W], d[nm][:, c0:c0 + W], W=[rtab])
                G["hT"] = (hT, rhT)
                G["tab"] = (tab, rtab)
            items.append([s_load])

            for cidx in range(5):
                c = {}

                def a0(c=c, G=G, cidx=cidx, W=W):
                    ps, rps = b.psum("p")
                    hT, rhT = G["hT"]
                    proj_fm(b, ps, rps, W, wev, rwev, 8, cidx * 128, hT, rhT)
                    c["ps"] = (ps, rps)

                def a1(c=c, W=W):
                    ps, rps = c["ps"]
                    sq, rsq = tmp.sq.next()
                    b.act(sq[:, :W], ps[:, :W], AF.Square, R=[rps], W=[rsq])
                    q32, rq32 = tmp.q32.next()
                    b.cp("dve", q32[:, :W], ps[:, :W], R=[rps], W=[rq32])
                    c["q32"] = (q32, rq32)
                    ssps, rssps = b.psum("s")
                    b.mm(ssps[:, :W], lhsT=bo[:], rhs=sq[:, :W], start=True, stop=True, R=[rbo, rsq], W=[rssps])
                    c["ss"] = (ssps, rssps)

                def a2(c=c, W=W):
                    ssps, rssps = c["ss"]
                    c["rs"] = rs_from_ss(b, tmp, ssps, rssps, W, 64)

                def store(ob, rob, cidx=cidx, c0=c0, W=W):
                    if cidx < 4:
                        b.dma("sp", d["QA"][cidx * 128:(cidx + 1) * 128, c0:c0 + W], ob[:, :W], R=[rob])
                    else:
                        b.dma("sp", d["KA"][:, c0:c0 + W], ob[:, :W], R=[rob])

                def tabA(G=G):
                    tab, rtab = G["tab"]
                    return tab[:, 0, :], tab[:, 1, :], rtab

                def a3(c=c, W=W, cidx=cidx, G=G, store=store):
                    q32, rq32 = c["q32"]
                    rs, rrs = c["rs"]
                    gcol = ecol[:, 0:1] if cidx < 4 else ecol[:, 1:2]
                    qn, rqn = tmp.qn.next()
                    b.stt("dve", qn[:, :W], q32[:, :W], gcol, rs[:, :W], ALU.mult, ALU.mult, R=[rq32, rrs, recol], W=[rqn])
                    c["qn"] = (qn, rqn)
                    tab, rtab = G["tab"]
                    sw_, cb_ = rope_stages(b, tmp, c, W, pA, rpA, tab[:, 0, :], tab[:, 1, :], rtab, store)
                    c["comb"] = cb_
                    sw_()

                def a4(c=c):
                    c["comb"]()
                items.append([a0, a1, a2, a3, a4])

            for sub in range(W // 128):
                c = {}

                def v0(c=c, G=G, sub=sub):
                    hT, rhT = G["hT"]
                    ps, rps = b.psum("p")
                    for j in range(8):
                        b.mm(ps[:, 0:128], lhsT=hT[:, j, sub * 128:(sub + 1) * 128], rhs=wev[:, j, 640:768], start=(j == 0), stop=(j == 7),
                             R=[rhT, rwev], W=[rps])
                    c["ps"] = (ps, rps)

                def v1(c=c, sub=sub, c0=c0):
                    ps, rps = c["ps"]
                    va, rva = var.next()
                    b.cp("act", va[:], ps[:, 0:128], R=[rps], W=[rva])
                    b.dma("sp", d["VA"][c0 + sub * 128:c0 + (sub + 1) * 128, :], va[:], R=[rva])
                items.append([v0, v1])

            for j in range(3):
                c = {}

                def q0(c=c, G=G, j=j, W=W):
                    hT, rhT = G["hT"]
                    ps, rps = b.psum("p")
                    proj_fm(b, ps, rps, W, wev, rwev, 8, 768 + j * 128, hT, rhT)
                    c["ps"] = (ps, rps)

                def q1(c=c, G=G, j=j, W=W):
                    ps, rps = c["ps"]
                    if j == 0:
                        G["cq"] = cqr.next()
                        G["ssq"] = b.psum("s")
                    cq, rcq = G["cq"]
                    ssq, rssq = G["ssq"]
                    b.cp("dve", cq[:, j, :W], ps[:, :W], R=[rps], W=[rcq])
                    sq, rsq = tmp.sq.next()
                    b.act(sq[:, :W], ps[:, :W], AF.Square, R=[rps], W=[rsq])
                    b.mm(ssq[:, :W], lhsT=ao[:], rhs=sq[:, :W], start=(j == 0), stop=(j == 2), R=[rao, rsq], W=[rssq])

                def q2(G=G, j=j, W=W):
                    if j == 2:
                        ssq, rssq = G["ssq"]
                        rs, rrs = rs_from_ss(b, tmp, ssq, rssq, W, 384)
                        tab, rtab = G["tab"]
                        rt, rrt = rtabr.next()
                        b.tt("dve", rt[:, 0, :W], rs[:, :W], tab[:, 2, :W], ALU.mult, R=[rrs, rtab], W=[rrt])
                        b.tt("pool", rt[:, 1, :W], rs[:, :W], tab[:, 3, :W], ALU.mult, R=[rrs, rtab], W=[rrt])
                        G["rtab"] = (rt, rrt)
                items.append([q0, q1, q2])
            for h in range(8):
                c = {}

                def h0(c=c, G=G, h=h, W=W):
                    cq, rcq = G["cq"]
                    ps, rps = b.psum("p")
                    for j in range(3):
                        b.mm(ps[:, :W], lhsT=wuq[:, j, h * 128:(h + 1) * 128], rhs=cq[:, j, :W], start=(j == 0), stop=(j == 2),
                             R=[rwuq, rcq], W=[rps])
                    c["ps"] = (ps, rps)

                def storeq(ob, rob, h=h, c0=c0, W=W):
                    b.dma("sp", d["QB"][h, :, c0:c0 + W], ob[0:96, :W], R=[rob])

                def h1(c=c, G=G, W=W, storeq=storeq):
                    c["xps"] = c["ps"]
                    rt, rrt = G["rtab"]
                    sw_, cb_ = rope_stages(b, tmp, c, W, pB, rpB, rt[:, 0, :], rt[:, 1, :], rrt, storeq)
                    c["comb"] = cb_
                    sw_()

                def h2(c=c):
                    c["comb"]()
                items.append([h0, h1, h2])

            for j in range(2):
                c = {}

                def k0(c=c, G=G, j=j, W=W):
                    hT, rhT = G["hT"]
                    ps, rps = b.psum("p")
                    proj_fm(b, ps, rps, W, wev, rwev, 8, 1152 + j * 128, hT, rhT)
                    c["ps"] = (ps, rps)

                def k1(c=c, G=G, j=j, W=W):
                    ps, rps = c["ps"]
                    if j == 0:
                        G["ckv"] = ckvr.next()
                        G["sqkv"] = sqkvr.next()
                        G["sskv"] = b.psum("s")
                    ckv, rckv = G["ckv"]
                    sqkv, rsqkv = G["sqkv"]
                    sskv, rsskv = G["sskv"]
                    b.cp("dve", ckv[:, j, :W], ps[:, :W], R=[rps], W=[rckv])
                    b.act(sqkv[:, j, :W], ps[:, :W], AF.Square, R=[rps], W=[rsqkv])
                    b.mm(sskv[:, :W], lhsT=ao[:], rhs=sqkv[:, j, :W], start=(j == 0), stop=(j == 1), R=[rao, rsqkv], W=[rsskv])

                def k2(G=G, j=j, W=W):
                    if j == 1:
                        sskv, rsskv = G["sskv"]
                        G["rskv"] = rs_from_ss(b, tmp, sskv, rsskv, W, 256)
                items.append([k0, k1, k2])
            for cc in range(4):
                c = {}

                def n0(c=c, G=G, cc=cc, W=W):
                    ckv, rckv = G["ckv"]
                    ps, rps = b.psum("p")
                    for j in range(2):
                        b.mm(ps[:, :W], lhsT=wukv[:, j, cc * 128:(cc + 1) * 128], rhs=ckv[:, j, :W], start=(j == 0), stop=(j == 1),
                             R=[rwukv, rckv], W=[rps])
                    c["ps"] = (ps, rps)

                def n1(c=c, G=G, cc=cc, W=W, c0=c0):
                    ps, rps = c["ps"]
                    rskv, rrskv = G["rskv"]
                    ob, rob = tmp.ob.next()
                    b.tt("dve", ob[:, :W], ps[:, :W], rskv[:, :W], ALU.mult, R=[rps, rrskv], W=[rob])
                    b.dma("sp", d["KBn"][cc * 128:(cc + 1) * 128, c0:c0 + W], ob[:, :W], R=[rob])
                items.append([n0, n1])
            for sub in range(W // 128):
                c = {}

                def w0(c=c, G=G, sub=sub):
                    ckv, rckv = G["ckv"]
                    sqkv, rsqkv = G["sqkv"]
                    ps, rps = b.psum("p")
                    pss, rpss = b.psum("s")
                    for j in range(2):
                        b.mm(ps[:, :], lhsT=ckv[:, j, sub * 128:(sub + 1) * 128], rhs=wukv[:, j, 512:1024], start=(j == 0), stop=(j == 1),
                             R=[rckv, rwukv], W=[rps])
                    for j in range(2):
                        b.mm(pss[:, 0:1], lhsT=sqkv[:, j, sub * 128:(sub + 1) * 128], rhs=ao[:, 0:1], start=(j == 0), stop=(j == 1),
                             R=[rsqkv, rao], W=[rpss])
                    c["ps"] = (ps, rps)
                    c["pss"] = (pss, rpss)

                def w1(c=c):
                    pss, rpss = c["pss"]
                    ss, rss = tmp.ss.next()
                    b.cp("dve", ss[:, 0:1], pss[:, 0:1], R=[rpss], W=[rss])
                    rstd_col(b, ss, rss, 256)
                    c["ss"] = (ss, rss)

                def w2(c=c, sub=sub, c0=c0):
                    ps, rps = c["ps"]
                    ss, rss = c["ss"]
                    vb, rvb = vbr.next()
                    b.ts("dve", vb[:], ps[:, :], ss[:, 2:3], None, ALU.mult, None, R=[rps, rss], W=[rvb])
                    b.dma("sp", d["VB"][c0 + sub * 128:c0 + (sub + 1) * 128, :], vb[:], R=[rvb])
                items.append([w0, w1, w2])
            c = {}

            def r0(c=c, G=G, W=W):
                hT, rhT = G["hT"]
                ps, rps = b.psum("p")
                proj_fm(b, ps, rps, W, wev, rwev, 8, 1408, hT, rhT)
                c["ps"] = (ps, rps)

            def storekr(ob, rob, c0=c0, W=W):
                b.dma("sp", d["KR"][:, c0:c0 + W], ob[64:96, :W], R=[rob])

            def r1(c=c, G=G, W=W, storekr=storekr):
                c["xps"] = c["ps"]
                tab, rtab = G["tab"]
                sw_, cb_ = rope_stages(b, tmp, c, W, pB, rpB, tab[:, 2, :], tab[:, 3, :], rtab, storekr)
                c["comb"] = cb_
                sw_()

            def r2(c=c):
                c["comb"]()
            items.append([r0, r1, r2])
        for (ld_, cp_) in side:
            items.append([ld_, cp_])
        run_pipeline(items)
        b.set_pools(None)


def qtiles_all(with_ctx=True):
    q = []
    if with_ctx:
        q.append((0, M, [0, 1]))
    for i in range(T // 512):
        q.append((M + 512 * i, 512, list(range(TA // 128))))
    return q


def phase_attn_even(b, d):
    with b.phase():
        import os
        nhead_dbg = int(os.environ.get("ATT_NH", "99"))
        Kr = b.ring(2, [128, TA], BF16, "Kt")
        Vr = b.ring(2, [128, TA // 128, 128], BF16, "Vt")
        for vt, rv in Vr.items:
            b.memset("pool", vt[:, :, 64:128], 1.0, [rv])
        Qr = b.ring(3, [128, 512], BF16, "Qt")
        Pr = b.ring(3, [128, 3, 512], BF16, "Pt")
        recr = b.ring(2, [64, 512], F32, "rec")
        outr = b.ring(3, [64, 512], BF16, "ao")
        sbanks = [0, 1, 2, 3, 4, 5]
        si = [0]
        oi = [0]

        stageA = []
        stageB = []

        def run_group(kparts, vsrc, dq, heads, scale):
            grp = {}

            def load_kv():
                Kt, rK = Kr.next()
                for ap, pb in kparts:
                    n = ap.shape[0]
                    b.dma("sp", Kt[pb:pb + n, :], ap, W=[rK])
                Vt, rV = Vr.next()
                b.dma("sp", Vt[:, :, 0:64], vsrc.rearrange("(c p) d -> p c d", p=128), W=[rV])
                grp["K"] = (Kt, rK)
                grp["V"] = (Vt, rV)
            first_in_group = [True]
            NB = 3
            for (qap, orow) in heads:
                for (c0, W, kcs) in qtiles_all():
                    qt = {}
                    batches = [kcs[i:i + NB] for i in range(0, len(kcs), NB)]
                    for bi_, batch in enumerate(batches):
                        ch = {}

                        def A(bi_=bi_, batch=batch, qt=qt, ch=ch, c0=c0, W=W, qap=qap, lk=(first_in_group[0] and bi_ == 0)):
                            if lk:
                                load_kv()
                            if bi_ == 0:
                                Qt, rQ = Qr.next()
                                b.dma("sp", Qt[0:dq, :W], qap[:, c0:c0 + W], W=[rQ])
                                qt["Q"] = (Qt, rQ)
                                ob = 6 + oi[0]
                                oi[0] ^= 1
                                qt["O"] = (b.ps[ob], b.psr[ob])
                                qt["KV"] = (grp["K"], grp["V"])
                            Qt, rQ = qt["Q"]
                            (Kt, rK), _ = qt["KV"]
                            base = 3 * si[0]
                            si[0] ^= 1
                            nb = len(batch)
                            for j, kc in enumerate(batch):
                                b.mm(b.ps[base + j][:, :W], lhsT=Kt[0:dq, kc * 128:(kc + 1) * 128], rhs=Qt[0:dq, :W], start=True, stop=True,
                                     R=[rK, rQ], W=[b.psr[base + j]])
                            Pt, rP = Pr.next()
                            src = b.psbig[:, base * 512:(base + nb) * 512].rearrange("p (n w) -> p n w", w=512)[:, :, :W]
                            b.act(Pt[:, 0:nb, :W], src, AF.Exp, R=[b.psr[base + j] for j in range(nb)], W=[rP], scale=scale)
                            ch["P"] = (Pt, rP)

                        def Bf(bi_=bi_, batch=batch, qt=qt, ch=ch, c0=c0, W=W, orow=orow, nbt=len(batches)):
                            ops_, rops = qt["O"]
                            _, (Vt, rV) = qt["KV"]
                            Pt, rP = ch["P"]
                            for j, kc in enumerate(batch):
                                b.mm(ops_[:, :W], lhsT=Vt[:, kc, :], rhs=Pt[:, j, :W], start=(bi_ == 0 and j == 0),
                                     stop=(bi_ == nbt - 1 and j == len(batch) - 1), R=[rV, rP], W=[rops])
                            if bi_ == nbt - 1:
                                rec, rrec = recr.next()
                                b.recip("dve", rec[:, :W], ops_[64:128, :W], R=[rops], W=[rrec])
                                ao, rao = outr.next()
                                b.tt("dve", ao[:, :W], ops_[0:64, :W], rec[:, :W], ALU.mult, R=[rops, rrec], W=[rao])
                                b.dma("sp", d["OT"][orow:orow + 64, c0:c0 + W], ao[:, :W], R=[rao])
                        stageA.append(A)
                        stageB.append(Bf)
                        first_in_group[0] = False

        nh = 0
        for g in range(2):
            heads = [(d["QA"][h * 64:(h + 1) * 64, :], h * 64) for h in range(4 * g, 4 * g + 4)]
            heads = heads[:max(0, nhead_dbg - nh)]
            nh += len(heads)
            if heads:
                run_group([(d["KA"][g * 64:(g + 1) * 64, :], 0)], d["VA"][:, g * 64:(g + 1) * 64], 64, heads, 0.125)
        for h in range(8):
            if nh >= nhead_dbg:
                break
            nh += 1
            run_group([(d["KBn"][h * 64:(h + 1) * 64, :], 0), (d["KR"][:, :], 64)], d["VB"][:, h * 64:(h + 1) * 64], 96,
                      [(d["QB"][h], 512 + h * 64)], 96 ** -0.5)
        LOOK = 1
        n = len(stageA)
        for i in range(n + LOOK):
            if i < n:
                stageA[i]()
            if i - LOOK >= 0:
                stageB[i - LOOK]()


def phase_outproj(b, d, l, wout_ap, src_lat, src_ctx, do_ctx):
    with b.phase():
        b.set_pools({"y": [0, 1, 2, 3], "t": [4, 5, 6, 7]})
        ident, rid = b.const("ident")
        wout, rwout = load_w_bf16(b, wout_ap, 8, 1024, "wout")
        nx = NormCtx(b)
        xr = b.ring(7, [128, 1024], F32, "xt")
        og = b.ring(3, [128, 8, 512], BF16, "OTg")
        hg = b.ring(3, [128, 8, 512], BF16, "hTg")
        x1r = b.ring(6, [128, 1024], F32, "x1")
        jobs = []
        if do_ctx:
            jobs.append((1, src_ctx, d["cres"], M // 128, 0, d["HTc"]))
        jobs.append((0, src_lat, d["xres"], T // 128, M, d["HTl"]))
        items = []
        for which, src, dst, ntile, otc0, HT in jobs:
            gt, rgt = load_mod(b, d, l, which, 2, "gt")
            mult2, rmult2 = load_mult(b, d, l, which, 4, d["norm_ffn"][l, :], "m2")
            sh2, rsh2 = load_mod(b, d, l, which, 3, "sh2")
            for g0 in range(0, ntile, 4):
                ng = min(4, ntile - g0)
                grp = {}
                for i in range(ng):
                    c = {"col": i * 128}

                    def s_ld(c=c, grp=grp, i=i, src=src, g0=g0, ng=ng, otc0=otc0):
                        if i == 0:
                            OTg, rOT = og.next()
                            b.dma("sp", OTg[:, :, 0:ng * 128],
                                  d["OT"][:, otc0 + g0 * 128: otc0 + (g0 + ng) * 128].rearrange("(j p) t -> p j t", p=128), W=[rOT])
                            grp["OT"] = (OTg, rOT)
                        r0 = (g0 + i) * 128
                        xt, rx = xr.next()
                        b.dma("sp", xt[:], src[r0:r0 + 128, :], W=[rx])
                        c["xt"] = (xt, rx)

                    def s_mm(c=c, grp=grp, i=i):
                        if i == 0:
                            grp["h"] = hg.next()
                        OTg, rOT = grp["OT"]
                        c["hTg"], c["rhTg"] = grp["h"]
                        c["ps"] = []
                        for half in range(2):
                            ps, rps = b.psum("y")
                            for j in range(8):
                                b.mm(ps[:, :], lhsT=OTg[:, j, i * 128:(i + 1) * 128], rhs=wout[:, j, half * 512:(half + 1) * 512],
                                     start=(j == 0), stop=(j == 7), R=[rOT, rwout], W=[rps])
                            c["ps"].append((ps, rps))

                    def s_res(c=c, i=i, g0=g0, dst=dst, gt=gt, rgt=rgt):
                        xt, rx = c["xt"]
                        x1, rx1 = x1r.next()
                        for half in range(2):
                            ps, rps = c["ps"][half]
                            b.tt("dve", x1[:, half * 512:(half + 1) * 512], ps[:, :], gt[:, half * 512:(half + 1) * 512], ALU.mult,
                                 R=[rps, rgt], W=[rx1])
                        b.tt("dve", x1[:], x1[:], xt[:], ALU.add, R=[rx1, rx], W=[rx1])
                        r0 = (g0 + i) * 128
                        b.dma("sp", dst[r0:r0 + 128, :], x1[:], R=[rx1])
                        c["x"], c["rx"] = x1[:], rx1
                    if i == ng - 1:
                        def after(c=c, HT=HT, g0=g0, ng=ng):
                            b.dma("sp", HT[:, 1 + g0 * 128: 1 + (g0 + ng) * 128].rearrange("(j p) t -> p j t", p=128),
                                  c["hTg"][:, :, 0:ng * 128], R=[c["rhTg"]])
                        c["after"] = after
                    st = norm_stages(b, nx, c, mult2, rmult2, sh2, rsh2, ident, rid)

                    def s_res_sq(s_res=s_res, sq=st[0]):
                        s_res()
                        sq()
                    items.append([s_ld, NOOP, NOOP, s_mm, s_res_sq, st[1], NOOP, st[2]])
        run_pipeline(items)
        b.set_pools(None)


FT = 384


def phase_ffn(b, d, l, do_ctx, final=False):
    with b.phase():
        if final:
            fng, rfng = load_bc(b, d["final_norm"][0, :], name="fn")
            fss = b.ring(4, [128, 4], F32, "fss")
        import os
        ntl = int(os.environ.get("FFN_NT", "99"))
        wup, rwup_g = load_w_bf16(b, d["ffn_up"][l * 1024:(l + 1) * 1024, :], 8, 5632, "wup", ngrp=4, order=[0, 2, 1, 3])
        wdn, rwdn = load_w_bf16(b, d["ffn_down"][l * 2816:(l + 1) * 2816, :], 22, 1024, "wdn")
        fc = b.sb([128, 176], F32, "fcol")
        rfc = Res()
        b.dma("sp", fc[:], d["fcol"][l * 128:(l + 1) * 128, :], W=[rfc])
        hTr = b.ring(2, [128, 8, FT + 2], BF16, "h2T")
        hid = b.sb([128, 22, FT], BF16, "hid")
        rhid = [Res("hid%d" % j) for j in range(22)]
        gar = b.ring(2, [128, FT], F32, "ga")
        var_ = b.ring(2, [128, FT], F32, "va")
        sgr = b.ring(2, [128, FT], F32, "sg")
        xr = b.ring(2, [128, 1024], F32, "xt")
        yr = b.ring(2, [128, 1024], F32, "yt")
        jobs = []
        if do_ctx:
            jobs.append((1, d["cres"], d["HTc"], [(0, M)]))
        lat_tiles = [(t0, min(FT, T - t0)) for t0 in range(0, T, FT)][:ntl]
        jobs.append((0, d["xres"], d["HTl"], lat_tiles))

        def conv(ps, rps, n, ch, dst, rdst, q2):
            w0 = fc[:, ch:ch + 1]
            w1 = fc[:, 44 + ch:44 + ch + 1]
            w2 = fc[:, 88 + ch:88 + ch + 1]
            bb = fc[:, 132 + ch:132 + ch + 1]
            b.act(dst[:, :n], ps[:, 1:n + 1], AF.Identity, R=[rps, rfc], W=[rdst], bias=bb, scale=w1)
            b.stt("dve", dst[:, :n], ps[:, 0:n], w0, dst[:, :n], ALU.mult, ALU.add, R=[rps, rfc, rdst], W=[rdst])
            b.stt("dve", dst[:, :n], ps[:, 2:n + 2], w2, dst[:, :n], ALU.mult, ALU.add, R=[rps, rfc, rdst], W=[rdst])

        for which, xdst, HT, tiles in jobs:
            gt2, rgt2 = load_mod(b, d, l, which, 5, "gt2")
            pre = {}

            def load_hT(ti, HT=HT, tiles=tiles, pre=pre):
                t0_, n_ = tiles[ti]
                hT_, rhT_ = hTr.next()
                b.dma("sp", hT_[:, :, 0:n_ + 2], HT[:, t0_:t0_ + n_ + 2].rearrange("(j p) t -> p j t", p=128), W=[rhT_])
                pre[ti] = (hT_, rhT_)
            load_hT(0)
            for ti, (t0, n) in enumerate(tiles):
                hT, rhT = pre.pop(ti)
                for j in range(22):
                    psg, rpsg = b.psum()
                    for k in range(8):
                        b.mm(psg[:, :n + 2], lhsT=wup[:, k, j * 128:(j + 1) * 128], rhs=hT[:, k, :n + 2], start=(k == 0), stop=(k == 7),
                             R=[rwup_g[j // 11], rhT], W=[rpsg])
                    psv, rpsv = b.psum()
                    for k in range(8):
                        b.mm(psv[:, :n + 2], lhsT=wup[:, k, (22 + j) * 128:(23 + j) * 128], rhs=hT[:, k, :n + 2], start=(k == 0), stop=(k == 7),
                             R=[rwup_g[(22 + j) // 11], rhT], W=[rpsv])
                    ga, rga = gar.next()
                    conv(psg, rpsg, n, j, ga, rga, "dve")
                    va, rva = var_.next()
                    conv(psv, rpsv, n, 22 + j, va, rva, "dve")
                    sg, rsg = sgr.next()
                    b.act(sg[:, :n], ga[:, :n], AF.Silu, R=[rga], W=[rsg])
                    b.tt("pool", hid[:, j, :n], sg[:, :n], va[:, :n], ALU.mult, R=[rsg, rva], W=[rhid[j]])
                if ti + 1 < len(tiles):
                    load_hT(ti + 1)
                for sub in range((n + 127) // 128):
                    m = min(128, n - sub * 128)
                    r0 = t0 + sub * 128
                    xt, rx = xr.next()
                    b.dma("sp", xt[:m, :], xdst[r0:r0 + m, :], W=[rx])
                    yt, ry = yr.next()
                    for half in range(2):
                        ps, rps = b.psum()
                        for j in range(22):
                            b.mm(ps[:m, :], lhsT=hid[:, j, sub * 128:sub * 128 + m], rhs=wdn[:, j, half * 512:(half + 1) * 512],
                                 start=(j == 0), stop=(j == 21), R=[rhid[j], rwdn], W=[rps])
                        b.tt("dve", yt[:m, half * 512:(half + 1) * 512], ps[:m, :], gt2[:m, half * 512:(half + 1) * 512], ALU.mult,
                             R=[rps, rgt2], W=[ry])
                    b.tt("pool", yt[:m, :], yt[:m, :], xt[:m, :], ALU.add, R=[ry, rx], W=[ry])
                    if final and which == 0:
                        ss, rss = fss.next()
                        b.act(xt[:m, :], yt[:m, :], AF.Square, R=[ry], W=[rx, rss], accum=ss[:m, 0:1])
                        rstd_col(b, ss, rss, 1024)
                        b.stt("dve", yt[:m, :], yt[:m, :], ss[:m, 2:3], fng[:m, :], ALU.mult, ALU.mult, R=[ry, rss, rfng], W=[ry])
                        b.dma("sp", d["out"][r0:r0 + m, :], yt[:m, :], R=[ry])
                    else:
                        b.dma("sp", xdst[r0:r0 + m, :], yt[:m, :], R=[ry])


HT_ = 510


def phase_proj_odd(b, d):
    with b.phase():
        b.set_pools({"p": [0, 1, 2], "s": [3, 4], "w": [5, 6, 7]})
        ocol, rocol = b.const("ocol")
        pA, rpA = b.const("permA")
        ident, rid = b.const("ident")
        wod, rwod = load_w_bf16(b, d["wod"], 8, 2304, "wod")
        tmp = ProjTmp(b)
        hg = b.ring(2, [128, 8, 512], BF16, "hT")
        tabs = b.ring(2, [128, 2, 512], F32, "tabs")
        var = b.ring(3, [128, 128], BF16, "va")
        items = []
        for (isc, sc0, c0, W) in groups_all():
            G = {}

            def s_load(G=G, isc=isc, sc0=sc0, c0=c0, W=W):
                hT, rhT = hg.next()
                src = d["HTc"] if isc else d["HTl"]
                b.dma("sp", hT[:, :, :W], src[:, sc0:sc0 + W].rearrange("(j p) t -> p j t", p=128), W=[rhT])
                tab, rtab = tabs.next()
                for i, nm in enumerate(("cosA", "sinA")):
                    b.dma("sp", tab[:, i, :W], d[nm][:, c0:c0 + W], W=[rtab])
                G["hT"] = (hT, rhT)
                G["tab"] = (tab, rtab)
            items.append([s_load])
            for cidx in range(5):
                if isc and cidx < 4:
                    continue
                c = {}

                def a0(c=c, G=G, cidx=cidx, W=W):
                    hT, rhT = G["hT"]
                    ps, rps = b.psum("p")
                    proj_fm(b, ps, rps, W, wod, rwod, 8, cidx * 128, hT, rhT)
                    c["ps"] = (ps, rps)

                def store(ob, rob, cidx=cidx, c0=c0, W=W):
                    if cidx < 4:
                        b.dma("sp", d["QA"][cidx * 128:(cidx + 1) * 128, c0:c0 + W], ob[:, :W], R=[rob])
                    else:
                        b.dma("sp", d["KA"][:, c0:c0 + W], ob[:, :W], R=[rob])

                def a1(c=c, G=G, W=W, store=store):
                    c["xps"] = c["ps"]
                    tab, rtab = G["tab"]
                    sw_, cb_ = rope_stages(b, tmp, c, W, pA, rpA, tab[:, 0, :], tab[:, 1, :], rtab, store)
                    c["comb"] = cb_
                    sw_()

                def a2(c=c):
                    c["comb"]()
                items.append([a0, a1, a2])
            for sub in range(W // 128):
                c = {}

                def v0(c=c, G=G, sub=sub):
                    hT, rhT = G["hT"]
                    ps, rps = b.psum("p")
                    for j in range(8):
                        b.mm(ps[:, 0:128], lhsT=hT[:, j, sub * 128:(sub + 1) * 128], rhs=wod[:, j, 640:768], start=(j == 0), stop=(j == 7),
                             R=[rhT, rwod], W=[rps])
                    c["ps"] = (ps, rps)

                def v1(c=c, sub=sub, c0=c0):
                    ps, rps = c["ps"]
                    va, rva = var.next()
                    b.cp("act", va[:], ps[:, 0:128], R=[rps], W=[rva])
                    b.dma("sp", d["VA"][c0 + sub * 128:c0 + (sub + 1) * 128, :], va[:], R=[rva])
                items.append([v0, v1])
        hh = b.ring(2, [128, 8, 512], BF16, "hTh")
        cvr = b.ring(8, [128, 512], F32, "cv")
        ur = b.ring(3, [128, 512], F32, "u")
        ubr = b.ring(3, [128, 512], BF16, "ub")
        utr = b.ring(2, [128, 4, 512], BF16, "utT")

        def conv(ps, rps, n, ch, dst, rdst):
            w0 = ocol[:, ch:ch + 1]
            w1 = ocol[:, 12 + ch:12 + ch + 1]
            w2 = ocol[:, 24 + ch:24 + ch + 1]
            bb = ocol[:, 36 + ch:36 + ch + 1]
            b.act(dst[:, :n], ps[:, 1:n + 1], AF.Identity, R=[rps, rocol], W=[rdst], bias=bb, scale=w1)
            b.stt("dve", dst[:, :n], ps[:, 0:n], w0, dst[:, :n], ALU.mult, ALU.add, R=[rps, rocol, rdst], W=[rdst])
            b.stt("dve", dst[:, :n], ps[:, 2:n + 2], w2, dst[:, :n], ALU.mult, ALU.add, R=[rps, rocol, rdst], W=[rdst])

        tiles = [(t0, min(HT_, T - t0)) for t0 in range(0, T, HT_)]
        for (t0, n) in tiles:
            G = {}
            nsub = (n + 127) // 128

            def h_load(G=G, t0=t0, n=n):
                hT, rhT = hh.next()
                b.dma("sp", hT[:, :, 0:n + 2], d["HTl"][:, t0:t0 + n + 2].rearrange("(j p) t -> p j t", p=128), W=[rhT])
                G["hT"] = (hT, rhT)
                G["ut"] = utr.next()
            items.append([h_load])
            for i in range(4):
                I = {}
                for part in range(3):
                    c = {}
                    ch = part * 4 + i

                    def p0(c=c, G=G, ch=ch, n=n):
                        hT, rhT = G["hT"]
                        ps, rps = b.psum("p")
                        proj_fm(b, ps, rps, n + 2, wod, rwod, 8, 768 + ch * 128, hT, rhT)
                        c["ps"] = (ps, rps)

                    def p1(c=c, I=I, ch=ch, part=part, n=n):
                        ps, rps = c["ps"]
                        cv, rcv = cvr.next()
                        conv(ps, rps, n, ch, cv, rcv)
                        I[part] = (cv, rcv)
                    st = [p0, p1]
                    if part == 2:
                        def p2(c=c, I=I, G=G, i=i, t0=t0, n=n, nsub=nsub):
                            (x0c, rx0), (x1c, rx1), (vc, rvc) = I[0], I[1], I[2]
                            b.dma("sp", d["X0"][i * 128:(i + 1) * 128, t0:t0 + n], x0c[:, :n], R=[rx0])
                            u, ru = ur.next()
                            b.tt("pool", u[:, :n], x1c[:, :n], vc[:, :n], ALU.mult, R=[rx1, rvc], W=[ru])
                            b.dma("sp", d["UF"][i * 128:(i + 1) * 128, t0:t0 + n], u[:, :n], R=[ru])
                            ub, rub = ubr.next()
                            b.cp("pool", ub[:, :n], u[:, :n], R=[ru], W=[rub])
                            ps, rps = b.psum("s")
                            for sub in range(nsub):
                                m = min(128, n - sub * 128)
                                b.mm(ps[:m, sub * 128:(sub + 1) * 128], lhsT=ub[:, sub * 128:sub * 128 + m], rhs=ident[:], start=True, stop=True,
                                     R=[rub, rid], W=[rps])
                            c["tp"] = (ps, rps)

                        def p3(c=c, G=G, i=i, t0=t0, n=n, nsub=nsub):
                            ps, rps = c["tp"]
                            utT, rut = G["ut"]
                            for sub in range(nsub):
                                m = min(128, n - sub * 128)
                                b.cp("act" if sub % 2 == 0 else "dve", utT[:m, sub, i * 128:(i + 1) * 128], ps[:m, sub * 128:(sub + 1) * 128],
                                     R=[rps], W=[rut])
                            if i == 3:
                                for sub in range(nsub):
                                    m = min(128, n - sub * 128)
                                    b.dma("sp", d["UT"][t0 + sub * 128:t0 + sub * 128 + m, :], utT[:m, sub, :], R=[rut])
                        st += [p2, p3]
                    items.append(st)
        run_pipeline(items)
        b.set_pools(None)


def phase_attn_odd(b, d):
    with b.phase():
        ocol, rocol = b.const("ocol")
        mP, rmP = b.const("maskP")
        mN, rmN = b.const("maskN")
        es = b.sb([128, 8], F32, "esink")
        res_ = Res()
        b.act(es[:], ocol[:, 52:60], AF.Exp, R=[rocol], W=[res_])
        ones512 = b.sb([128, 128], F32, "ones")
        b.memset("dve", ones512[:], 1.0, [res_])
        es4 = []
        for g_ in range(2):
            t_ = b.sb([128, 512], F32, "es4")
            for hh_ in range(4):
                b.ts("dve", t_[:, hh_ * 128:(hh_ + 1) * 128], ones512[:], es[:, 4 * g_ + hh_:4 * g_ + hh_ + 1], None, ALU.mult, None,
                     R=[res_], W=[res_])
            es4.append(t_)
        Kr = b.ring(2, [128, TA], BF16, "Kt")
        Vr = b.ring(2, [128, TA // 128, 128], BF16, "Vt")
        for vt, rv in Vr.items:
            b.memset("pool", vt[:, :, 64:128], 1.0, [rv])
        Qr = b.ring(2, [64, 4, 512], BF16, "Qt")
        Pr = b.ring(16, [128, 512], BF16, "Pt")
        recr = b.ring(3, [64, 512], F32, "rec")
        aor = b.ring(3, [64, 4, 512], BF16, "ao")
        items = []
        for g in range(2):
            Gk = {}
            for q4 in range(T // 512):
                Gq = {}
                for qi in range(4):
                    c = {}
                    qb = q4 * 4 + qi

                    def sA(c=c, Gk=Gk, Gq=Gq, g=g, q4=q4, qi=qi, qb=qb):
                        if q4 == 0 and qi == 0:
                            Kt, rK = Kr.next()
                            b.dma("sp", Kt[0:64, :], d["KA"][g * 64:(g + 1) * 64, :], W=[rK])
                            Vt, rV = Vr.next()
                            b.dma("sp", Vt[:, :, 0:64], d["VA"][:, g * 64:(g + 1) * 64].rearrange("(c p) d -> p c d", p=128), W=[rV])
                            Gk["K"] = (Kt, rK)
                            Gk["V"] = (Vt, rV)
                        if qi == 0:
                            Qt, rQ = Qr.next()
                            for hh_ in range(4):
                                h = 4 * g + hh_
                                b.dma("sp", Qt[:, hh_, :], d["QA"][h * 64:(h + 1) * 64, M + q4 * 512: M + (q4 + 1) * 512], W=[rQ])
                            Gq["Q"] = (Qt, rQ)
                            Gq["ao"] = aor.next()
                        Kt, rK = Gk["K"]
                        Qt, rQ = Gq["Q"]
                        chunks = [(0, None), (1, None)]
                        if qb >= 1:
                            chunks.append((2 + qb - 1, (mP, rmP)))
                        chunks.append((2 + qb, None))
                        if qb <= T // 128 - 2:
                            chunks.append((2 + qb + 1, (mN, rmN)))
                        pts = []
                        for (kc, mask) in chunks:
                            sps, rsps = b.psum()
                            for hh_ in range(4):
                                b.mm(sps[:, hh_ * 128:(hh_ + 1) * 128], lhsT=Kt[0:64, kc * 128:(kc + 1) * 128],
                                     rhs=Qt[:, hh_, qi * 128:(qi + 1) * 128], start=True, stop=True, R=[rK, rQ], W=[rsps])
                            Pt, rP = Pr.next()
                            b.act(Pt[:], sps[:], AF.Exp, R=[rsps], W=[rP], scale=0.125)
                            if mask is not None:
                                b.tt("pool", Pt[:], Pt[:], mask[0][:], ALU.mult, R=[rP, mask[1]], W=[rP])
                            pts.append((kc, Pt, rP))
                        c["pts"] = pts

                    def sB(c=c, Gk=Gk):
                        Vt, rV = Gk["V"]
                        pts = c["pts"]
                        ops_, rops = b.psum()
                        for hh_ in range(4):
                            for ci, (kc, Pt, rP) in enumerate(pts):
                                b.mm(ops_[:, hh_ * 128:(hh_ + 1) * 128], lhsT=Vt[:, kc, :], rhs=Pt[:, hh_ * 128:(hh_ + 1) * 128],
                                     start=(ci == 0), stop=(ci == len(pts) - 1), R=[rV, rP], W=[rops])
                        c["O"] = (ops_, rops)

                    def sC(c=c, Gq=Gq, g=g, q4=q4, qi=qi):
                        ops_, rops = c["O"]
                        ao, rao = Gq["ao"]
                        rec, rrec = recr.next()
                        b.tt("dve", rec[:], ops_[64:128, :], es4[g][64:128, :], ALU.add, R=[rops, res_], W=[rrec])
                        b.act(rec[:], rec[:], AF.Ln, R=[rrec], W=[rrec])
                        b.act(rec[:], rec[:], AF.Exp, R=[rrec], W=[rrec], scale=-1.0)
                        b.tt("dve", ao[:, :, qi * 128:(qi + 1) * 128], ops_[0:64, :].rearrange("p (h q) -> p h q", h=4),
                             rec[:].rearrange("p (h q) -> p h q", h=4), ALU.mult, R=[rops, rrec], W=[rao])
                        if qi == 3:
                            for hh_ in range(4):
                                h = 4 * g + hh_
                                b.dma("sp", d["OT"][h * 64:(h + 1) * 64, M + q4 * 512: M + (q4 + 1) * 512], ao[:, hh_, :], R=[rao])
                    items.append([sA, sB, sC])
        run_pipeline(items)


def phase_hyena_fwd(b, d):
    with b.phase():
        import os
        ocol, rocol = b.const("ocol")
        hw, rhw = b.const("hy_w")
        uT = b.sb([128, 32, 512], BF16, "uT")
        ruT = Res()
        for c4 in range(4):
            b.dma("sp", uT[:, c4 * 8:(c4 + 1) * 8, :], d["UT"][c4 * 1024:(c4 + 1) * 1024, :].rearrange("(c p) f -> p c f", p=128), W=[ruT])
        hsum = b.sb([128, 32, 512], BF16, "hsum")
        hdif = b.sb([128, 32, 512], BF16, "hdif")
        rhs_, rhd_ = Res(), Res()
        with contextlib.ExitStack() as st2:
            old = b.stack
            b.stack = st2
            zT = b.sb([5, T], F32, "zT")
            w1 = b.sb([5, 64], F32, "w1")
            w2 = b.sb([64, 64], F32, "w2")
            w3 = b.sb([64, 64], F32, "w3")
            w4 = b.sb([64, 1024], F32, "w4")
            rz, rw = Res(), Res()
            b.dma("sp", zT[:], d["hy_zT"], W=[rz])
            b.dma("sp", w1[:], d["hy_w1"], W=[rw])
            b.dma("sp", w2[:], d["hy_w2"], W=[rw])
            b.dma("sp", w3[:], d["hy_w3"], W=[rw])
            b.dma("sp", w4[:], d["hy_w4"], W=[rw])
            ar = b.ring(2, [64, 512], F32, "ha")
            kr_ = b.ring(2, [64, 512], F32, "hk")
            hr = b.ring(3, [64, 512], F32, "hh")
            decr = b.ring(2, [128, 1024], F32, "dec")
            hfr = b.ring(2, [128, 512], F32, "hf")
            hbr = b.ring(2, [128, 512], F32, "hb")

            def sin_layer(ps, rps, li):
                bcol = ocol[0:64, 60 + li:61 + li]
                fcol = ocol[0:64, 63 + li:64 + li]
                a, ra = ar.next()
                b.ts("dve", a[:], ps[0:64, :], bcol, fcol, ALU.add, ALU.mult, R=[rps, rocol], W=[ra])
                k, rk = kr_.next()
                b.ts("dve", k[:], a[:], I2P, MAGIC, ALU.mult, ALU.add, R=[ra], W=[rk])
                b.ts("dve", k[:], k[:], -MAGIC, None, ALU.add, None, R=[rk], W=[rk])
                b.stt("dve", a[:], a[:], I2P, k[:], ALU.mult, ALU.subtract, R=[ra, rk], W=[ra])
                h, rh = hr.next()
                b.act(h[:], a[:], AF.Sin, R=[ra], W=[rh], scale=6.283185)
                return h, rh

            for tt in range(8):
                ps, rps = b.psum()
                b.mm(ps[0:64, :], lhsT=w1[:], rhs=zT[:, tt * 512:(tt + 1) * 512], start=True, stop=True, R=[rw, rz], W=[rps])
                h, rh = sin_layer(ps, rps, 0)
                ps, rps = b.psum()
                b.mm(ps[0:64, :], lhsT=w2[:], rhs=h[:], start=True, stop=True, R=[rw, rh], W=[rps])
                h, rh = sin_layer(ps, rps, 1)
                ps, rps = b.psum()
                b.mm(ps[0:64, :], lhsT=w3[:], rhs=h[:], start=True, stop=True, R=[rw, rh], W=[rps])
                h3, rh3 = sin_layer(ps, rps, 2)
                for sub in range(4):
                    ck = tt * 4 + sub
                    dec, rdec = decr.next()
                    b.dma("sp", dec[:], d["hy_dec"][ck * 128:(ck + 1) * 128, :], W=[rdec])
                    psf, rpsf = b.psum()
                    b.mm(psf[:], lhsT=h3[:, sub * 128:(sub + 1) * 128], rhs=w4[:, 0:512], start=True, stop=True, R=[rh3, rw], W=[rpsf])
                    psb, rpsb = b.psum()
                    b.mm(psb[:], lhsT=h3[:, sub * 128:(sub + 1) * 128], rhs=w4[:, 512:1024], start=True, stop=True, R=[rh3, rw], W=[rpsb])
                    hf, rhf = hfr.next()
                    b.tt("dve", hf[:], psf[:], dec[:, 0:512], ALU.mult, R=[rpsf, rdec], W=[rhf])
                    hb, rhb = hbr.next()
                    b.tt("dve", hb[:], psb[:], dec[:, 512:1024], ALU.mult, R=[rpsb, rdec], W=[rhb])
                    b.tt("pool", hsum[:, ck, :], hf[:], hb[:], ALU.add, R=[rhf, rhb], W=[rhs_])
                    b.tt("pool", hdif[:, ck, :], hf[:], hb[:], ALU.subtract, R=[rhf, rhb], W=[rhd_])
            b.P.barrier()
            b.P.emit_block(b.sems)
            b.stack = old
        Cr = b.ring(2, [128, 33, 256], BF16, "Ct")
        Sr = b.ring(2, [128, 33, 256], BF16, "St")
        kcr = b.ring(2, [128, 512], F32, "kc")
        ksr = b.ring(2, [128, 512], F32, "ks")
        t1r = b.ring(2, [128, 512], F32, "t1")
        t2r = b.ring(2, [128, 512], F32, "t2")
        ycr = b.ring(2, [128, 512], BF16, "yc")
        ysr = b.ring(2, [128, 512], BF16, "ys")
        nfc = int(os.environ.get("HY_NF", "33"))
        Ct = St = rC = rS = None
        dtiles = {}

        def load_dft(jc):
            Ct, rC = Cr.next()
            b.dma("sp", Ct[:], d["dftC"][jc], W=[rC])
            St, rS = Sr.next()
            b.dma("sp", St[:], d["dftS"][jc], W=[rS])
            dtiles[jc] = (Ct, rC, St, rS)
        load_dft(0)
        for a in range(nfc):
            jc, off = a // 2, (a % 2) * 128
            if a % 2 == 0:
                Ct, rC, St, rS = dtiles.pop(jc)
                if 2 * (jc + 1) < nfc:
                    load_dft(jc + 1)
            pA, rpA_ = b.psum()
            pKc, rpKc = b.psum()
            pB, rpB_ = b.psum()
            pKs, rpKs = b.psum()
            for s in range(32):
                st_, sp_ = (s == 0), (s == 31)
                b.mm(pA[:], lhsT=Ct[:, s, off:off + 128], rhs=uT[:, s, :], start=st_, stop=sp_, R=[rC, ruT], W=[rpA_])
                b.mm(pKc[:], lhsT=Ct[:, s, off:off + 128], rhs=hsum[:, s, :], start=st_, stop=sp_, R=[rC, rhs_], W=[rpKc])
                b.mm(pB[:], lhsT=St[:, s, off:off + 128], rhs=uT[:, s, :], start=st_, stop=sp_, R=[rS, ruT], W=[rpB_])
                b.mm(pKs[:], lhsT=St[:, s, off:off + 128], rhs=hdif[:, s, :], start=st_, stop=sp_, R=[rS, rhd_], W=[rpKs])
            kc, rkc = kcr.next()
            b.act(kc[:], pKc[:], AF.Copy, R=[rpKc, rhw], W=[rkc], scale=hw[:, a:a + 1])
            ks, rks = ksr.next()
            b.act(ks[:], pKs[:], AF.Copy, R=[rpKs, rhw], W=[rks], scale=hw[:, a:a + 1])
            t1, rt1 = t1r.next()
            b.tt("dve", t1[:], pA[:], kc[:], ALU.mult, R=[rpA_, rkc], W=[rt1])
            t2, rt2 = t2r.next()
            b.tt("dve", t2[:], pB[:], ks[:], ALU.mult, R=[rpB_, rks], W=[rt2])
            yc, ryc = ycr.next()
            b.tt("pool", yc[:], t1[:], t2[:], ALU.subtract, R=[rt1, rt2], W=[ryc])
            b.dma("sp", d["YC"][a * 128:(a + 1) * 128, :], yc[:], R=[ryc])
            t1, rt1 = t1r.next()
            b.tt("dve", t1[:], pA[:], ks[:], ALU.mult, R=[rpA_, rks], W=[rt1])
            t2, rt2 = t2r.next()
            b.tt("dve", t2[:], pB[:], kc[:], ALU.mult, R=[rpB_, rkc], W=[rt2])
            ys, rys = ysr.next()
            b.tt("pool", ys[:], t1[:], t2[:], ALU.add, R=[rt1, rt2], W=[rys])
            b.dma("sp", d["YS"][a * 128:(a + 1) * 128, :], ys[:], R=[rys])


def phase_hyena_inv(b, d):
    with b.phase():
        import os
        ocol, rocol = b.const("ocol")
        Yc = b.sb([128, 33, 512], BF16, "Yc")
        Ys = b.sb([128, 33, 512], BF16, "Ys")
        rYc, rYs = Res(), Res()
        b.dma("sp", Yc[:], d["YC"].rearrange("(f p) c -> p f c", p=128), W=[rYc])
        b.dma("sp", Ys[:], d["YS"].rearrange("(f p) c -> p f c", p=128), W=[rYs])
        Cr = b.ring(2, [128, 33, 256], BF16, "Ct")
        Sr = b.ring(2, [128, 33, 256], BF16, "St")
        ur = b.ring(3, [128, 256], F32, "u")
        xr = b.ring(3, [128, 256], F32, "x0")
        tr = b.ring(2, [128, 256], F32, "t")
        ocr = b.ring(3, [128, 256], BF16, "oc")
        ntt = int(os.environ.get("HY_NT", "16"))
        dtiles = {}

        def load_dft(jc):
            Ct, rC = Cr.next()
            b.dma("sp", Ct[:], d["dftC"][jc], W=[rC])
            St, rS = Sr.next()
            b.dma("sp", St[:], d["dftS"][jc], W=[rS])
            dtiles[jc] = (Ct, rC, St, rS)
        load_dft(0)
        for jc in range(ntt):
            Ct, rC, St, rS = dtiles.pop(jc)
            if jc + 1 < ntt:
                load_dft(jc + 1)
            for i in range(4):
                u, ru = ur.next()
                b.dma("sp", u[:], d["UF"][i * 128:(i + 1) * 128, jc * 256:(jc + 1) * 256], W=[ru])
                x0, rx0 = xr.next()
                b.dma("sp", x0[:], d["X0"][i * 128:(i + 1) * 128, jc * 256:(jc + 1) * 256], W=[rx0])
                ps, rps = b.psum()
                for f in range(33):
                    b.mm(ps[:, 0:256], lhsT=Yc[:, f, i * 128:(i + 1) * 128], rhs=Ct[:, f, :], start=(f == 0), stop=False,
                         R=[rYc, rC], W=[rps])
                    b.mm(ps[:, 0:256], lhsT=Ys[:, f, i * 128:(i + 1) * 128], rhs=St[:, f, :], start=False, stop=(f == 32),
                         R=[rYs, rS], W=[rps])
                t, rt = tr.next()
                b.stt("dve", t[:], u[:], ocol[:, 48 + i:49 + i], ps[:, 0:256], ALU.mult, ALU.add, R=[ru, rocol, rps], W=[rt])
                oc, roc = ocr.next()
                b.tt("pool", oc[:], t[:], x0[:], ALU.mult, R=[rt, rx0], W=[roc])
                b.dma("sp", d["OT"][512 + i * 128:512 + (i + 1) * 128, M + jc * 256:M + (jc + 1) * 256], oc[:], R=[roc])


def phase_final(b, d):
    with b.phase():
        g, rg = load_bc(b, d["final_norm"][0, :], name="fn")
        xr = b.ring(7, [128, 1024], F32, "xt")
        jr = b.ring(2, [128, 1024], BF16, "junk")
        ssr = b.ring(6, [128, 4], F32, "ss")
        orr = b.ring(3, [128, 1024], F32, "o")
        items = []
        for i in range(T // 128):
            c = {}

            def s0(c=c, i=i):
                xt, rx = xr.next()
                b.dma("sp", xt[:], d["xres"][i * 128:(i + 1) * 128, :], W=[rx])
                c["x"] = (xt, rx)

            def s1(c=c):
                xt, rx = c["x"]
                junk, rj = jr.next()
                ss, rss = ssr.next()
                b.act(junk[:], xt[:], AF.Square, R=[rx], W=[rj, rss], accum=ss[:, 0:1])
                c["ss"] = (ss, rss)

            def s2(c=c, i=i):
                xt, rx = c["x"]
                ss, rss = c["ss"]
                rstd_col(b, ss, rss, 1024)
                o, ro = orr.next()
                b.stt("dve", o[:], xt[:], ss[:, 2:3], g[:], ALU.mult, ALU.mult, R=[rx, rss, rg], W=[ro])
                b.dma("sp", d["out"][i * 128:(i + 1) * 128, :], o[:], R=[ro])
            items.append([s0, NOOP, NOOP, s1, s2])
        run_pipeline(items)


_NC_CACHE = {}


def kernel(**inputs):
    inp = {k: np.asarray(v) for k, v in inputs.items()}
    if "nc" not in _NC_CACHE:
        _NC_CACHE["nc"] = build()
    nc = _NC_CACHE["nc"]
    _W_CACHE.clear()
    in_maps = [core_inputs(inp, bidx) for bidx in range(8)]
    res = run_bass_kernel_spmd(nc, in_maps, core_ids=list(range(8)))
    out = np.stack([np.asarray(r["out"], dtype=np.float32) for r in res.results], axis=0)
    return out
```
